# Optimizing a Trainium2 kernel written in Bass

```python
import math
import jax, jax.numpy as jnp
from jax import lax
import numpy as np

D_MODEL = 1024
BATCH = 4
SEQ = 8192
DEPTH = 4

N_META = 16
D_MIX = D_MODEL
ATTN_WIDTH = D_MIX // 2
CONV_CH = D_MIX - ATTN_WIDTH
N_ATTN_HEADS = 4
V_HEAD_DIM = ATTN_WIDTH // N_ATTN_HEADS
HEAD_DIM = V_HEAD_DIM // 2
QK_WIDTH = N_ATTN_HEADS * 2 * HEAD_DIM
IN_WIDTH = 2 * QK_WIDTH + ATTN_WIDTH + 2 * CONV_CH
CONV_WIDTH = 31
D_FF = -(-8 * D_MODEL // (3 * 256)) * 256
ROPE_THETA = 10000.0
Q_BLOCK = 128
NORM_EPS = 1e-5

kernel_name = "hybrid_diffattn_conformer_conv_swiglu"


def rmsnorm(x, g):
    x32 = x.astype(jnp.float32)
    y = x32 * lax.rsqrt(jnp.mean(x32 * x32, axis=-1, keepdims=True) + NORM_EPS)
    return (y * g.astype(jnp.float32)).astype(x.dtype)


def layer_norm(x, g, b):
    x32 = x.astype(jnp.float32)
    mu = jnp.mean(x32, axis=-1, keepdims=True)
    xc = x32 - mu
    y = xc * lax.rsqrt(jnp.mean(xc * xc, axis=-1, keepdims=True) + NORM_EPS)
    return (y * g.astype(jnp.float32) + b.astype(jnp.float32)).astype(x.dtype)


def rope_tables(length, dtype):
    pos = jnp.arange(length, dtype=jnp.float32)
    inv = ROPE_THETA ** (-jnp.arange(0, HEAD_DIM, 2, dtype=jnp.float32) / HEAD_DIM)
    ang = pos[:, None] * inv[None, :]
    ang = jnp.concatenate([ang, ang], axis=-1)
    return jnp.cos(ang).astype(dtype), jnp.sin(ang).astype(dtype)


def apply_rope(t, cos, sin):
    c = cos[None, :, None, None, :]
    s = sin[None, :, None, None, :]
    t1, t2 = jnp.split(t, 2, axis=-1)
    return t * c + jnp.concatenate([-t2, t1], axis=-1) * s


def diff_attention(q, k, v, lam, sub_g, lam_init):
    B, L = q.shape[0], q.shape[1]
    n_blk = -(-L // Q_BLOCK)
    L_pad = n_blk * Q_BLOCK
    pad = L_pad - L
    q = jnp.pad(q, ((0, 0), (0, pad), (0, 0), (0, 0), (0, 0)))
    k = jnp.pad(k, ((0, 0), (0, pad), (0, 0), (0, 0), (0, 0)))
    v = jnp.pad(v, ((0, 0), (0, pad), (0, 0), (0, 0)))
    qb = q.reshape(B, n_blk, Q_BLOCK, N_ATTN_HEADS, 2, HEAD_DIM).transpose(1, 0, 2, 3, 4, 5)
    k_pos = jnp.arange(L_pad)
    scale = HEAD_DIM ** -0.5

    def one_block(args):
        q_blk, start = args
        s = jnp.einsum('bqhcd,bkhcd->bhcqk', q_blk, k).astype(jnp.float32) * scale
        q_pos = start + jnp.arange(Q_BLOCK)
        mask = k_pos[None, :] <= q_pos[:, None]
        s = jnp.where(mask, s, -jnp.inf)
        p = jax.nn.softmax(s, axis=-1)
        a = p[:, :, 0] - lam * p[:, :, 1]
        return jnp.einsum('bhqk,bkhe->bqhe', a.astype(v.dtype), v)

    starts = jnp.arange(n_blk) * Q_BLOCK
    o = lax.map(one_block, (qb, starts))
    o = o.transpose(1, 0, 2, 3, 4).reshape(B, L_pad, N_ATTN_HEADS, V_HEAD_DIM)[:, :L]
    o = rmsnorm(o, sub_g) * (1.0 - lam_init)
    return o.reshape(B, L, ATTN_WIDTH)


def conformer_conv(u, conv_w, conv_b, ln_g, ln_b):
    a, gate = jnp.split(u, 2, axis=-1)
    z = a * jax.nn.sigmoid(gate)
    z = lax.conv_general_dilated(
        z, conv_w[:, None, :], window_strides=(1,), padding=((CONV_WIDTH - 1, 0),),
        dimension_numbers=('NWC', 'WIO', 'NWC'), feature_group_count=CONV_CH)
    z = z + conv_b
    z = layer_norm(z, ln_g, ln_b)
    return jax.nn.silu(z)


def setup_inputs(seed: int = 0) -> dict:
    key = jax.random.key(seed)
    ks = jax.random.split(key, 20)
    f = jnp.float32
    nrm = lambda k, shape, s: jax.random.normal(k, shape, f) * s
    return {
        "x": nrm(ks[0], (BATCH, SEQ, D_MODEL), 1.0),
        "meta_tokens": nrm(ks[1], (N_META, D_MODEL), 1.0),
        "norm1_g": 1.0 + nrm(ks[2], (DEPTH, D_MODEL), 0.02),
        "w_in": nrm(ks[3], (DEPTH, D_MODEL, IN_WIDTH), D_MODEL ** -0.5),
        "b_glu": nrm(ks[4], (DEPTH, 2 * CONV_CH), 0.02),
        "conv_w": nrm(ks[5], (DEPTH, CONV_WIDTH, CONV_CH), CONV_WIDTH ** -0.5),
        "conv_b": nrm(ks[6], (DEPTH, CONV_CH), 0.02),
        "conv_ln_g": 1.0 + nrm(ks[7], (DEPTH, CONV_CH), 0.02),
        "conv_ln_b": nrm(ks[8], (DEPTH, CONV_CH), 0.02),
        "lam_q1": nrm(ks[9], (DEPTH, HEAD_DIM), 0.1),
        "lam_k1": nrm(ks[10], (DEPTH, HEAD_DIM), 0.1),
        "lam_q2": nrm(ks[11], (DEPTH, HEAD_DIM), 0.1),
        "lam_k2": nrm(ks[12], (DEPTH, HEAD_DIM), 0.1),
        "subln_g": 1.0 + nrm(ks[13], (DEPTH, V_HEAD_DIM), 0.02),
        "w_out": nrm(ks[14], (DEPTH, D_MIX, D_MODEL), D_MIX ** -0.5),
        "norm2_g": 1.0 + nrm(ks[15], (DEPTH, D_MODEL), 0.02),
        "w_gate_up": nrm(ks[16], (DEPTH, D_MODEL, 2 * D_FF), D_MODEL ** -0.5),
        "w_down": nrm(ks[17], (DEPTH, D_FF, D_MODEL), D_FF ** -0.5),
        "final_g": 1.0 + nrm(ks[18], (D_MODEL,), 0.02),
    }


def reference(x, meta_tokens, norm1_g, w_in, b_glu, conv_w, conv_b, conv_ln_g, conv_ln_b,
              lam_q1, lam_k1, lam_q2, lam_k2, subln_g, w_out, norm2_g, w_gate_up, w_down,
              final_g):
    B, S, D = x.shape
    meta = jnp.broadcast_to(meta_tokens[None].astype(x.dtype), (B, N_META, D))
    h = jnp.concatenate([meta, x], axis=1)
    L = S + N_META
    cos, sin = rope_tables(L, x.dtype)
    for l in range(DEPTH):
        lam_init = 0.8 - 0.6 * math.exp(-0.3 * l)
        hn = rmsnorm(h, norm1_g[l])
        proj = hn @ w_in[l]
        q, k, v, u = jnp.split(proj, [QK_WIDTH, 2 * QK_WIDTH, 2 * QK_WIDTH + ATTN_WIDTH], axis=-1)
        q = apply_rope(q.reshape(B, L, N_ATTN_HEADS, 2, HEAD_DIM), cos, sin)
        k = apply_rope(k.reshape(B, L, N_ATTN_HEADS, 2, HEAD_DIM), cos, sin)
        v = v.reshape(B, L, N_ATTN_HEADS, V_HEAD_DIM)
        lam = (jnp.exp(jnp.sum(lam_q1[l].astype(jnp.float32) * lam_k1[l].astype(jnp.float32)))
               - jnp.exp(jnp.sum(lam_q2[l].astype(jnp.float32) * lam_k2[l].astype(jnp.float32)))
               + lam_init)
        attn_out = diff_attention(q, k, v, lam, subln_g[l], lam_init)
        conv_out = conformer_conv(u + b_glu[l], conv_w[l], conv_b[l],
                                  conv_ln_g[l], conv_ln_b[l])
        h = h + jnp.concatenate([attn_out, conv_out], axis=-1) @ w_out[l]
        hn = rmsnorm(h, norm2_g[l])
        gate, up = jnp.split(hn @ w_gate_up[l], 2, axis=-1)
        h = h + (jax.nn.silu(gate) * up) @ w_down[l]
    h = rmsnorm(h, final_g)
    return h[:, N_META:]
```

```python
import math
import contextlib
import numpy as np
import ml_dtypes
import concourse.bass as bass
import concourse.mybir as mybir
from concourse.bass_utils import run_bass_kernel_spmd

F32 = mybir.dt.float32
BF16 = mybir.dt.bfloat16
AF = mybir.ActivationFunctionType
ALU = mybir.AluOpType

D = 1024
KC = 8
DFF = 2816
FC = 22
NMETA = 16
CW = 31
HALO = 30
EPS = 1e-5
WIN_COLS = 3584
PIECE_ROWS = 2048


class Ins:
    __slots__ = ("eng", "fn", "deps", "signal", "val", "key", "slot", "inc")


class Slot:
    def __init__(self):
        self.count = 0
        self.sem = None


class Buf:
    def __init__(self, ap=None, name=""):
        self.ap = ap
        self.name = name
        self.w = {}
        self.r = {}
        self.slot = None


class Kern:
    def __init__(self, nc, arena, arena_cols, banks):
        self.nc = nc
        self.engs = {n: [] for n in ("pe", "act", "dve", "pool", "sp")}
        self.pending = {n: [] for n in self.engs}
        self.last = {}
        self.arena = arena
        self.arena_cols = arena_cols
        self.off = 0
        self.banks = banks
        self.bank_i = 0
        self.slots = []
        self.free_slots = []
        self.live = []
        self.dram_bufs = {}

    def sb(self, shape, dtype, name=""):
        esz = 4 if dtype == F32 else 2
        n = 1
        for x in shape[1:]:
            n *= x
        nbytes = (n * esz + 63) // 64 * 64
        o = self.off
        self.off += nbytes
        assert self.off <= self.arena_cols * 2, f"SBUF arena overflow at {name}: {self.off}"
        ap = self.arena[:, o // 2:(o + n * esz) // 2]
        if dtype == F32:
            ap = ap.bitcast(F32)
        if len(shape) == 3:
            ap = ap.rearrange("p (a b) -> p a b", b=shape[2])
        elif len(shape) == 4:
            ap = ap.rearrange("p (a b c) -> p a b c", b=shape[2], c=shape[3])
        b = Buf(ap, name)
        self.live.append((o, b))
        return b

    def mark(self):
        return self.off

    def release(self, m):
        keep = []
        for (o, b) in self.live:
            if o >= m:
                if b.slot is not None:
                    self.free_slots.append(b.slot)
                    b.slot = None
            else:
                keep.append((o, b))
        self.live = keep
        self.off = m

    def bank(self):
        b = self.banks[self.bank_i % len(self.banks)]
        self.bank_i += 1
        return b

    def D(self, name, j=0):
        k = (name, j)
        if k not in self.dram_bufs:
            self.dram_bufs[k] = Buf(None, f"{name}:{j}")
        return self.dram_bufs[k]

    def _add(self, eng, fn, reads, writes, dbuf=None, inc=16):
        ins = Ins()
        ins.eng = eng
        ins.fn = fn
        ins.signal = False
        ins.val = None
        ins.slot = None
        ins.inc = 1
        if dbuf is not None:
            if dbuf.slot is None:
                if self.free_slots:
                    dbuf.slot = self.free_slots.pop()
                else:
                    dbuf.slot = Slot()
                    self.slots.append(dbuf.slot)
            sl = dbuf.slot
            ins.slot = sl
            ins.inc = inc
            ins.key = ("d", id(sl))
            sl.count += inc
            ins.val = sl.count
            ins.signal = True
        else:
            ins.key = eng
        deps = {}
        for b in reads:
            for d in b.w.values():
                deps[id(d)] = d
        for b in writes:
            for d in b.w.values():
                deps[id(d)] = d
            for d in b.r.values():
                deps[id(d)] = d
        for d in self.pending[eng]:
            deps[id(d)] = d
        self.pending[eng] = []
        ins.deps = [d for d in deps.values() if not (d.key == "pe" and eng == "pe")]
        for d in ins.deps:
            d.signal = True
        for b in writes:
            b.w[ins.key] = ins
            b.r = {}
        for b in reads:
            b.r[ins.key] = ins
        self.engs[eng].append(ins)
        self.last[ins.key] = ins
        return ins

    def barrier(self):
        lst = list(self.last.values())
        for n in self.engs:
            self.pending[n] = list(lst)

    def finish(self):
        self.barrier()
        for n in self.engs:
            self._add(n, None, [], [])

    def mm(self, out, lhsT, rhs, start, stop, reads, writes, **kw):
        return self._add("pe", lambda e: e.matmul(out, lhsT, rhs, start=start, stop=stop, **kw), reads, writes)

    def transpose(self, out, in_, ident, reads, writes):
        return self._add("pe", lambda e: e.transpose(out, in_, ident), reads, writes)

    def act(self, out, in_, func, reads, writes, bias=None, scale=None):
        kw = {}
        if bias is not None:
            kw["bias"] = bias
        if scale is not None:
            kw["scale"] = scale
        return self._add("act", lambda e: e.activation(out, in_, func, **kw), reads, writes)

    def tt(self, eng, out, in0, in1, op, reads, writes):
        return self._add(eng, lambda e: e.tensor_tensor(out, in0, in1, op), reads, writes)

    def ts(self, eng, out, in0, s1, op0, reads, writes, s2=None, op1=None):
        if op1 is None:
            return self._add(eng, lambda e: e.tensor_scalar(out, in0, s1, None, op0), reads, writes)
        return self._add(eng, lambda e: e.tensor_scalar(out, in0, s1, s2, op0, op1), reads, writes)

    def stt(self, out, in0, scalar, in1, op0, op1, reads, writes, accum_out=None):
        return self._add("dve", lambda e: e.scalar_tensor_tensor(out, in0, scalar, in1, op0, op1, accum_out=accum_out),
                         reads, writes)

    def copy(self, eng, out, in_, reads, writes):
        if eng == "act":
            return self._add("act", lambda e: e.activation(out, in_, AF.Copy), reads, writes)
        return self._add(eng, lambda e: e.tensor_copy(out, in_), reads, writes)

    def recip(self, out, in_, reads, writes):
        return self._add("dve", lambda e: e.reciprocal(out, in_), reads, writes)

    def memset(self, eng, ap, val, writes):
        return self._add(eng, lambda e: e.memset(ap, val), [], writes)

    def dma(self, q, out, in_, sbuf, reads, writes):
        return self._add(q, lambda e: e.dma_start(out=out, in_=in_), reads, writes, dbuf=sbuf)

    def emit(self, stack):
        nc = self.nc
        esem = {n: stack.enter_context(nc.semaphore("es_" + n)) for n in self.engs}
        for i, sl in enumerate(self.slots):
            sl.sem = stack.enter_context(nc.semaphore(f"ds{i}"))
        for n, lst in self.engs.items():
            c = 0
            for ins in lst:
                if ins.slot is None and ins.signal:
                    c += 1
                    ins.val = c

        def sem_of(ins):
            return ins.slot.sem if ins.slot is not None else esem[ins.eng]

        def body_for(name):
            def body(e):
                waited = {}
                for ins in self.engs[name]:
                    for d in sorted(ins.deps, key=lambda d: d.val):
                        sm = sem_of(d)
                        k = id(sm)
                        if waited.get(k, 0) >= d.val:
                            continue
                        e.wait_ge(sm, d.val)
                        waited[k] = d.val
                    if ins.fn is None:
                        continue
                    bi = ins.fn(e)
                    if ins.signal:
                        bi.then_inc(sem_of(ins), ins.inc)
            return body

        with nc.Block() as block:
            block.tensor(body_for("pe"))
            block.scalar(body_for("act"))
            block.vector(body_for("dve"))
            block.gpsimd(body_for("pool"))
            block.sync(body_for("sp"))


def prm_layout(depth):
    o = {}
    c = 0
    for name, n in (("g1", depth * 8), ("g2", depth * 8), ("gf", 8), ("bglu", depth * 8), ("convw", depth * 4 * CW),
                    ("convb", depth * 4), ("lng", depth * 4), ("lnb", depth * 4), ("sel", 2),
                    ("lq1", depth * 64), ("lk1", depth * 64), ("lq2", depth * 64), ("lk2", depth * 64),
                    ("subg", depth * 128)):
        o[name] = c
        c += n
    o["_n"] = c
    return o


CB_IDENT, CB_ONES, CB_ME, CB_MO, CB_MM, CB_N = 0, 128, 256, 384, 512, 528


def build_program(NS, DEPTH, stages, fused, n_pairs):
    NX = NS * 128
    TL = NX + NMETA
    NB = NS // 4
    NKT = 1 + 2 * NS
    NKEY = NMETA + 2 * NX
    OV = 4 * 128 * NX
    OZ = OV + NX * 512
    NXCH = OZ + 4 * 128 * (NS + 1) * HALO
    XR = NXCH // 512
    PL = prm_layout(DEPTH)

    nc = bass.Bass("TRN2", target_bir_lowering=False)
    stack = contextlib.ExitStack()

    phases = set(stages)
    first_stage = stages[0]
    last_stage = stages[-1]

    produced = {}
    ext_in = []
    ext_out = []

    def dram(name, shape, dtype, role):
        kind = {"in": "ExternalInput", "out": "ExternalOutput", "tmp": "Internal"}[role]
        t = nc.dram_tensor(name, list(shape), dtype, kind=kind)
        if role == "in":
            ext_in.append(name)
        if role == "out":
            ext_out.append(name)
        return t.ap()

    xT = dram("xT", [8, 128, TL], F32, "in")
    prm_d = dram("prm", [128, PL["_n"]], F32, "in")
    cb_d = dram("cb", [128, CB_N], BF16, "in")
    cos_d = dram("cosT", [128, TL], F32, "in")
    sin_d = dram("sinT", [128, TL], F32, "in")
    w_in = dram("w_in", [DEPTH, D, WIN_COLS], F32, "in")
    w_out = dram("w_out", [DEPTH, D, D], F32, "in")
    w_gu = dram("w_gu", [DEPTH, D, 2 * DFF], F32, "in")
    w_dn = dram("w_dn", [DEPTH, DFF, D], F32, "in")

    def handoff(name, shape, dtype, producer_phase_of, consumer_phases_of):
        if fused:
            ap = dram(name, shape, dtype, "tmp")
            return {"r": ap, "w": ap}
        res = {}
        res["r"] = dram(name + "_in", shape, dtype, "in")
        res["w"] = dram(name + "_out", shape, dtype, "out")
        return res

    hT = handoff("hT", [8, 128, TL], F32, None, None)
    qT = handoff("qT", [4, 128, TL], BF16, None, None)
    zT = handoff("zT", [4, 128, TL], BF16, None, None)
    kTm = handoff("kTm", [4, 128, NMETA], BF16, None, None)
    vm = handoff("vm", [NMETA, 512], BF16, None, None)
    PR = PIECE_ROWS
    HP = min(4, (PR * 4) // NX)
    TP = min(NX, PR)
    pieces = [(f"k{j}", HP * NX // 4) for j in range(4 // HP)] + [(f"v{j}", TP) for j in range(NX // TP)] + \
             [("z", 30 * (NS + 1))]
    xown_t = {}
    xall_t = {}
    for (pn, rows) in pieces:
        if fused:
            xown_t[pn] = [dram(f"xo_{pn}{i}", [rows, 512], BF16, "tmp") for i in range(2)]
            xall_t[pn] = [dram(f"xa_{pn}{i}", [2 * rows, 512], BF16, "tmp") for i in range(2)]
        else:
            o_ = dram(f"xo_{pn}_out", [rows, 512], BF16, "out")
            a_ = dram(f"xa_{pn}_in", [2 * rows, 512], BF16, "in")
            xown_t[pn] = [o_, o_]
            xall_t[pn] = [a_, a_]

    def _flat(t):
        return t.rearrange("r c -> (r c)")

    def k_own(l, j):
        return _flat(xown_t[f"k{j}"][l % 2]).rearrange("(h p t) -> h p t", h=HP, p=128)

    def k_all(l, r, h):
        j, hl = h // HP, h % HP
        sz = HP * 128 * NX
        return _flat(xall_t[f"k{j}"][l % 2])[r * sz:(r + 1) * sz].rearrange("(h p t) -> h p t", h=HP, p=128)[hl]

    def v_own(l, t0, n):
        j = t0 // TP
        return xown_t[f"v{j}"][l % 2][t0 - j * TP:t0 - j * TP + n, :]

    def v_all(l, r, t0, n):
        j = t0 // TP
        return xall_t[f"v{j}"][l % 2][r * TP + t0 - j * TP:r * TP + t0 - j * TP + n, :]

    def z_own(l):
        return _flat(xown_t["z"][l % 2]).rearrange("(c p s t) -> c p s t", c=4, p=128, s=NS + 1)

    def z_all(l, r):
        sz = 4 * 128 * (NS + 1) * HALO
        return _flat(xall_t["z"][l % 2])[r * sz:(r + 1) * sz].rearrange("(c p s t) -> c p s t", c=4, p=128, s=NS + 1)

    ao_d = dram("ao", [TL, 512], BF16, "tmp")
    mixc_d = dram("mixc", [4, 128, TL], BF16, "tmp")
    hmid_d = dram("hmid", [8, 128, TL], F32, "tmp")
    hn2_d = dram("hn2", [8, 128, TL], BF16, "tmp")
    outT = dram("outT", [8, 128, NX], F32, "out")

    ARENA_COLS = 94 * 1024
    arena = stack.enter_context(nc.sbuf_tensor("arena", [128, ARENA_COLS], BF16))
    psum_all = stack.enter_context(nc.psum_tensor("psall", [128, 8 * 512], F32))
    banks = [Buf(psum_all[:, i * 512:(i + 1) * 512], f"bank{i}") for i in range(8)]
    bankpair = [Buf(psum_all[:, 0:1024].rearrange("p (c x) -> p c x", x=512), "bp0"),
                Buf(psum_all[:, 1024:2048].rearrange("p (c x) -> p c x", x=512), "bp1")]
    K = Kern(nc, arena, ARENA_COLS, banks)

    PRM = K.sb([128, PL["_n"]], F32, "prm")
    CB = K.sb([128, CB_N], BF16, "cb")
    NEGLAM = K.sb([128, 4], F32, "neglam")
    K.dma("sp", PRM.ap, prm_d, PRM, [], [PRM])
    K.dma("sp", CB.ap, cb_d, CB, [], [CB])
    ident = CB.ap[:, CB_IDENT:CB_IDENT + 128]
    ones = CB.ap[:, CB_ONES:CB_ONES + 128]
    maskE = CB.ap[:, CB_ME:CB_ME + 128]
    maskO = CB.ap[:, CB_MO:CB_MO + 128]
    maskM = CB.ap[:, CB_MM:CB_MM + 16]

    def pc(name, idx):
        o = PL[name] + idx
        return PRM.ap[:, o:o + 1]

    xblocks = [(j * 512, 512) for j in range(NB)] + [(NX, NMETA)]

    def rmsnorm_block(hb, n, gname, gidx0, SQ, RS, HN, out_dtype_f32_inplace=False):
        K.act(SQ.ap[:, :, :n], hb.ap[:, :, :n], AF.Square, [hb], [SQ])
        bk = K.bank()
        for c in range(8):
            K.mm(bk.ap[:, :n], ones, SQ.ap[:, c, :n], c == 0, c == 7, [SQ, CB], [bk])
        K.act(RS.ap[:, :n], bk.ap[:, :n], AF.Sqrt, [bk], [RS], bias=EPS, scale=1.0 / D)
        K.recip(RS.ap[:, :n], RS.ap[:, :n], [RS], [RS])
        for c in range(8):
            dst = hb if out_dtype_f32_inplace else HN
            K.stt(dst.ap[:, c, :n], hb.ap[:, c, :n], pc(gname, gidx0 + c), RS.ap[:, :n], ALU.mult, ALU.mult,
                  [hb, RS, PRM], [dst])

    def phaseA(l, h_src):
        m0 = K.mark()
        WGR = [None] * 7
        for g in (0, 5, 1, 6, 3, 4, 2):
            WGR[g] = K.sb([128, 8, 512], BF16, f"W_in{g}")
            K.dma("pool", WGR[g].ap, w_in[l, :, g * 512:(g + 1) * 512].rearrange("(kc p) c -> p kc c", p=128),
                  WGR[g], [], [WGR[g]])

        def Wc(kc, col):
            g, o = col // 512, col % 512
            return WGR[g].ap[:, kc, o:o + 128], WGR[g]

        def Wv(kc):
            return WGR[2].ap[:, kc, :], WGR[2]
        HB = [K.sb([128, 8, 512], F32, "hb") for _ in range(2)]
        CS = [K.sb([128, 2, 512], F32, "cs") for _ in range(2)]
        SQ = K.sb([128, 8, 512], BF16, "sq")
        RS = K.sb([128, 512], F32, "rs")
        HNb = [K.sb([128, 8, 512], BF16, "hn") for _ in range(2)]
        T1 = [K.sb([128, 512], F32, "t1") for _ in range(2)]
        T2 = [K.sb([128, 512], F32, "t2") for _ in range(2)]
        SG = [K.sb([128, 512], F32, "sg") for _ in range(2)]
        QST = [K.sb([128, 4, 512], BF16, "qst") for _ in range(2)]
        KST = [K.sb([128, 4, 512], BF16, "kst") for _ in range(2)]
        ZST = [K.sb([128, 4, 512], BF16, "zst") for _ in range(2)]
        VST = [K.sb([128, 4, 512], BF16, "vst") for _ in range(2)]
        ZER = K.sb([128, 4, 16], BF16, "zer")
        zt_o = z_own(l)
        K.memset("pool", ZER.ap, 0.0, [ZER])
        K.dma("sp", zt_o[:, :, 0, 0:14].rearrange("c p t -> p c t"), ZER.ap[:, :, 0:14], ZER, [ZER], [K.D("xown", l)])
        tcount = 0
        def loadA(bi):
            c0, n = xblocks[bi]
            hb = HB[bi % 2]
            cs = CS[bi % 2]
            K.dma("sp", hb.ap[:, :, :n], h_src[:, :, c0:c0 + n].rearrange("c p t -> p c t"), hb, [K.D("hT", bi)], [hb])
            K.dma("sp", cs.ap[:, 0, :n], cos_d[:, c0:c0 + n], cs, [], [cs])
            K.dma("sp", cs.ap[:, 1, :n], sin_d[:, c0:c0 + n], cs, [], [cs])

        loadA(0)
        rmsnorm_block(HB[0], xblocks[0][1], "g1", l * 8, SQ, RS, HNb[0])
        for bi, (c0, n) in enumerate(xblocks):
            meta = (n == NMETA)
            hb = HB[bi % 2]
            cs = CS[bi % 2]
            HN = HNb[bi % 2]
            if bi + 1 < len(xblocks):
                loadA(bi + 1)
                rmsnorm_block(HB[(bi + 1) % 2], xblocks[bi + 1][1], "g1", l * 8, SQ, RS, HNb[(bi + 1) % 2])
            qst, kst, zst, vst = QST[bi % 2], KST[bi % 2], ZST[bi % 2], VST[bi % 2]
            for which, st in (("q", qst), ("k", kst)):
                for hh in range(4):
                    oc = (0 if which == "q" else 512) + hh * 128
                    ocr = 2560 + (0 if which == "q" else 512) + hh * 128
                    b1 = K.bank()
                    for kc in range(8):
                        wa, wb = Wc(kc, oc)
                        K.mm(b1.ap[:, :n], wa, HN.ap[:, kc, :n], kc == 0, kc == 7, [wb, HN], [b1])
                    b2 = K.bank()
                    for kc in range(8):
                        wa, wb = Wc(kc, ocr)
                        K.mm(b2.ap[:, :n], wa, HN.ap[:, kc, :n], kc == 0, kc == 7, [wb, HN], [b2])
                    t1 = T1[tcount % 2]
                    t2 = T2[tcount % 2]
                    tcount += 1
                    K.tt("dve", t1.ap[:, :n], b1.ap[:, :n], cs.ap[:, 0, :n], ALU.mult, [b1, cs], [t1])
                    K.tt("dve", t2.ap[:, :n], b2.ap[:, :n], cs.ap[:, 1, :n], ALU.mult, [b2, cs], [t2])
                    K.tt("pool", st.ap[:, hh, :n], t1.ap[:, :n], t2.ap[:, :n], ALU.add, [t1, t2], [st])
            K.dma("sp", qT["w"][:, :, c0:c0 + n].rearrange("h p t -> p h t"), qst.ap[:, :, :n], qst, [qst],
                  [K.D("qT", bi)])
            if meta:
                K.dma("sp", kTm["w"].rearrange("h p t -> p h t"), kst.ap[:, :, :n], kst, [kst], [K.D("kTm")])
            else:
                for j in range(4 // HP):
                    K.dma("sp", k_own(l, j)[:, :, c0:c0 + n].rearrange("h p t -> p h t"),
                          kst.ap[:, j * HP:(j + 1) * HP, :n], kst, [kst], [K.D("xown", l)])
            for cc in range(4):
                ba = K.bank()
                for kc in range(8):
                    wa, wb = Wc(kc, 1536 + cc * 128)
                    K.mm(ba.ap[:, :n], wa, HN.ap[:, kc, :n], kc == 0, kc == 7, [wb, HN], [ba])
                bg = K.bank()
                for kc in range(8):
                    wa, wb = Wc(kc, 2048 + cc * 128)
                    K.mm(bg.ap[:, :n], wa, HN.ap[:, kc, :n], kc == 0, kc == 7, [wb, HN], [bg])
                sg = SG[cc % 2]
                K.act(sg.ap[:, :n], bg.ap[:, :n], AF.Sigmoid, [bg, PRM], [sg], bias=pc("bglu", l * 8 + 4 + cc))
                K.stt(zst.ap[:, cc, :n], ba.ap[:, :n], pc("bglu", l * 8 + cc), sg.ap[:, :n], ALU.add, ALU.mult,
                      [ba, sg, PRM], [zst])
            K.dma("sp", zT["w"][:, :, c0:c0 + n].rearrange("c p t -> p c t"), zst.ap[:, :, :n], zst, [zst],
                  [K.D("zT", bi)])
            if meta:
                K.dma("sp", zt_o[:, :, 0, 14:30].rearrange("c p t -> p c t"), zst.ap[:, :, 0:16], zst, [zst],
                      [K.D("xown", l)])
            else:
                for i in range(4):
                    s = (c0 // 128) + i
                    K.dma("sp", zt_o[:, :, s + 1, :].rearrange("c p t -> p c t"),
                          zst.ap[:, :, i * 128 + 98:i * 128 + 128], zst, [zst], [K.D("xown", l)])
            nts = (n + 127) // 128
            for ts_ in range(nts):
                m = min(128, n - ts_ * 128)
                bv = K.bank()
                for kc in range(8):
                    wa, wb = Wv(kc)
                    K.mm(bv.ap[:m, :512], HN.ap[:, kc, ts_ * 128:ts_ * 128 + m], wa, kc == 0, kc == 7, [wb, HN], [bv])
                K.copy("act", vst.ap[:m, ts_, :], bv.ap[:m, :512], [bv], [vst])
            if meta:
                K.dma("sp", vm["w"], vst.ap[:NMETA, 0, :], vst, [vst], [K.D("vm")])
            else:
                K.dma("sp", v_own(l, c0, n).rearrange("(s i) e -> i s e", i=128), vst.ap[:, :, :], vst, [vst],
                      [K.D("xown", l)])
        K.barrier()
        K.release(m0)

    XSEM = {pn: Buf(None, "xsem_" + pn) for (pn, _) in pieces}

    def exchange(l):
        for (pn, rows) in pieces:
            o_, a_ = xown_t[pn][l % 2], xall_t[pn][l % 2]
            K._add("pool", lambda e, o_=o_, a_=a_: e.collective_compute(
                "AllGather", ALU.bypass, replica_groups=[[2 * i, 2 * i + 1] for i in range(n_pairs)],
                ins=[o_.opt()], outs=[a_.opt()]), [K.D("xown", l)], [K.D("xall", l)],
                dbuf=XSEM[pn], inc=1)

    def phaseB(l):
        lam_init = 0.8 - 0.6 * math.exp(-0.3 * l)
        zviews = [z_all(l, r) for r in range(2)]
        m0 = K.mark()
        LJ = K.sb([128, 64], F32, "lj")
        LD = K.sb([128, 4], F32, "ld")
        K.stt(LJ.ap, PRM.ap[:, PL["lq1"] + l * 64:PL["lq1"] + (l + 1) * 64], 1.0,
              PRM.ap[:, PL["lk1"] + l * 64:PL["lk1"] + (l + 1) * 64], ALU.mult, ALU.mult, [PRM], [LJ, LD],
              accum_out=LD.ap[:, 0:1])
        K.stt(LJ.ap, PRM.ap[:, PL["lq2"] + l * 64:PL["lq2"] + (l + 1) * 64], 1.0,
              PRM.ap[:, PL["lk2"] + l * 64:PL["lk2"] + (l + 1) * 64], ALU.mult, ALU.mult, [PRM, LJ], [LJ, LD],
              accum_out=LD.ap[:, 1:2])
        K.act(LD.ap[:, 2:4], LD.ap[:, 0:2], AF.Exp, [LD], [LD])
        K.tt("dve", LD.ap[:, 0:1], LD.ap[:, 2:3], LD.ap[:, 3:4], ALU.subtract, [LD], [LD])
        K.ts("dve", NEGLAM.ap[:, 0:1], LD.ap[:, 0:1], -1.0, ALU.mult, [LD], [NEGLAM], s2=-lam_init, op1=ALU.add)

        KTb = [K.sb([128, NKEY], BF16, "kt"), None]
        Vb = [K.sb([128, NKT, 129], BF16, "v"), None]
        QTb = [K.sb([128, TL], BF16, "qt"), None]
        K.memset("pool", Vb[0].ap[:, :, 128:129], 1.0, [Vb[0]])

        def loadH(hh):
            KT, V, QT = KTb[hh % 2], Vb[hh % 2], QTb[hh % 2]
            K.dma("sp", KT.ap[:, 0:NMETA], kTm["r"][hh], KT, [K.D("kTm")], [KT])
            K.dma("sp", V.ap[:NMETA, 0, 0:128], vm["r"][:, hh * 128:(hh + 1) * 128], V, [K.D("vm")], [V])
            for r in range(2):
                for sa in range(0, NS, 4):
                    sb_ = min(NS, sa + 4)
                    K.dma("sp", KT.ap[:, NMETA:].rearrange("p (s r c) -> p s r c", r=2, c=128)[:, sa:sb_, r, :],
                          k_all(l, r, hh).rearrange("p (s c) -> p s c", c=128)[:, sa:sb_, :], KT, [K.D("xall", l)],
                          [KT])
                    K.dma("sp", V.ap[:, 1:, :].rearrange("p (s r) e -> p s r e", r=2)[:, sa:sb_, r, 0:128],
                          v_all(l, r, sa * 128, (sb_ - sa) * 128).rearrange("(s i) e -> i s e", i=128)[:, :, hh * 128:(hh + 1) * 128],
                          V, [K.D("xall", l)], [V])
            K.dma("sp", QT.ap, qT["r"][hh], QT, [K.D("qT", j) for j in range(len(xblocks))], [QT])

        m1 = K.mark()
        DG = K.sb([128, 4, CW, 128], BF16, "dg")
        for cc in range(4):
            for j in range(CW):
                K.ts("dve", DG.ap[:, cc, j, :], ident, pc("convw", (l * 4 + cc) * CW + j), ALU.mult, [CB, PRM], [DG])
        ZC = [K.sb([128, 4, 4, 158], BF16, "zc") for _ in range(2)]
        CA = [K.sb([128, 4, 4, HALO], BF16, "ca") for _ in range(2)]
        CBB = [K.sb([128, 4, 4, HALO], BF16, "cbb") for _ in range(2)]
        CT = K.sb([128, 4, 4, HALO], F32, "ct")
        Y32b = [K.sb([128, 4, 512], F32, "y32") for _ in range(2)]
        YBF = K.sb([128, 4, 512], BF16, "ybf")
        YSQ = K.sb([128, 4, 512], BF16, "ysq")
        MEAN = K.sb([128, 512], F32, "mean")
        MSQ = K.sb([128, 512], F32, "msq")
        RSD = K.sb([128, 512], F32, "rsd")
        TT = [K.sb([128, 512], F32, "tt") for _ in range(2)]
        CST = [K.sb([128, 4, 512], BF16, "cst") for _ in range(2)]
        def loadB1(bi):
            c0, n = xblocks[bi]
            meta = (n == NMETA)
            nsl, wd = (1, NMETA) if meta else (4, 128)
            zc = ZC[bi % 2]
            for cc in range(4):
                K.dma("sp", zc.ap[:, cc, :nsl, HALO:HALO + wd],
                      zT["r"][cc, :, c0:c0 + n].rearrange("p (s t) -> p s t", t=wd), zc, [K.D("zT", bi)], [zc])
            if not meta:
                ca, cbb = CA[bi % 2], CBB[bi % 2]
                s0 = c0 // 128
                for cc in range(4):
                    K.dma("sp", ca.ap[:, cc, :, :], zviews[0][cc, :, s0 + 1:s0 + 5, :], ca, [K.D("xall", l)], [ca])
                    K.dma("sp", cbb.ap[:, cc, :, :], zviews[1][cc, :, s0:s0 + 4, :], cbb, [K.D("xall", l)], [cbb])

        def frontB1(bi):
            c0, n = xblocks[bi]
            meta = (n == NMETA)
            nsl, wd = (1, NMETA) if meta else (4, 128)
            zc = ZC[bi % 2]
            Y32 = Y32b[bi % 2]
            if meta:
                K.memset("pool", zc.ap[:, :, 0, 0:HALO], 0.0, [zc])
            else:
                ca, cbb = CA[bi % 2], CBB[bi % 2]
                K.ts("dve", CT.ap, ca.ap, pc("sel", 0), ALU.mult, [ca, PRM], [CT])
                K.stt(zc.ap[:, :, :, 0:HALO], cbb.ap, pc("sel", 1), CT.ap, ALU.mult, ALU.add, [cbb, CT, PRM], [zc])
            for cc in range(4):
                bk = K.bank()
                for j in range(CW):
                    K.mm(bk.ap[:, :n].rearrange("p (s t) -> p s t", t=wd), DG.ap[:, cc, j, :],
                         zc.ap[:, cc, :nsl, j:j + wd], j == 0, j == CW - 1, [DG, zc], [bk])
                K.act(Y32.ap[:, cc, :n], bk.ap[:, :n], AF.Identity, [bk, PRM], [Y32], bias=pc("convb", l * 4 + cc))

        def backB1(bi):
            c0, n = xblocks[bi]
            Y32 = Y32b[bi % 2]
            K.copy("pool", YBF.ap[:, :, :n], Y32.ap[:, :, :n], [Y32], [YBF])
            K.act(YSQ.ap[:, :, :n], Y32.ap[:, :, :n], AF.Square, [Y32], [YSQ])
            bs = K.bank()
            for cc in range(4):
                K.mm(bs.ap[:, :n], ones, YBF.ap[:, cc, :n], cc == 0, cc == 3, [YBF, CB], [bs])
            bq = K.bank()
            for cc in range(4):
                K.mm(bq.ap[:, :n], ones, YSQ.ap[:, cc, :n], cc == 0, cc == 3, [YSQ, CB], [bq])
            K.ts("dve", MEAN.ap[:, :n], bs.ap[:, :n], 1.0 / 512, ALU.mult, [bs], [MEAN])
            K.tt("pool", MSQ.ap[:, :n], MEAN.ap[:, :n], MEAN.ap[:, :n], ALU.mult, [MEAN], [MSQ])
            K.stt(RSD.ap[:, :n], bq.ap[:, :n], 1.0 / 512, MSQ.ap[:, :n], ALU.mult, ALU.subtract, [bq, MSQ], [RSD])
            K.act(RSD.ap[:, :n], RSD.ap[:, :n], AF.Sqrt, [RSD], [RSD], bias=EPS, scale=1.0)
            K.recip(RSD.ap[:, :n], RSD.ap[:, :n], [RSD], [RSD])
            cst = CST[bi % 2]
            for cc in range(4):
                t = TT[cc % 2]
                K.tt("dve", t.ap[:, :n], Y32.ap[:, cc, :n], MEAN.ap[:, :n], ALU.subtract, [Y32, MEAN], [t])
                K.tt("pool", t.ap[:, :n], t.ap[:, :n], RSD.ap[:, :n], ALU.mult, [t, RSD], [t])
                K.act(cst.ap[:, cc, :n], t.ap[:, :n], AF.Silu, [t, PRM], [cst], bias=pc("lnb", l * 4 + cc),
                      scale=pc("lng", l * 4 + cc))
            K.dma("sp", mixc_d[:, :, c0:c0 + n].rearrange("c p t -> p c t"), cst.ap[:, :, :n], cst, [cst],
                  [K.D("mixc", bi)])

        nblk = len(xblocks)
        loadB1(0)
        if nblk > 1:
            loadB1(1)
        loadH(0)
        frontB1(0)
        for bi in range(nblk):
            if bi + 2 < nblk:
                loadB1(bi + 2)
            if bi + 1 < nblk:
                frontB1(bi + 1)
            backB1(bi)
        K.barrier()
        K.release(m1)

        KTb[1] = K.sb([128, NKEY], BF16, "kt")
        Vb[1] = K.sb([128, NKT, 129], BF16, "v")
        QTb[1] = K.sb([128, TL], BF16, "qt")
        K.memset("pool", Vb[1].ap[:, :, 128:129], 1.0, [Vb[1]])
        Pb = [K.sb([128, 2, 512], BF16, "p") for _ in range(4)]
        OA = [K.sb([128, 4, 2, 129], F32, "oa") for _ in range(2)]
        RZ = K.sb([128, 4, 2], F32, "rz")
        S1 = K.sb([128, 4], F32, "s1")
        O0 = [K.sb([128, 128], F32, "o0") for _ in range(2)]
        OO = [K.sb([128, 4, 128], F32, "oo") for _ in range(2)]
        JK = K.sb([128, 128], F32, "jk")
        SS = K.sb([128, 4], F32, "ss")
        RSTD = K.sb([128, 4], F32, "rstd")
        AOS = [K.sb([128, 4, 128], BF16, "aos") for _ in range(2)]
        abank = banks[4:8]
        sc_ = 1.0 / (128.0 * (1.0 - lam_init) ** 2)
        bb_ = EPS / ((1.0 - lam_init) ** 2)
        it = 0
        pidx = 0
        for hh in range(4):
            KT, V, QT = KTb[hh % 2], Vb[hh % 2], QTb[hh % 2]
            if hh + 1 < 4:
                loadH(hh + 1)
            qblocks = [(m, 4, 128) for m in range(NB)] + [(NB, 1, NMETA)]
            for (m, nsl, wq) in qblocks:
                meta = (wq == NMETA)
                if meta:
                    ktiles = [(0, NMETA, 0, "M", 0)]
                    qbase = NX
                else:
                    ktiles = [(0, NMETA, 0, None, 0)]
                    for g in range(8 * m + 8):
                        r = g - 8 * m
                        if r < 0:
                            ktiles.append((1 + g, 128, 0, None, 0))
                        else:
                            ktiles.append((1 + g, 128, r // 2, "E" if r % 2 == 0 else "O", r // 2))
                    qbase = 4 * m * 128
                oa = OA[it % 2]
                nkt = len(ktiles)
                pps = {}

                def qk_stage(idx):
                    nonlocal pidx
                    (kt, nk, i0, mk, im) = ktiles[idx]
                    ncols = (nsl - i0) * wq
                    q0 = qbase + i0 * wq
                    k0 = 0 if kt == 0 else NMETA + (kt - 1) * 128
                    bp = bankpair[pidx % 2]
                    pp = Pb[pidx % len(Pb)]
                    pidx += 1
                    pps[idx] = pp
                    for c in range(2):
                        K.mm(bp.ap[:nk, c, :ncols], KT.ap[c * 64:(c + 1) * 64, k0:k0 + nk],
                             QT.ap[c * 64:(c + 1) * 64, q0:q0 + ncols], True, True, [KT, QT], [bp])
                    K.act(pp.ap[:nk, :, :ncols], bp.ap[:nk, :, :ncols], AF.Exp, [bp], [pp], scale=0.125)
                    if mk is not None:
                        mka = {"E": maskE, "O": maskO, "M": maskM}[mk]
                        for c in range(2):
                            a = pp.ap[:nk, c, (im - i0) * wq:(im - i0 + 1) * wq]
                            K.tt("dve", a, a, mka[:nk, :wq], ALU.mult, [pp, CB], [pp])

                def pv_stage(idx):
                    (kt, nk, i0, mk, im) = ktiles[idx]
                    pp = pps.pop(idx)
                    last = idx == nkt - 1
                    for i in range(i0, nsl):
                        for c in range(2):
                            K.mm(abank[i].ap[:wq, c * 256:c * 256 + 129], pp.ap[:nk, c, (i - i0) * wq:(i - i0 + 1) * wq],
                                 V.ap[:nk, kt, :], idx == 0 and c == 0, last, [pp, V], [abank[i]],
                                 skip_group_check=True)

                for idx in range(nkt + 1):
                    if idx < nkt:
                        qk_stage(idx)
                    if idx >= 1:
                        pv_stage(idx - 1)
                for i in range(nsl):
                    K.copy("dve", oa.ap[:wq, i, :, :],
                           abank[i].ap[:wq, :].rearrange("p (c x) -> p c x", x=256)[:, :, 0:129], [abank[i]], [oa])
                K.recip(RZ.ap[:wq, :nsl, :], oa.ap[:wq, :nsl, :, 128], [oa], [RZ])
                K.ts("pool", S1.ap[:wq, :nsl], RZ.ap[:wq, :nsl, 1], NEGLAM.ap[:wq, 0:1], ALU.mult, [RZ, NEGLAM], [S1])
                aos = AOS[it % 2]
                oo = OO[it % 2]
                it += 1
                for i in range(nsl):
                    o0 = O0[i % 2]
                    K.ts("pool", o0.ap[:wq, :], oa.ap[:wq, i, 0, 0:128], RZ.ap[:wq, i, 0:1], ALU.mult, [oa, RZ], [o0])
                    K.stt(oo.ap[:wq, i, :], oa.ap[:wq, i, 1, 0:128], S1.ap[:wq, i:i + 1], o0.ap[:wq, :], ALU.mult,
                          ALU.add, [oa, S1, o0], [oo])
                    K.stt(JK.ap[:wq, :], oo.ap[:wq, i, :], 1.0, oo.ap[:wq, i, :], ALU.mult, ALU.mult, [oo], [JK, SS],
                          accum_out=SS.ap[:wq, i:i + 1])
                K.act(RSTD.ap[:wq, :nsl], SS.ap[:wq, :nsl], AF.Ln, [SS], [RSTD], bias=bb_, scale=sc_)
                K.act(RSTD.ap[:wq, :nsl], RSTD.ap[:wq, :nsl], AF.Exp, [RSTD], [RSTD], scale=-0.5)
                for i in range(nsl):
                    K.stt(aos.ap[:wq, i, :], oo.ap[:wq, i, :], RSTD.ap[:wq, i:i + 1],
                          PRM.ap[:wq, PL["subg"] + l * 128:PL["subg"] + (l + 1) * 128], ALU.mult, ALU.mult,
                          [oo, RSTD, PRM], [aos])
                K.dma("sp", ao_d[qbase:qbase + nsl * wq, hh * 128:(hh + 1) * 128].rearrange("(s q) e -> q s e", q=wq),
                      aos.ap[:wq, :nsl, :], aos, [aos], [K.D("ao", m)])
        K.barrier()
        K.release(m0)

    def phaseC(l, h_src, final):
        m0 = K.mark()
        WO = K.sb([128, 8, D], BF16, "wo")
        for kc in range(8):
            K.dma("pool", WO.ap[:, kc, :], w_out[l, kc * 128:(kc + 1) * 128, :], WO, [], [WO])
        HB = [K.sb([128, 8, 512], F32, "hb") for _ in range(3)]
        AOB = [K.sb([128, 4, 512], BF16, "aob") for _ in range(2)]
        MIX = [K.sb([128, 8, 512], BF16, "mix") for _ in range(2)]
        SQ = K.sb([128, 8, 512], BF16, "sq")
        RS = K.sb([128, 512], F32, "rs")
        HN2 = [K.sb([128, 8, 512], BF16, "hn2") for _ in range(2)]
        def loadC1(bi):
            c0, n = xblocks[bi]
            meta = (n == NMETA)
            nsl, wd = (1, NMETA) if meta else (4, 128)
            hb, aob, mix = HB[bi % 3], AOB[bi % 2], MIX[bi % 2]
            K.dma("sp", hb.ap[:, :, :n], h_src[:, :, c0:c0 + n].rearrange("c p t -> p c t"), hb, [K.D("hT", bi)], [hb])
            K.dma("sp", aob.ap[:wd, :nsl, :], ao_d[c0:c0 + n, :].rearrange("(s q) e -> q s e", q=wd), aob,
                  [K.D("ao", bi)], [aob])
            K.dma("sp", mix.ap[:, 4:8, :n], mixc_d[:, :, c0:c0 + n].rearrange("c p t -> p c t"), mix,
                  [K.D("mixc", bi)], [mix])

        def frontC1(bi):
            c0, n = xblocks[bi]
            meta = (n == NMETA)
            nsl, wd = (1, NMETA) if meta else (4, 128)
            hb, aob, mix = HB[bi % 3], AOB[bi % 2], MIX[bi % 2]
            for hh in range(4):
                bt = K.bank()
                btb = bt.ap.bitcast(BF16)
                for i in range(nsl):
                    K.transpose(btb[:, i * wd:(i + 1) * wd], aob.ap[:wd, i, hh * 128:(hh + 1) * 128], ident[:wd, :wd],
                                [aob, CB], [bt])
                K.copy("act" if hh % 2 == 0 else "dve", mix.ap[:, hh, :n], btb[:, :n], [bt], [mix])
            for oc in range(8):
                by = K.bank()
                for kc in range(8):
                    K.mm(by.ap[:, :n], WO.ap[:, kc, oc * 128:(oc + 1) * 128], mix.ap[:, kc, :n], kc == 0, kc == 7,
                         [WO, mix], [by])
                K.tt("dve", hb.ap[:, oc, :n], hb.ap[:, oc, :n], by.ap[:, :n], ALU.add, [hb, by], [hb])
            K.dma("sp", hmid_d[:, :, c0:c0 + n].rearrange("c p t -> p c t"), hb.ap[:, :, :n], hb, [hb],
                  [K.D("hmid", bi)])

        def backC1(bi):
            c0, n = xblocks[bi]
            hb, hn2 = HB[bi % 3], HN2[bi % 2]
            rmsnorm_block(hb, n, "g2", l * 8, SQ, RS, hn2)
            K.dma("sp", hn2_d[:, :, c0:c0 + n].rearrange("c p t -> p c t"), hn2.ap[:, :, :n], hn2, [hn2],
                  [K.D("hn2", bi)])

        nblk = len(xblocks)
        loadC1(0)
        if nblk > 1:
            loadC1(1)
        frontC1(0)
        for bi in range(nblk):
            if bi + 2 < nblk:
                loadC1(bi + 2)
            if bi + 1 < nblk:
                frontC1(bi + 1)
            backC1(bi)
        K.barrier()
        K.release(m0)
        NG = (FC + 3) // 4
        WGg = [None] * NG
        WUg = [None] * NG
        for g in range(NG):
            nf = min(4, FC - 4 * g)
            WGg[g] = K.sb([128, 8, nf * 128], BF16, f"wg{g}")
            WUg[g] = K.sb([128, 8, nf * 128], BF16, f"wu{g}")
            K.dma("pool", WGg[g].ap, w_gu[l, :, g * 512:g * 512 + nf * 128].rearrange("(kc p) c -> p kc c", p=128),
                  WGg[g], [], [WGg[g]])
            K.dma("pool", WUg[g].ap,
                  w_gu[l, :, DFF + g * 512:DFF + g * 512 + nf * 128].rearrange("(kc p) c -> p kc c", p=128),
                  WUg[g], [], [WUg[g]])
        WDh = [K.sb([128, FC // 2, D], BF16, f"wd{i}") for i in range(2)]
        for i in range(2):
            K.dma("pool", WDh[i].ap,
                  w_dn[l, i * (FC // 2) * 128:(i + 1) * (FC // 2) * 128, :].rearrange("(f p) c -> p f c", p=128),
                  WDh[i], [], [WDh[i]])
        HN = [K.sb([128, 8, 256], BF16, "hn") for _ in range(2)]
        HB2 = [K.sb([128, 8, 256], F32, "hb2") for _ in range(2)]
        ACT = K.sb([128, FC, 256], BF16, "act")
        SG = [K.sb([128, 256], F32, "sg") for _ in range(2)]
        SQ2 = K.sb([128, 8, 256], BF16, "sq2")
        RS2 = K.sb([128, 256], F32, "rs2")
        fblocks = [(j * 256, 256) for j in range(NX // 256)] + [(NX, NMETA)]
        def loadC2(bi):
            c0, n = fblocks[bi]
            sbi = NB if n == NMETA else c0 // 512
            hn, hb = HN[bi % 2], HB2[bi % 2]
            K.dma("sp", hn.ap[:, :, :n], hn2_d[:, :, c0:c0 + n].rearrange("c p t -> p c t"), hn, [K.D("hn2", sbi)], [hn])
            K.dma("sp", hb.ap[:, :, :n], hmid_d[:, :, c0:c0 + n].rearrange("c p t -> p c t"), hb, [K.D("hmid", sbi)],
                  [hb])

        loadC2(0)
        for bi, (c0, n) in enumerate(fblocks):
            meta = (n == NMETA)
            sbi = NB if meta else c0 // 512
            hn, hb = HN[bi % 2], HB2[bi % 2]
            if bi + 1 < len(fblocks):
                loadC2(bi + 1)
            for f in range(FC):
                bg = K.bank()
                for kc in range(8):
                    K.mm(bg.ap[:, :n], WGg[f // 4].ap[:, kc, (f % 4) * 128:(f % 4 + 1) * 128], hn.ap[:, kc, :n],
                         kc == 0, kc == 7, [WGg[f // 4], hn], [bg])
                bu = K.bank()
                for kc in range(8):
                    K.mm(bu.ap[:, :n], WUg[f // 4].ap[:, kc, (f % 4) * 128:(f % 4 + 1) * 128], hn.ap[:, kc, :n],
                         kc == 0, kc == 7, [WUg[f // 4], hn], [bu])
                sg = SG[f % 2]
                K.act(sg.ap[:, :n], bg.ap[:, :n], AF.Silu, [bg], [sg])
                K.tt("dve", ACT.ap[:, f, :n], sg.ap[:, :n], bu.ap[:, :n], ALU.mult, [sg, bu], [ACT])
            for oc in range(8):
                bd = K.bank()
                for f in range(FC):
                    wdb = WDh[f // (FC // 2)]
                    K.mm(bd.ap[:, :n], wdb.ap[:, f % (FC // 2), oc * 128:(oc + 1) * 128], ACT.ap[:, f, :n], f == 0,
                         f == FC - 1, [wdb, ACT], [bd])
                K.tt("dve", hb.ap[:, oc, :n], hb.ap[:, oc, :n], bd.ap[:, :n], ALU.add, [hb, bd], [hb])
            if final:
                if not meta:
                    rmsnorm_block(hb, n, "gf", 0, SQ2, RS2, None, out_dtype_f32_inplace=True)
                    K.dma("sp", outT[:, :, c0:c0 + n].rearrange("c p t -> p c t"), hb.ap[:, :, :n], hb, [hb],
                          [K.D("outT", bi)])
            else:
                K.dma("sp", hT["w"][:, :, c0:c0 + n].rearrange("c p t -> p c t"), hb.ap[:, :, :n], hb, [hb],
                      [K.D("hT", sbi)])
        K.barrier()
        K.release(m0)

    h_written_here = False
    for (ph, l) in stages:
        if ph == "A":
            phaseA(l, xT if l == 0 else (hT["w"] if h_written_here else hT["r"]))
            if fused:
                exchange(l)
        elif ph == "B":
            phaseB(l)
        elif ph == "C":
            hsrc = xT if l == 0 else hT["r"]
            phaseC(l, hsrc, final=(l == DEPTH - 1))
            h_written_here = True
    K.finish()
    K.emit(stack)
    stack.close()
    return nc, ext_in, ext_out


def _bf(a):
    return np.asarray(a, dtype=np.float32).astype(ml_dtypes.bfloat16)


def host_prepare(inputs, NS, DEPTH, B):
    NX = NS * 128
    TL = NX + NMETA
    PL = prm_layout(DEPTH)
    f32 = np.float32
    x = np.asarray(inputs["x"], f32)
    meta = np.asarray(inputs["meta_tokens"], f32)
    w_in = np.asarray(inputs["w_in"], f32)
    perm = np.arange(512).reshape(4, 2, 64)
    perm = np.concatenate([perm[:, :, 32:], perm[:, :, :32]], axis=-1).reshape(512)
    w_in_ext = np.ascontiguousarray(np.concatenate([w_in, w_in[:, :, perm], w_in[:, :, 512 + perm]], axis=-1))
    w_out = np.ascontiguousarray(np.asarray(inputs["w_out"], f32))
    w_gu = np.ascontiguousarray(np.asarray(inputs["w_gate_up"], f32))
    w_dn = np.ascontiguousarray(np.asarray(inputs["w_down"], f32))

    def colmajor(v, nch):
        v = np.asarray(v, f32)
        lead = v.shape[:-1]
        v = v.reshape(*lead, nch, 128)
        v = np.moveaxis(v, -1, 0)
        return v.reshape(128, -1)

    def rep(v):
        v = np.asarray(v, f32).reshape(1, -1)
        return np.broadcast_to(v, (128, v.shape[1]))

    prm_base = np.zeros((128, PL["_n"]), f32)
    prm_base[:, PL["g1"]:PL["g1"] + DEPTH * 8] = colmajor(inputs["norm1_g"], 8)
    prm_base[:, PL["g2"]:PL["g2"] + DEPTH * 8] = colmajor(inputs["norm2_g"], 8)
    prm_base[:, PL["gf"]:PL["gf"] + 8] = colmajor(inputs["final_g"], 8)
    prm_base[:, PL["bglu"]:PL["bglu"] + DEPTH * 8] = colmajor(inputs["b_glu"], 8)
    cw = np.asarray(inputs["conv_w"], f32)
    cw = cw.reshape(DEPTH, CW, 4, 128).transpose(3, 0, 2, 1).reshape(128, -1)
    prm_base[:, PL["convw"]:PL["convw"] + DEPTH * 4 * CW] = cw
    prm_base[:, PL["convb"]:PL["convb"] + DEPTH * 4] = colmajor(inputs["conv_b"], 4)
    prm_base[:, PL["lng"]:PL["lng"] + DEPTH * 4] = colmajor(inputs["conv_ln_g"], 4)
    prm_base[:, PL["lnb"]:PL["lnb"] + DEPTH * 4] = colmajor(inputs["conv_ln_b"], 4)
    for nm, key in (("lq1", "lam_q1"), ("lk1", "lam_k1"), ("lq2", "lam_q2"), ("lk2", "lam_k2")):
        prm_base[:, PL[nm]:PL[nm] + DEPTH * 64] = rep(inputs[key])
    prm_base[:, PL["subg"]:PL["subg"] + DEPTH * 128] = rep(inputs["subln_g"])

    inv = (10000.0 ** (-np.arange(0, 64, 2, dtype=f32) / 64.0)).astype(f32)
    tri = (np.arange(128)[:, None] <= np.arange(128)[None, :]).astype(f32)
    in_maps = []
    for b in range(B):
        for p in range(2):
            tiles = [x[b, 128 * (2 * s + p):128 * (2 * s + p) + 128, :] for s in range(NS)]
            hx = np.concatenate(tiles + [meta], axis=0)
            xTc = np.ascontiguousarray(hx.T).reshape(8, 128, TL)
            pos = np.concatenate([NMETA + 128 * (2 * s + p) + np.arange(128) for s in range(NS)] + [np.arange(NMETA)])
            ang = pos.astype(f32)[:, None] * inv[None, :]
            ang = np.concatenate([ang, ang], axis=-1).astype(f32)
            cosd = np.cos(ang).astype(f32).T
            sind = np.sin(ang).astype(f32).T
            sign = np.where(np.arange(64) < 32, -1.0, 1.0).astype(f32)[:, None]
            sind = sind * sign
            cosT = np.ascontiguousarray(np.concatenate([cosd, cosd], axis=0))
            sinT = np.ascontiguousarray(np.concatenate([sind, sind], axis=0))
            cb = np.zeros((128, CB_N), f32)
            cb[:, CB_IDENT:CB_IDENT + 128] = np.eye(128, dtype=f32)
            cb[:, CB_ONES:CB_ONES + 128] = 1.0
            cb[:, CB_ME:CB_ME + 128] = tri if p == 0 else 1.0
            cb[:, CB_MO:CB_MO + 128] = 0.0 if p == 0 else tri
            cb[:16, CB_MM:CB_MM + 16] = tri[:16, :16]
            prm = prm_base.copy()
            prm[:, PL["sel"]] = 1.0 if p == 1 else 0.0
            prm[:, PL["sel"] + 1] = 0.0 if p == 1 else 1.0
            in_maps.append({"xT": xTc, "prm": prm, "cb": _bf(cb), "cosT": cosT, "sinT": sinT, "w_in": w_in_ext,
                            "w_out": w_out, "w_gu": w_gu, "w_dn": w_dn})
    return in_maps


def assemble_output(outs, NS, B):
    NX = NS * 128
    S = 2 * NX
    out = np.empty((B, S, D), np.float32)
    for b in range(B):
        for p in range(2):
            o = np.asarray(outs[2 * b + p]).reshape(D, NX).T
            for s in range(NS):
                g = 2 * s + p
                out[b, 128 * g:128 * g + 128, :] = o[128 * s:128 * s + 128, :]
    return out


_PROG_CACHE = {}


def run_model(inputs, NS, DEPTH, B, fused=True):
    n_cores = 2 * B
    in_maps = host_prepare(inputs, NS, DEPTH, B)
    if fused:
        stages = []
        for l in range(DEPTH):
            stages += [("A", l), ("B", l), ("C", l)]
        key = ("f", NS, DEPTH, B)
        if key not in _PROG_CACHE:
            _PROG_CACHE[key] = build_program(NS, DEPTH, stages, True, B)
        nc, ext_in, ext_out = _PROG_CACHE[key]
        res = run_bass_kernel_spmd(nc, [{k: m[k] for k in ext_in} for m in in_maps], core_ids=list(range(n_cores)))
        return assemble_output([r["outT"] for r in res.results], NS, B)
    groups = [[("A", 0)]]
    for l in range(DEPTH):
        g = [("B", l), ("C", l)]
        if l + 1 < DEPTH:
            g.append(("A", l + 1))
        groups.append(g)
    state = [dict() for _ in range(n_cores)]
    outs = None
    for gi, g in enumerate(groups):
        nc, ext_in, ext_out = build_program(NS, DEPTH, g, False, B)
        maps = []
        for c in range(n_cores):
            m = {}
            for k in ext_in:
                if k in in_maps[c]:
                    m[k] = in_maps[c][k]
                elif k.endswith("_in"):
                    base = k[:-3]
                    if base in state[c]:
                        m[k] = state[c][base]
                    else:
                        shp, dt = _shape_of(nc, k)
                        m[k] = np.zeros(shp, dt)
                else:
                    raise KeyError(k)
            maps.append(m)
        res = run_bass_kernel_spmd(nc, maps, core_ids=list(range(n_cores)))
        for c in range(n_cores):
            r = res.results[c]
            for k in ext_out:
                if k.endswith("_out"):
                    state[c][k[:-4]] = np.asarray(r[k])
        for b in range(B):
            for kname in list(state[2 * b].keys()):
                if kname.startswith("xo_"):
                    xa = np.concatenate([state[2 * b][kname], state[2 * b + 1][kname]], axis=0)
                    state[2 * b]["xa_" + kname[3:]] = xa
                    state[2 * b + 1]["xa_" + kname[3:]] = xa
        outs = [np.asarray(r["outT"]) for r in res.results]
    return assemble_output(outs, NS, B)


def _shape_of(nc, name):
    for alloc in nc.allocations:
        if isinstance(alloc, mybir.MemoryLocationSet) and alloc.memorylocations and alloc.memorylocations[0].name == name:
            return tuple(alloc.tensor_shape), mybir.dt.np(alloc.dtype)
    raise KeyError(name)


def kernel(**inputs):
    return run_model(inputs, NS=32, DEPTH=4, B=4, fused=True)
```

```python
import math
import contextlib
import numpy as np
import ml_dtypes
import concourse.bass as bass
import concourse.mybir as mybir
from concourse.bass_utils import run_bass_kernel_spmd

F32 = mybir.dt.float32
BF16 = mybir.dt.bfloat16
AF = mybir.ActivationFunctionType
ALU = mybir.AluOpType

D = 1024
KC = 8
DFF = 2816
FC = 22
NMETA = 16
CW = 31
HALO = 30
EPS = 1e-5
WIN_COLS = 3584
PIECE_ROWS = 2048


class Ins:
    __slots__ = ("eng", "fn", "deps", "signal", "val", "key", "slot", "inc")


class Slot:
    def __init__(self):
        self.count = 0
        self.sem = None


class Buf:
    def __init__(self, ap=None, name=""):
        self.ap = ap
        self.name = name
        self.w = {}
        self.r = {}
        self.slot = None


class Kern:
    def __init__(self, nc, arena, arena_cols, banks):
        self.nc = nc
        self.engs = {n: [] for n in ("pe", "act", "dve", "pool", "sp")}
        self.pending = {n: [] for n in self.engs}
        self.last = {}
        self.arena = arena
        self.arena_cols = arena_cols
        self.off = 0
        self.banks = banks
        self.bank_i = 0
        self.slots = []
        self.free_slots = []
        self.live = []
        self.dram_bufs = {}

    def sb(self, shape, dtype, name=""):
        esz = 4 if dtype == F32 else 2
        n = 1
        for x in shape[1:]:
            n *= x
        nbytes = (n * esz + 63) // 64 * 64
        o = self.off
        self.off += nbytes
        assert self.off <= self.arena_cols * 2, f"SBUF arena overflow at {name}: {self.off}"
        ap = self.arena[:, o // 2:(o + n * esz) // 2]
        if dtype == F32:
            ap = ap.bitcast(F32)
        if len(shape) == 3:
            ap = ap.rearrange("p (a b) -> p a b", b=shape[2])
        elif len(shape) == 4:
            ap = ap.rearrange("p (a b c) -> p a b c", b=shape[2], c=shape[3])
        b = Buf(ap, name)
        self.live.append((o, b))
        return b

    def mark(self):
        return self.off

    def release(self, m):
        keep = []
        for (o, b) in self.live:
            if o >= m:
                if b.slot is not None:
                    self.free_slots.append(b.slot)
                    b.slot = None
            else:
                keep.append((o, b))
        self.live = keep
        self.off = m

    def bank(self):
        b = self.banks[self.bank_i % len(self.banks)]
        self.bank_i += 1
        return b

    def D(self, name, j=0):
        k = (name, j)
        if k not in self.dram_bufs:
            self.dram_bufs[k] = Buf(None, f"{name}:{j}")
        return self.dram_bufs[k]

    def _add(self, eng, fn, reads, writes, dbuf=None, inc=16):
        ins = Ins()
        ins.eng = eng
        ins.fn = fn
        ins.signal = False
        ins.val = None
        ins.slot = None
        ins.inc = 1
        if dbuf is not None:
            if dbuf.slot is None:
                if self.free_slots:
                    dbuf.slot = self.free_slots.pop()
                else:
                    dbuf.slot = Slot()
                    self.slots.append(dbuf.slot)
            sl = dbuf.slot
            ins.slot = sl
            ins.inc = inc
            ins.key = ("d", id(sl))
            sl.count += inc
            ins.val = sl.count
            ins.signal = True
        else:
            ins.key = eng
        deps = {}
        for b in reads:
            for d in b.w.values():
                deps[id(d)] = d
        for b in writes:
            for d in b.w.values():
                deps[id(d)] = d
            for d in b.r.values():
                deps[id(d)] = d
        for d in self.pending[eng]:
            deps[id(d)] = d
        self.pending[eng] = []
        ins.deps = [d for d in deps.values() if not (d.key == "pe" and eng == "pe")]
        for d in ins.deps:
            d.signal = True
        for b in writes:
            b.w[ins.key] = ins
            b.r = {}
        for b in reads:
            b.r[ins.key] = ins
        self.engs[eng].append(ins)
        self.last[ins.key] = ins
        return ins

    def barrier(self):
        lst = list(self.last.values())
        for n in self.engs:
            self.pending[n] = list(lst)

    def finish(self):
        self.barrier()
        for n in self.engs:
            self._add(n, None, [], [])

    def mm(self, out, lhsT, rhs, start, stop, reads, writes, **kw):
        return self._add("pe", lambda e: e.matmul(out, lhsT, rhs, start=start, stop=stop, **kw), reads, writes)

    def transpose(self, out, in_, ident, reads, writes):
        return self._add("pe", lambda e: e.transpose(out, in_, ident), reads, writes)

    def act(self, out, in_, func, reads, writes, bias=None, scale=None):
        kw = {}
        if bias is not None:
            kw["bias"] = bias
        if scale is not None:
            kw["scale"] = scale
        return self._add("act", lambda e: e.activation(out, in_, func, **kw), reads, writes)

    def tt(self, eng, out, in0, in1, op, reads, writes):
        return self._add(eng, lambda e: e.tensor_tensor(out, in0, in1, op), reads, writes)

    def ts(self, eng, out, in0, s1, op0, reads, writes, s2=None, op1=None):
        if op1 is None:
            return self._add(eng, lambda e: e.tensor_scalar(out, in0, s1, None, op0), reads, writes)
        return self._add(eng, lambda e: e.tensor_scalar(out, in0, s1, s2, op0, op1), reads, writes)

    def stt(self, out, in0, scalar, in1, op0, op1, reads, writes, accum_out=None):
        return self._add("dve", lambda e: e.scalar_tensor_tensor(out, in0, scalar, in1, op0, op1, accum_out=accum_out),
                         reads, writes)

    def copy(self, eng, out, in_, reads, writes):
        if eng == "act":
            return self._add("act", lambda e: e.activation(out, in_, AF.Copy), reads, writes)
        return self._add(eng, lambda e: e.tensor_copy(out, in_), reads, writes)

    def recip(self, out, in_, reads, writes):
        return self._add("dve", lambda e: e.reciprocal(out, in_), reads, writes)

    def memset(self, eng, ap, val, writes):
        return self._add(eng, lambda e: e.memset(ap, val), [], writes)

    def dma(self, q, out, in_, sbuf, reads, writes):
        return self._add(q, lambda e: e.dma_start(out=out, in_=in_), reads, writes, dbuf=sbuf)

    def emit(self, stack):
        nc = self.nc
        esem = {n: stack.enter_context(nc.semaphore("es_" + n)) for n in self.engs}
        for i, sl in enumerate(self.slots):
            sl.sem = stack.enter_context(nc.semaphore(f"ds{i}"))
        for n, lst in self.engs.items():
            c = 0
            for ins in lst:
                if ins.slot is None and ins.signal:
                    c += 1
                    ins.val = c

        def sem_of(ins):
            return ins.slot.sem if ins.slot is not None else esem[ins.eng]

        def body_for(name):
            def body(e):
                waited = {}
                for ins in self.engs[name]:
                    for d in sorted(ins.deps, key=lambda d: d.val):
                        sm = sem_of(d)
                        k = id(sm)
                        if waited.get(k, 0) >= d.val:
                            continue
                        e.wait_ge(sm, d.val)
                        waited[k] = d.val
                    if ins.fn is None:
                        continue
                    bi = ins.fn(e)
                    if ins.signal:
                        bi.then_inc(sem_of(ins), ins.inc)
            return body

        with nc.Block() as block:
            block.tensor(body_for("pe"))
            block.scalar(body_for("act"))
            block.vector(body_for("dve"))
            block.gpsimd(body_for("pool"))
            block.sync(body_for("sp"))


def prm_layout(depth):
    o = {}
    c = 0
    for name, n in (("g1", depth * 8), ("g2", depth * 8), ("gf", 8), ("bglu", depth * 8), ("convw", depth * 4 * CW),
                    ("convb", depth * 4), ("lng", depth * 4), ("lnb", depth * 4), ("sel", 2),
                    ("lq1", depth * 64), ("lk1", depth * 64), ("lq2", depth * 64), ("lk2", depth * 64),
                    ("subg", depth * 128)):
        o[name] = c
        c += n
    o["_n"] = c
    return o


CB_IDENT, CB_ONES, CB_ME, CB_MO, CB_MM, CB_N = 0, 128, 256, 384, 512, 528


def build_program(NS, DEPTH, stages, fused, n_pairs):
    NX = NS * 128
    TL = NX + NMETA
    NB = NS // 4
    NKT = 1 + 2 * NS
    NKEY = NMETA + 2 * NX
    OV = 4 * 128 * NX
    OZ = OV + NX * 512
    NXCH = OZ + 4 * 128 * (NS + 1) * HALO
    XR = NXCH // 512
    PL = prm_layout(DEPTH)

    nc = bass.Bass("TRN2", target_bir_lowering=False)
    stack = contextlib.ExitStack()

    phases = set(stages)
    first_stage = stages[0]
    last_stage = stages[-1]

    produced = {}
    ext_in = []
    ext_out = []

    def dram(name, shape, dtype, role):
        kind = {"in": "ExternalInput", "out": "ExternalOutput", "tmp": "Internal"}[role]
        t = nc.dram_tensor(name, list(shape), dtype, kind=kind)
        if role == "in":
            ext_in.append(name)
        if role == "out":
            ext_out.append(name)
        return t.ap()

    xT = dram("xT", [8, 128, TL], F32, "in")
    prm_d = dram("prm", [128, PL["_n"]], F32, "in")
    cb_d = dram("cb", [128, CB_N], BF16, "in")
    cos_d = dram("cosT", [128, TL], F32, "in")
    sin_d = dram("sinT", [128, TL], F32, "in")
    w_in = dram("w_in", [DEPTH, D, WIN_COLS], F32, "in")
    w_out = dram("w_out", [DEPTH, D, D], F32, "in")
    w_gu = dram("w_gu", [DEPTH, D, 2 * DFF], F32, "in")
    w_dn = dram("w_dn", [DEPTH, DFF, D], F32, "in")

    def handoff(name, shape, dtype, producer_phase_of, consumer_phases_of):
        if fused:
            ap = dram(name, shape, dtype, "tmp")
            return {"r": ap, "w": ap}
        res = {}
        res["r"] = dram(name + "_in", shape, dtype, "in")
        res["w"] = dram(name + "_out", shape, dtype, "out")
        return res

    hT = handoff("hT", [8, 128, TL], F32, None, None)
    qT = handoff("qT", [4, 128, TL], BF16, None, None)
    zT = handoff("zT", [4, 128, TL], BF16, None, None)
    kTm = handoff("kTm", [4, 128, NMETA], BF16, None, None)
    vm = handoff("vm", [NMETA, 512], BF16, None, None)
    PR = PIECE_ROWS
    HP = min(4, (PR * 4) // NX)
    TP = min(NX, PR)
    pieces = [(f"k{j}", HP * NX // 4) for j in range(4 // HP)] + [(f"v{j}", TP) for j in range(NX // TP)] + \
             [("z", 30 * (NS + 1))]
    xown_t = {}
    xall_t = {}
    for (pn, rows) in pieces:
        if fused:
            xown_t[pn] = [dram(f"xo_{pn}{i}", [rows, 512], BF16, "tmp") for i in range(2)]
            xall_t[pn] = [dram(f"xa_{pn}{i}", [2 * rows, 512], BF16, "tmp") for i in range(2)]
        else:
            o_ = dram(f"xo_{pn}_out", [rows, 512], BF16, "out")
            a_ = dram(f"xa_{pn}_in", [2 * rows, 512], BF16, "in")
            xown_t[pn] = [o_, o_]
            xall_t[pn] = [a_, a_]

    def _flat(t):
        return t.rearrange("r c -> (r c)")

    def k_own(l, j):
        return _flat(xown_t[f"k{j}"][l % 2]).rearrange("(h p t) -> h p t", h=HP, p=128)

    def k_all(l, r, h):
        j, hl = h // HP, h % HP
        sz = HP * 128 * NX
        return _flat(xall_t[f"k{j}"][l % 2])[r * sz:(r + 1) * sz].rearrange("(h p t) -> h p t", h=HP, p=128)[hl]

    def v_own(l, t0, n):
        j = t0 // TP
        return xown_t[f"v{j}"][l % 2][t0 - j * TP:t0 - j * TP + n, :]

    def v_all(l, r, t0, n):
        j = t0 // TP
        return xall_t[f"v{j}"][l % 2][r * TP + t0 - j * TP:r * TP + t0 - j * TP + n, :]

    def z_own(l):
        return _flat(xown_t["z"][l % 2]).rearrange("(c p s t) -> c p s t", c=4, p=128, s=NS + 1)

    def z_all(l, r):
        sz = 4 * 128 * (NS + 1) * HALO
        return _flat(xall_t["z"][l % 2])[r * sz:(r + 1) * sz].rearrange("(c p s t) -> c p s t", c=4, p=128, s=NS + 1)

    ao_d = dram("ao", [TL, 512], BF16, "tmp")
    mixc_d = dram("mixc", [4, 128, TL], BF16, "tmp")
    hmid_d = dram("hmid", [8, 128, TL], F32, "tmp")
    hn2_d = dram("hn2", [8, 128, TL], BF16, "tmp")
    outT = dram("outT", [8, 128, NX], F32, "out")

    ARENA_COLS = 94 * 1024
    arena = stack.enter_context(nc.sbuf_tensor("arena", [128, ARENA_COLS], BF16))
    psum_all = stack.enter_context(nc.psum_tensor("psall", [128, 8 * 512], F32))
    banks = [Buf(psum_all[:, i * 512:(i + 1) * 512], f"bank{i}") for i in range(8)]
    bankpair = [Buf(psum_all[:, 0:1024].rearrange("p (c x) -> p c x", x=512), "bp0"),
                Buf(psum_all[:, 1024:2048].rearrange("p (c x) -> p c x", x=512), "bp1")]
    K = Kern(nc, arena, ARENA_COLS, banks)

    PRM = K.sb([128, PL["_n"]], F32, "prm")
    CB = K.sb([128, CB_N], BF16, "cb")
    NEGLAM = K.sb([128, 4], F32, "neglam")
    K.dma("sp", PRM.ap, prm_d, PRM, [], [PRM])
    K.dma("sp", CB.ap, cb_d, CB, [], [CB])
    ident = CB.ap[:, CB_IDENT:CB_IDENT + 128]
    ones = CB.ap[:, CB_ONES:CB_ONES + 128]
    maskE = CB.ap[:, CB_ME:CB_ME + 128]
    maskO = CB.ap[:, CB_MO:CB_MO + 128]
    maskM = CB.ap[:, CB_MM:CB_MM + 16]

    def pc(name, idx):
        o = PL[name] + idx
        return PRM.ap[:, o:o + 1]

    xblocks = [(j * 512, 512) for j in range(NB)] + [(NX, NMETA)]

    def rmsnorm_block(hb, n, gname, gidx0, SQ, RS, HN, out_dtype_f32_inplace=False):
        K.act(SQ.ap[:, :, :n], hb.ap[:, :, :n], AF.Square, [hb], [SQ])
        bk = K.bank()
        for c in range(8):
            K.mm(bk.ap[:, :n], ones, SQ.ap[:, c, :n], c == 0, c == 7, [SQ, CB], [bk])
        K.act(RS.ap[:, :n], bk.ap[:, :n], AF.Sqrt, [bk], [RS], bias=EPS, scale=1.0 / D)
        K.recip(RS.ap[:, :n], RS.ap[:, :n], [RS], [RS])
        for c in range(8):
            dst = hb if out_dtype_f32_inplace else HN
            K.stt(dst.ap[:, c, :n], hb.ap[:, c, :n], pc(gname, gidx0 + c), RS.ap[:, :n], ALU.mult, ALU.mult,
                  [hb, RS, PRM], [dst])

    def phaseA(l, h_src):
        m0 = K.mark()
        WGR = [None] * 7
        for g in (0, 5, 1, 6, 3, 4, 2):
            WGR[g] = K.sb([128, 8, 512], BF16, f"W_in{g}")
            K.dma("pool", WGR[g].ap, w_in[l, :, g * 512:(g + 1) * 512].rearrange("(kc p) c -> p kc c", p=128),
                  WGR[g], [], [WGR[g]])

        def Wc(kc, col):
            g, o = col // 512, col % 512
            return WGR[g].ap[:, kc, o:o + 128], WGR[g]

        def Wv(kc):
            return WGR[2].ap[:, kc, :], WGR[2]
        HB = [K.sb([128, 8, 512], F32, "hb") for _ in range(2)]
        CS = [K.sb([128, 2, 512], F32, "cs") for _ in range(2)]
        SQ = K.sb([128, 8, 512], BF16, "sq")
        RS = K.sb([128, 512], F32, "rs")
        HNb = [K.sb([128, 8, 512], BF16, "hn") for _ in range(2)]
        T1 = [K.sb([128, 512], F32, "t1") for _ in range(2)]
        T2 = [K.sb([128, 512], F32, "t2") for _ in range(2)]
        SG = [K.sb([128, 512], F32, "sg") for _ in range(2)]
        QST = [K.sb([128, 4, 512], BF16, "qst") for _ in range(2)]
        KST = [K.sb([128, 4, 512], BF16, "kst") for _ in range(2)]
        ZST = [K.sb([128, 4, 512], BF16, "zst") for _ in range(2)]
        VST = [K.sb([128, 4, 512], BF16, "vst") for _ in range(2)]
        ZER = K.sb([128, 4, 16], BF16, "zer")
        zt_o = z_own(l)
        K.memset("pool", ZER.ap, 0.0, [ZER])
        K.dma("sp", zt_o[:, :, 0, 0:14].rearrange("c p t -> p c t"), ZER.ap[:, :, 0:14], ZER, [ZER], [K.D("xown", l)])
        tcount = 0
        def loadA(bi):
            c0, n = xblocks[bi]
            hb = HB[bi % 2]
            cs = CS[bi % 2]
            K.dma("sp", hb.ap[:, :, :n], h_src[:, :, c0:c0 + n].rearrange("c p t -> p c t"), hb, [K.D("hT", bi)], [hb])
            K.dma("sp", cs.ap[:, 0, :n], cos_d[:, c0:c0 + n], cs, [], [cs])
            K.dma("sp", cs.ap[:, 1, :n], sin_d[:, c0:c0 + n], cs, [], [cs])

        loadA(0)
        rmsnorm_block(HB[0], xblocks[0][1], "g1", l * 8, SQ, RS, HNb[0])
        for bi, (c0, n) in enumerate(xblocks):
            meta = (n == NMETA)
            hb = HB[bi % 2]
            cs = CS[bi % 2]
            HN = HNb[bi % 2]
            if bi + 1 < len(xblocks):
                loadA(bi + 1)
                rmsnorm_block(HB[(bi + 1) % 2], xblocks[bi + 1][1], "g1", l * 8, SQ, RS, HNb[(bi + 1) % 2])
            qst, kst, zst, vst = QST[bi % 2], KST[bi % 2], ZST[bi % 2], VST[bi % 2]
            for which, st in (("q", qst), ("k", kst)):
                for hh in range(4):
                    oc = (0 if which == "q" else 512) + hh * 128
                    ocr = 2560 + (0 if which == "q" else 512) + hh * 128
                    b1 = K.bank()
                    for kc in range(8):
                        wa, wb = Wc(kc, oc)
                        K.mm(b1.ap[:, :n], wa, HN.ap[:, kc, :n], kc == 0, kc == 7, [wb, HN], [b1])
                    b2 = K.bank()
                    for kc in range(8):
                        wa, wb = Wc(kc, ocr)
                        K.mm(b2.ap[:, :n], wa, HN.ap[:, kc, :n], kc == 0, kc == 7, [wb, HN], [b2])
                    t1 = T1[tcount % 2]
                    t2 = T2[tcount % 2]
                    tcount += 1
                    K.tt("dve", t1.ap[:, :n], b1.ap[:, :n], cs.ap[:, 0, :n], ALU.mult, [b1, cs], [t1])
                    K.tt("dve", t2.ap[:, :n], b2.ap[:, :n], cs.ap[:, 1, :n], ALU.mult, [b2, cs], [t2])
                    K.tt("pool", st.ap[:, hh, :n], t1.ap[:, :n], t2.ap[:, :n], ALU.add, [t1, t2], [st])
            K.dma("sp", qT["w"][:, :, c0:c0 + n].rearrange("h p t -> p h t"), qst.ap[:, :, :n], qst, [qst],
                  [K.D("qT", bi)])
            if meta:
                K.dma("sp", kTm["w"].rearrange("h p t -> p h t"), kst.ap[:, :, :n], kst, [kst], [K.D("kTm")])
            else:
                for j in range(4 // HP):
                    K.dma("sp", k_own(l, j)[:, :, c0:c0 + n].rearrange("h p t -> p h t"),
                          kst.ap[:, j * HP:(j + 1) * HP, :n], kst, [kst], [K.D("xown", l)])
            for cc in range(4):
                ba = K.bank()
                for kc in range(8):
                    wa, wb = Wc(kc, 1536 + cc * 128)
                    K.mm(ba.ap[:, :n], wa, HN.ap[:, kc, :n], kc == 0, kc == 7, [wb, HN], [ba])
                bg = K.bank()
                for kc in range(8):
                    wa, wb = Wc(kc, 2048 + cc * 128)
                    K.mm(bg.ap[:, :n], wa, HN.ap[:, kc, :n], kc == 0, kc == 7, [wb, HN], [bg])
                sg = SG[cc % 2]
                K.act(sg.ap[:, :n], bg.ap[:, :n], AF.Sigmoid, [bg, PRM], [sg], bias=pc("bglu", l * 8 + 4 + cc))
                K.stt(zst.ap[:, cc, :n], ba.ap[:, :n], pc("bglu", l * 8 + cc), sg.ap[:, :n], ALU.add, ALU.mult,
                      [ba, sg, PRM], [zst])
            K.dma("sp", zT["w"][:, :, c0:c0 + n].rearrange("c p t -> p c t"), zst.ap[:, :, :n], zst, [zst],
                  [K.D("zT", bi)])
            if meta:
                K.dma("sp", zt_o[:, :, 0, 14:30].rearrange("c p t -> p c t"), zst.ap[:, :, 0:16], zst, [zst],
                      [K.D("xown", l)])
            else:
                for i in range(4):
                    s = (c0 // 128) + i
                    K.dma("sp", zt_o[:, :, s + 1, :].rearrange("c p t -> p c t"),
                          zst.ap[:, :, i * 128 + 98:i * 128 + 128], zst, [zst], [K.D("xown", l)])
            nts = (n + 127) // 128
            for ts_ in range(nts):
                m = min(128, n - ts_ * 128)
                bv = K.bank()
                for kc in range(8):
                    wa, wb = Wv(kc)
                    K.mm(bv.ap[:m, :512], HN.ap[:, kc, ts_ * 128:ts_ * 128 + m], wa, kc == 0, kc == 7, [wb, HN], [bv])
                K.copy("act", vst.ap[:m, ts_, :], bv.ap[:m, :512], [bv], [vst])
            if meta:
                K.dma("sp", vm["w"], vst.ap[:NMETA, 0, :], vst, [vst], [K.D("vm")])
            else:
                K.dma("sp", v_own(l, c0, n).rearrange("(s i) e -> i s e", i=128), vst.ap[:, :, :], vst, [vst],
                      [K.D("xown", l)])
        K.barrier()
        K.release(m0)

    XSEM = {pn: Buf(None, "xsem_" + pn) for (pn, _) in pieces}

    def exchange(l):
        for (pn, rows) in pieces:
            o_, a_ = xown_t[pn][l % 2], xall_t[pn][l % 2]
            K._add("pool", lambda e, o_=o_, a_=a_: e.collective_compute(
                "AllGather", ALU.bypass, replica_groups=[[2 * i, 2 * i + 1] for i in range(n_pairs)],
                ins=[o_.opt()], outs=[a_.opt()]), [K.D("xown", l)], [K.D("xall", l)],
                dbuf=XSEM[pn], inc=1)

    def phaseB(l):
        lam_init = 0.8 - 0.6 * math.exp(-0.3 * l)
        zviews = [z_all(l, r) for r in range(2)]
        m0 = K.mark()
        LJ = K.sb([128, 64], F32, "lj")
        LD = K.sb([128, 4], F32, "ld")
        K.stt(LJ.ap, PRM.ap[:, PL["lq1"] + l * 64:PL["lq1"] + (l + 1) * 64], 1.0,
              PRM.ap[:, PL["lk1"] + l * 64:PL["lk1"] + (l + 1) * 64], ALU.mult, ALU.mult, [PRM], [LJ, LD],
              accum_out=LD.ap[:, 0:1])
        K.stt(LJ.ap, PRM.ap[:, PL["lq2"] + l * 64:PL["lq2"] + (l + 1) * 64], 1.0,
              PRM.ap[:, PL["lk2"] + l * 64:PL["lk2"] + (l + 1) * 64], ALU.mult, ALU.mult, [PRM, LJ], [LJ, LD],
              accum_out=LD.ap[:, 1:2])
        K.act(LD.ap[:, 2:4], LD.ap[:, 0:2], AF.Exp, [LD], [LD])
        K.tt("dve", LD.ap[:, 0:1], LD.ap[:, 2:3], LD.ap[:, 3:4], ALU.subtract, [LD], [LD])
        K.ts("dve", NEGLAM.ap[:, 0:1], LD.ap[:, 0:1], -1.0, ALU.mult, [LD], [NEGLAM], s2=-lam_init, op1=ALU.add)

        KTb = [K.sb([128, NKEY], BF16, "kt"), None]
        Vb = [K.sb([128, NKT, 129], BF16, "v"), None]
        QTb = [K.sb([128, TL], BF16, "qt"), None]
        K.memset("pool", Vb[0].ap[:, :, 128:129], 1.0, [Vb[0]])

        def loadH(hh):
            KT, V, QT = KTb[hh % 2], Vb[hh % 2], QTb[hh % 2]
            K.dma("sp", KT.ap[:, 0:NMETA], kTm["r"][hh], KT, [K.D("kTm")], [KT])
            K.dma("sp", V.ap[:NMETA, 0, 0:128], vm["r"][:, hh * 128:(hh + 1) * 128], V, [K.D("vm")], [V])
            for r in range(2):
                for sa in range(0, NS, 4):
                    sb_ = min(NS, sa + 4)
                    K.dma("sp", KT.ap[:, NMETA:].rearrange("p (s r c) -> p s r c", r=2, c=128)[:, sa:sb_, r, :],
                          k_all(l, r, hh).rearrange("p (s c) -> p s c", c=128)[:, sa:sb_, :], KT, [K.D("xall", l)],
                          [KT])
                    K.dma("sp", V.ap[:, 1:, :].rearrange("p (s r) e -> p s r e", r=2)[:, sa:sb_, r, 0:128],
                          v_all(l, r, sa * 128, (sb_ - sa) * 128).rearrange("(s i) e -> i s e", i=128)[:, :, hh * 128:(hh + 1) * 128],
                          V, [K.D("xall", l)], [V])
            K.dma("sp", QT.ap, qT["r"][hh], QT, [K.D("qT", j) for j in range(len(xblocks))], [QT])

        m1 = K.mark()
        DG = K.sb([128, 4, CW, 128], BF16, "dg")
        for cc in range(4):
            for j in range(CW):
                K.ts("dve", DG.ap[:, cc, j, :], ident, pc("convw", (l * 4 + cc) * CW + j), ALU.mult, [CB, PRM], [DG])
        ZC = [K.sb([128, 4, 4, 158], BF16, "zc") for _ in range(2)]
        CA = [K.sb([128, 4, 4, HALO], BF16, "ca") for _ in range(2)]
        CBB = [K.sb([128, 4, 4, HALO], BF16, "cbb") for _ in range(2)]
        CT = K.sb([128, 4, 4, HALO], F32, "ct")
        Y32b = [K.sb([128, 4, 512], F32, "y32") for _ in range(2)]
        YBF = K.sb([128, 4, 512], BF16, "ybf")
        YSQ = K.sb([128, 4, 512], BF16, "ysq")
        MEAN = K.sb([128, 512], F32, "mean")
        MSQ = K.sb([128, 512], F32, "msq")
        RSD = K.sb([128, 512], F32, "rsd")
        TT = [K.sb([128, 512], F32, "tt") for _ in range(2)]
        CST = [K.sb([128, 4, 512], BF16, "cst") for _ in range(2)]
        def loadB1(bi):
            c0, n = xblocks[bi]
            meta = (n == NMETA)
            nsl, wd = (1, NMETA) if meta else (4, 128)
            zc = ZC[bi % 2]
            for cc in range(4):
                K.dma("sp", zc.ap[:, cc, :nsl, HALO:HALO + wd],
                      zT["r"][cc, :, c0:c0 + n].rearrange("p (s t) -> p s t", t=wd), zc, [K.D("zT", bi)], [zc])
            if not meta:
                ca, cbb = CA[bi % 2], CBB[bi % 2]
                s0 = c0 // 128
                for cc in range(4):
                    K.dma("sp", ca.ap[:, cc, :, :], zviews[0][cc, :, s0 + 1:s0 + 5, :], ca, [K.D("xall", l)], [ca])
                    K.dma("sp", cbb.ap[:, cc, :, :], zviews[1][cc, :, s0:s0 + 4, :], cbb, [K.D("xall", l)], [cbb])

        def frontB1(bi):
            c0, n = xblocks[bi]
            meta = (n == NMETA)
            nsl, wd = (1, NMETA) if meta else (4, 128)
            zc = ZC[bi % 2]
            Y32 = Y32b[bi % 2]
            if meta:
                K.memset("pool", zc.ap[:, :, 0, 0:HALO], 0.0, [zc])
            else:
                ca, cbb = CA[bi % 2], CBB[bi % 2]
                K.ts("dve", CT.ap, ca.ap, pc("sel", 0), ALU.mult, [ca, PRM], [CT])
                K.stt(zc.ap[:, :, :, 0:HALO], cbb.ap, pc("sel", 1), CT.ap, ALU.mult, ALU.add, [cbb, CT, PRM], [zc])
            for cc in range(4):
                bk = K.bank()
                for j in range(CW):
                    K.mm(bk.ap[:, :n].rearrange("p (s t) -> p s t", t=wd), DG.ap[:, cc, j, :],
                         zc.ap[:, cc, :nsl, j:j + wd], j == 0, j == CW - 1, [DG, zc], [bk])
                K.act(Y32.ap[:, cc, :n], bk.ap[:, :n], AF.Identity, [bk, PRM], [Y32], bias=pc("convb", l * 4 + cc))

        def backB1(bi):
            c0, n = xblocks[bi]
            Y32 = Y32b[bi % 2]
            K.copy("pool", YBF.ap[:, :, :n], Y32.ap[:, :, :n], [Y32], [YBF])
            K.act(YSQ.ap[:, :, :n], Y32.ap[:, :, :n], AF.Square, [Y32], [YSQ])
            bs = K.bank()
            for cc in range(4):
                K.mm(bs.ap[:, :n], ones, YBF.ap[:, cc, :n], cc == 0, cc == 3, [YBF, CB], [bs])
            bq = K.bank()
            for cc in range(4):
                K.mm(bq.ap[:, :n], ones, YSQ.ap[:, cc, :n], cc == 0, cc == 3, [YSQ, CB], [bq])
            K.ts("dve", MEAN.ap[:, :n], bs.ap[:, :n], 1.0 / 512, ALU.mult, [bs], [MEAN])
            K.tt("pool", MSQ.ap[:, :n], MEAN.ap[:, :n], MEAN.ap[:, :n], ALU.mult, [MEAN], [MSQ])
            K.stt(RSD.ap[:, :n], bq.ap[:, :n], 1.0 / 512, MSQ.ap[:, :n], ALU.mult, ALU.subtract, [bq, MSQ], [RSD])
            K.act(RSD.ap[:, :n], RSD.ap[:, :n], AF.Sqrt, [RSD], [RSD], bias=EPS, scale=1.0)
            K.recip(RSD.ap[:, :n], RSD.ap[:, :n], [RSD], [RSD])
            cst = CST[bi % 2]
            for cc in range(4):
                t = TT[cc % 2]
                K.tt("dve", t.ap[:, :n], Y32.ap[:, cc, :n], MEAN.ap[:, :n], ALU.subtract, [Y32, MEAN], [t])
                K.tt("pool", t.ap[:, :n], t.ap[:, :n], RSD.ap[:, :n], ALU.mult, [t, RSD], [t])
                K.act(cst.ap[:, cc, :n], t.ap[:, :n], AF.Silu, [t, PRM], [cst], bias=pc("lnb", l * 4 + cc),
                      scale=pc("lng", l * 4 + cc))
            K.dma("sp", mixc_d[:, :, c0:c0 + n].rearrange("c p t -> p c t"), cst.ap[:, :, :n], cst, [cst],
                  [K.D("mixc", bi)])

        nblk = len(xblocks)
        loadB1(0)
        if nblk > 1:
            loadB1(1)
        loadH(0)
        frontB1(0)
        for bi in range(nblk):
            if bi + 2 < nblk:
                loadB1(bi + 2)
            if bi + 1 < nblk:
                frontB1(bi + 1)
            backB1(bi)
        K.barrier()
        K.release(m1)

        KTb[1] = K.sb([128, NKEY], BF16, "kt")
        Vb[1] = K.sb([128, NKT, 129], BF16, "v")
        QTb[1] = K.sb([128, TL], BF16, "qt")
        K.memset("pool", Vb[1].ap[:, :, 128:129], 1.0, [Vb[1]])
        Pb = [K.sb([128, 2, 512], BF16, "p") for _ in range(4)]
        OA = [K.sb([128, 4, 2, 129], F32, "oa") for _ in range(2)]
        RZ = K.sb([128, 4, 2], F32, "rz")
        S1 = K.sb([128, 4], F32, "s1")
        O0 = [K.sb([128, 128], F32, "o0") for _ in range(2)]
        OO = [K.sb([128, 4, 128], F32, "oo") for _ in range(2)]
        JK = K.sb([128, 128], F32, "jk")
        SS = K.sb([128, 4], F32, "ss")
        RSTD = K.sb([128, 4], F32, "rstd")
        AOS = [K.sb([128, 4, 128], BF16, "aos") for _ in range(2)]
        abank = banks[4:8]
        sc_ = 1.0 / (128.0 * (1.0 - lam_init) ** 2)
        bb_ = EPS / ((1.0 - lam_init) ** 2)
        it = 0
        pidx = 0
        for hh in range(4):
            KT, V, QT = KTb[hh % 2], Vb[hh % 2], QTb[hh % 2]
            if hh + 1 < 4:
                loadH(hh + 1)
            qblocks = [(m, 4, 128) for m in range(NB)] + [(NB, 1, NMETA)]
            for (m, nsl, wq) in qblocks:
                meta = (wq == NMETA)
                if meta:
                    ktiles = [(0, NMETA, 0, "M", 0)]
                    qbase = NX
                else:
                    ktiles = [(0, NMETA, 0, None, 0)]
                    for g in range(8 * m + 8):
                        r = g - 8 * m
                        if r < 0:
                            ktiles.append((1 + g, 128, 0, None, 0))
                        else:
                            ktiles.append((1 + g, 128, r // 2, "E" if r % 2 == 0 else "O", r // 2))
                    qbase = 4 * m * 128
                oa = OA[it % 2]
                nkt = len(ktiles)
                pps = {}

                def qk_stage(idx):
                    nonlocal pidx
                    (kt, nk, i0, mk, im) = ktiles[idx]
                    ncols = (nsl - i0) * wq
                    q0 = qbase + i0 * wq
                    k0 = 0 if kt == 0 else NMETA + (kt - 1) * 128
                    bp = bankpair[pidx % 2]
                    pp = Pb[pidx % len(Pb)]
                    pidx += 1
                    pps[idx] = pp
                    for c in range(2):
                        K.mm(bp.ap[:nk, c, :ncols], KT.ap[c * 64:(c + 1) * 64, k0:k0 + nk],
                             QT.ap[c * 64:(c + 1) * 64, q0:q0 + ncols], True, True, [KT, QT], [bp])
                    K.act(pp.ap[:nk, :, :ncols], bp.ap[:nk, :, :ncols], AF.Exp, [bp], [pp], scale=0.125)
                    if mk is not None:
                        mka = {"E": maskE, "O": maskO, "M": maskM}[mk]
                        for c in range(2):
                            a = pp.ap[:nk, c, (im - i0) * wq:(im - i0 + 1) * wq]
                            K.tt("dve", a, a, mka[:nk, :wq], ALU.mult, [pp, CB], [pp])

                def pv_stage(idx):
                    (kt, nk, i0, mk, im) = ktiles[idx]
                    pp = pps.pop(idx)
                    last = idx == nkt - 1
                    for i in range(i0, nsl):
                        for c in range(2):
                            K.mm(abank[i].ap[:wq, c * 256:c * 256 + 129], pp.ap[:nk, c, (i - i0) * wq:(i - i0 + 1) * wq],
                                 V.ap[:nk, kt, :], idx == 0 and c == 0, last, [pp, V], [abank[i]],
                                 skip_group_check=True)

                for idx in range(nkt + 2):
                    if idx < nkt:
                        qk_stage(idx)
                    if idx >= 2:
                        pv_stage(idx - 2)
                for i in range(nsl):
                    K.copy("dve", oa.ap[:wq, i, :, :],
                           abank[i].ap[:wq, :].rearrange("p (c x) -> p c x", x=256)[:, :, 0:129], [abank[i]], [oa])
                K.recip(RZ.ap[:wq, :nsl, :], oa.ap[:wq, :nsl, :, 128], [oa], [RZ])
                K.ts("pool", S1.ap[:wq, :nsl], RZ.ap[:wq, :nsl, 1], NEGLAM.ap[:wq, 0:1], ALU.mult, [RZ, NEGLAM], [S1])
                aos = AOS[it % 2]
                oo = OO[it % 2]
                it += 1
                for i in range(nsl):
                    o0 = O0[i % 2]
                    K.ts("pool", o0.ap[:wq, :], oa.ap[:wq, i, 0, 0:128], RZ.ap[:wq, i, 0:1], ALU.mult, [oa, RZ], [o0])
                    K.stt(oo.ap[:wq, i, :], oa.ap[:wq, i, 1, 0:128], S1.ap[:wq, i:i + 1], o0.ap[:wq, :], ALU.mult,
                          ALU.add, [oa, S1, o0], [oo])
                    K.stt(JK.ap[:wq, :], oo.ap[:wq, i, :], 1.0, oo.ap[:wq, i, :], ALU.mult, ALU.mult, [oo], [JK, SS],
                          accum_out=SS.ap[:wq, i:i + 1])
                K.act(RSTD.ap[:wq, :nsl], SS.ap[:wq, :nsl], AF.Ln, [SS], [RSTD], bias=bb_, scale=sc_)
                K.act(RSTD.ap[:wq, :nsl], RSTD.ap[:wq, :nsl], AF.Exp, [RSTD], [RSTD], scale=-0.5)
                for i in range(nsl):
                    K.stt(aos.ap[:wq, i, :], oo.ap[:wq, i, :], RSTD.ap[:wq, i:i + 1],
                          PRM.ap[:wq, PL["subg"] + l * 128:PL["subg"] + (l + 1) * 128], ALU.mult, ALU.mult,
                          [oo, RSTD, PRM], [aos])
                K.dma("sp", ao_d[qbase:qbase + nsl * wq, hh * 128:(hh + 1) * 128].rearrange("(s q) e -> q s e", q=wq),
                      aos.ap[:wq, :nsl, :], aos, [aos], [K.D("ao", m)])
        K.barrier()
        K.release(m0)

    def phaseC(l, h_src, final):
        m0 = K.mark()
        WO = K.sb([128, 8, D], BF16, "wo")
        for kc in range(8):
            K.dma("pool", WO.ap[:, kc, :], w_out[l, kc * 128:(kc + 1) * 128, :], WO, [], [WO])
        HB = [K.sb([128, 8, 512], F32, "hb") for _ in range(3)]
        AOB = [K.sb([128, 4, 512], BF16, "aob") for _ in range(2)]
        MIX = [K.sb([128, 8, 512], BF16, "mix") for _ in range(2)]
        SQ = K.sb([128, 8, 512], BF16, "sq")
        RS = K.sb([128, 512], F32, "rs")
        HN2 = [K.sb([128, 8, 512], BF16, "hn2") for _ in range(2)]
        def loadC1(bi):
            c0, n = xblocks[bi]
            meta = (n == NMETA)
            nsl, wd = (1, NMETA) if meta else (4, 128)
            hb, aob, mix = HB[bi % 3], AOB[bi % 2], MIX[bi % 2]
            K.dma("sp", hb.ap[:, :, :n], h_src[:, :, c0:c0 + n].rearrange("c p t -> p c t"), hb, [K.D("hT", bi)], [hb])
            K.dma("sp", aob.ap[:wd, :nsl, :], ao_d[c0:c0 + n, :].rearrange("(s q) e -> q s e", q=wd), aob,
                  [K.D("ao", bi)], [aob])
            K.dma("sp", mix.ap[:, 4:8, :n], mixc_d[:, :, c0:c0 + n].rearrange("c p t -> p c t"), mix,
                  [K.D("mixc", bi)], [mix])

        def frontC1(bi):
            c0, n = xblocks[bi]
            meta = (n == NMETA)
            nsl, wd = (1, NMETA) if meta else (4, 128)
            hb, aob, mix = HB[bi % 3], AOB[bi % 2], MIX[bi % 2]
            for hh in range(4):
                bt = K.bank()
                btb = bt.ap.bitcast(BF16)
                for i in range(nsl):
                    K.transpose(btb[:, i * wd:(i + 1) * wd], aob.ap[:wd, i, hh * 128:(hh + 1) * 128], ident[:wd, :wd],
                                [aob, CB], [bt])
                K.copy("act" if hh % 2 == 0 else "dve", mix.ap[:, hh, :n], btb[:, :n], [bt], [mix])
            for oc in range(8):
                by = K.bank()
                for kc in range(8):
                    K.mm(by.ap[:, :n], WO.ap[:, kc, oc * 128:(oc + 1) * 128], mix.ap[:, kc, :n], kc == 0, kc == 7,
                         [WO, mix], [by])
                K.tt("dve", hb.ap[:, oc, :n], hb.ap[:, oc, :n], by.ap[:, :n], ALU.add, [hb, by], [hb])
            K.dma("sp", hmid_d[:, :, c0:c0 + n].rearrange("c p t -> p c t"), hb.ap[:, :, :n], hb, [hb],
                  [K.D("hmid", bi)])

        def backC1(bi):
            c0, n = xblocks[bi]
            hb, hn2 = HB[bi % 3], HN2[bi % 2]
            rmsnorm_block(hb, n, "g2", l * 8, SQ, RS, hn2)
            K.dma("sp", hn2_d[:, :, c0:c0 + n].rearrange("c p t -> p c t"), hn2.ap[:, :, :n], hn2, [hn2],
                  [K.D("hn2", bi)])

        nblk = len(xblocks)
        loadC1(0)
        if nblk > 1:
            loadC1(1)
        frontC1(0)
        for bi in range(nblk):
            if bi + 2 < nblk:
                loadC1(bi + 2)
            if bi + 1 < nblk:
                frontC1(bi + 1)
            backC1(bi)
        K.barrier()
        K.release(m0)
        NG = (FC + 3) // 4
        WGg = [None] * NG
        WUg = [None] * NG
        for g in range(NG):
            nf = min(4, FC - 4 * g)
            WGg[g] = K.sb([128, 8, nf * 128], BF16, f"wg{g}")
            WUg[g] = K.sb([128, 8, nf * 128], BF16, f"wu{g}")
            K.dma("pool", WGg[g].ap, w_gu[l, :, g * 512:g * 512 + nf * 128].rearrange("(kc p) c -> p kc c", p=128),
                  WGg[g], [], [WGg[g]])
            K.dma("pool", WUg[g].ap,
                  w_gu[l, :, DFF + g * 512:DFF + g * 512 + nf * 128].rearrange("(kc p) c -> p kc c", p=128),
                  WUg[g], [], [WUg[g]])
        WDh = [K.sb([128, FC // 2, D], BF16, f"wd{i}") for i in range(2)]
        for i in range(2):
            K.dma("pool", WDh[i].ap,
                  w_dn[l, i * (FC // 2) * 128:(i + 1) * (FC // 2) * 128, :].rearrange("(f p) c -> p f c", p=128),
                  WDh[i], [], [WDh[i]])
        HN = [K.sb([128, 8, 256], BF16, "hn") for _ in range(2)]
        HB2 = [K.sb([128, 8, 256], F32, "hb2") for _ in range(2)]
        ACT = K.sb([128, FC, 256], BF16, "act")
        SG = [K.sb([128, 256], F32, "sg") for _ in range(2)]
        SQ2 = K.sb([128, 8, 256], BF16, "sq2")
        RS2 = K.sb([128, 256], F32, "rs2")
        fblocks = [(j * 256, 256) for j in range(NX // 256)] + [(NX, NMETA)]
        def loadC2(bi):
            c0, n = fblocks[bi]
            sbi = NB if n == NMETA else c0 // 512
            hn, hb = HN[bi % 2], HB2[bi % 2]
            K.dma("sp", hn.ap[:, :, :n], hn2_d[:, :, c0:c0 + n].rearrange("c p t -> p c t"), hn, [K.D("hn2", sbi)], [hn])
            K.dma("sp", hb.ap[:, :, :n], hmid_d[:, :, c0:c0 + n].rearrange("c p t -> p c t"), hb, [K.D("hmid", sbi)],
                  [hb])

        loadC2(0)
        for bi, (c0, n) in enumerate(fblocks):
            meta = (n == NMETA)
            sbi = NB if meta else c0 // 512
            hn, hb = HN[bi % 2], HB2[bi % 2]
            if bi + 1 < len(fblocks):
                loadC2(bi + 1)
            for f in range(FC):
                bg = K.bank()
                for kc in range(8):
                    K.mm(bg.ap[:, :n], WGg[f // 4].ap[:, kc, (f % 4) * 128:(f % 4 + 1) * 128], hn.ap[:, kc, :n],
                         kc == 0, kc == 7, [WGg[f // 4], hn], [bg])
                bu = K.bank()
                for kc in range(8):
                    K.mm(bu.ap[:, :n], WUg[f // 4].ap[:, kc, (f % 4) * 128:(f % 4 + 1) * 128], hn.ap[:, kc, :n],
                         kc == 0, kc == 7, [WUg[f // 4], hn], [bu])
                sg = SG[f % 2]
                K.act(sg.ap[:, :n], bg.ap[:, :n], AF.Silu, [bg], [sg])
                K.tt("dve", ACT.ap[:, f, :n], sg.ap[:, :n], bu.ap[:, :n], ALU.mult, [sg, bu], [ACT])
            for oc in range(8):
                bd = K.bank()
                for f in range(FC):
                    wdb = WDh[f // (FC // 2)]
                    K.mm(bd.ap[:, :n], wdb.ap[:, f % (FC // 2), oc * 128:(oc + 1) * 128], ACT.ap[:, f, :n], f == 0,
                         f == FC - 1, [wdb, ACT], [bd])
                K.tt("dve", hb.ap[:, oc, :n], hb.ap[:, oc, :n], bd.ap[:, :n], ALU.add, [hb, bd], [hb])
            if final:
                if not meta:
                    rmsnorm_block(hb, n, "gf", 0, SQ2, RS2, None, out_dtype_f32_inplace=True)
                    K.dma("sp", outT[:, :, c0:c0 + n].rearrange("c p t -> p c t"), hb.ap[:, :, :n], hb, [hb],
                          [K.D("outT", bi)])
            else:
                K.dma("sp", hT["w"][:, :, c0:c0 + n].rearrange("c p t -> p c t"), hb.ap[:, :, :n], hb, [hb],
                      [K.D("hT", sbi)])
        K.barrier()
        K.release(m0)

    h_written_here = False
    for (ph, l) in stages:
        if ph == "A":
            phaseA(l, xT if l == 0 else (hT["w"] if h_written_here else hT["r"]))
            if fused:
                exchange(l)
        elif ph == "B":
            phaseB(l)
        elif ph == "C":
            hsrc = xT if l == 0 else hT["r"]
            phaseC(l, hsrc, final=(l == DEPTH - 1))
            h_written_here = True
    K.finish()
    K.emit(stack)
    stack.close()
    return nc, ext_in, ext_out


def _bf(a):
    return np.asarray(a, dtype=np.float32).astype(ml_dtypes.bfloat16)


def host_prepare(inputs, NS, DEPTH, B):
    NX = NS * 128
    TL = NX + NMETA
    PL = prm_layout(DEPTH)
    f32 = np.float32
    x = np.asarray(inputs["x"], f32)
    meta = np.asarray(inputs["meta_tokens"], f32)
    w_in = np.asarray(inputs["w_in"], f32)
    perm = np.arange(512).reshape(4, 2, 64)
    perm = np.concatenate([perm[:, :, 32:], perm[:, :, :32]], axis=-1).reshape(512)
    w_in_ext = np.ascontiguousarray(np.concatenate([w_in, w_in[:, :, perm], w_in[:, :, 512 + perm]], axis=-1))
    w_out = np.ascontiguousarray(np.asarray(inputs["w_out"], f32))
    w_gu = np.ascontiguousarray(np.asarray(inputs["w_gate_up"], f32))
    w_dn = np.ascontiguousarray(np.asarray(inputs["w_down"], f32))

    def colmajor(v, nch):
        v = np.asarray(v, f32)
        lead = v.shape[:-1]
        v = v.reshape(*lead, nch, 128)
        v = np.moveaxis(v, -1, 0)
        return v.reshape(128, -1)

    def rep(v):
        v = np.asarray(v, f32).reshape(1, -1)
        return np.broadcast_to(v, (128, v.shape[1]))

    prm_base = np.zeros((128, PL["_n"]), f32)
    prm_base[:, PL["g1"]:PL["g1"] + DEPTH * 8] = colmajor(inputs["norm1_g"], 8)
    prm_base[:, PL["g2"]:PL["g2"] + DEPTH * 8] = colmajor(inputs["norm2_g"], 8)
    prm_base[:, PL["gf"]:PL["gf"] + 8] = colmajor(inputs["final_g"], 8)
    prm_base[:, PL["bglu"]:PL["bglu"] + DEPTH * 8] = colmajor(inputs["b_glu"], 8)
    cw = np.asarray(inputs["conv_w"], f32)
    cw = cw.reshape(DEPTH, CW, 4, 128).transpose(3, 0, 2, 1).reshape(128, -1)
    prm_base[:, PL["convw"]:PL["convw"] + DEPTH * 4 * CW] = cw
    prm_base[:, PL["convb"]:PL["convb"] + DEPTH * 4] = colmajor(inputs["conv_b"], 4)
    prm_base[:, PL["lng"]:PL["lng"] + DEPTH * 4] = colmajor(inputs["conv_ln_g"], 4)
    prm_base[:, PL["lnb"]:PL["lnb"] + DEPTH * 4] = colmajor(inputs["conv_ln_b"], 4)
    for nm, key in (("lq1", "lam_q1"), ("lk1", "lam_k1"), ("lq2", "lam_q2"), ("lk2", "lam_k2")):
        prm_base[:, PL[nm]:PL[nm] + DEPTH * 64] = rep(inputs[key])
    prm_base[:, PL["subg"]:PL["subg"] + DEPTH * 128] = rep(inputs["subln_g"])

    inv = (10000.0 ** (-np.arange(0, 64, 2, dtype=f32) / 64.0)).astype(f32)
    tri = (np.arange(128)[:, None] <= np.arange(128)[None, :]).astype(f32)
    in_maps = []
    for b in range(B):
        for p in range(2):
            tiles = [x[b, 128 * (2 * s + p):128 * (2 * s + p) + 128, :] for s in range(NS)]
            hx = np.concatenate(tiles + [meta], axis=0)
            xTc = np.ascontiguousarray(hx.T).reshape(8, 128, TL)
            pos = np.concatenate([NMETA + 128 * (2 * s + p) + np.arange(128) for s in range(NS)] + [np.arange(NMETA)])
            ang = pos.astype(f32)[:, None] * inv[None, :]
            ang = np.concatenate([ang, ang], axis=-1).astype(f32)
            cosd = np.cos(ang).astype(f32).T
            sind = np.sin(ang).astype(f32).T
            sign = np.where(np.arange(64) < 32, -1.0, 1.0).astype(f32)[:, None]
            sind = sind * sign
            cosT = np.ascontiguousarray(np.concatenate([cosd, cosd], axis=0))
            sinT = np.ascontiguousarray(np.concatenate([sind, sind], axis=0))
            cb = np.zeros((128, CB_N), f32)
            cb[:, CB_IDENT:CB_IDENT + 128] = np.eye(128, dtype=f32)
            cb[:, CB_ONES:CB_ONES + 128] = 1.0
            cb[:, CB_ME:CB_ME + 128] = tri if p == 0 else 1.0
            cb[:, CB_MO:CB_MO + 128] = 0.0 if p == 0 else tri
            cb[:16, CB_MM:CB_MM + 16] = tri[:16, :16]
            prm = prm_base.copy()
            prm[:, PL["sel"]] = 1.0 if p == 1 else 0.0
            prm[:, PL["sel"] + 1] = 0.0 if p == 1 else 1.0
            in_maps.append({"xT": xTc, "prm": prm, "cb": _bf(cb), "cosT": cosT, "sinT": sinT, "w_in": w_in_ext,
                            "w_out": w_out, "w_gu": w_gu, "w_dn": w_dn})
    return in_maps


def assemble_output(outs, NS, B):
    NX = NS * 128
    S = 2 * NX
    out = np.empty((B, S, D), np.float32)
    for b in range(B):
        for p in range(2):
            o = np.asarray(outs[2 * b + p]).reshape(D, NX).T
            for s in range(NS):
                g = 2 * s + p
                out[b, 128 * g:128 * g + 128, :] = o[128 * s:128 * s + 128, :]
    return out


_PROG_CACHE = {}


def run_model(inputs, NS, DEPTH, B, fused=True):
    n_cores = 2 * B
    in_maps = host_prepare(inputs, NS, DEPTH, B)
    if fused:
        stages = []
        for l in range(DEPTH):
            stages += [("A", l), ("B", l), ("C", l)]
        key = ("f", NS, DEPTH, B)
        if key not in _PROG_CACHE:
            _PROG_CACHE[key] = build_program(NS, DEPTH, stages, True, B)
        nc, ext_in, ext_out = _PROG_CACHE[key]
        res = run_bass_kernel_spmd(nc, [{k: m[k] for k in ext_in} for m in in_maps], core_ids=list(range(n_cores)))
        return assemble_output([r["outT"] for r in res.results], NS, B)
    groups = [[("A", 0)]]
    for l in range(DEPTH):
        g = [("B", l), ("C", l)]
        if l + 1 < DEPTH:
            g.append(("A", l + 1))
        groups.append(g)
    state = [dict() for _ in range(n_cores)]
    outs = None
    for gi, g in enumerate(groups):
        nc, ext_in, ext_out = build_program(NS, DEPTH, g, False, B)
        maps = []
        for c in range(n_cores):
            m = {}
            for k in ext_in:
                if k in in_maps[c]:
                    m[k] = in_maps[c][k]
                elif k.endswith("_in"):
                    base = k[:-3]
                    if base in state[c]:
                        m[k] = state[c][base]
                    else:
                        shp, dt = _shape_of(nc, k)
                        m[k] = np.zeros(shp, dt)
                else:
                    raise KeyError(k)
            maps.append(m)
        res = run_bass_kernel_spmd(nc, maps, core_ids=list(range(n_cores)))
        for c in range(n_cores):
            r = res.results[c]
            for k in ext_out:
                if k.endswith("_out"):
                    state[c][k[:-4]] = np.asarray(r[k])
        for b in range(B):
            for kname in list(state[2 * b].keys()):
                if kname.startswith("xo_"):
                    xa = np.concatenate([state[2 * b][kname], state[2 * b + 1][kname]], axis=0)
                    state[2 * b]["xa_" + kname[3:]] = xa
                    state[2 * b + 1]["xa_" + kname[3:]] = xa
        outs = [np.asarray(r["outT"]) for r in res.results]
    return assemble_output(outs, NS, B)


def _shape_of(nc, name):
    for alloc in nc.allocations:
        if isinstance(alloc, mybir.MemoryLocationSet) and alloc.memorylocations and alloc.memorylocations[0].name == name:
            return tuple(alloc.tensor_shape), mybir.dt.np(alloc.dtype)
    raise KeyError(name)


def kernel(**inputs):
    return run_model(inputs, NS=32, DEPTH=4, B=4, fused=True)
```

```python
import math
import contextlib
import numpy as np
import ml_dtypes
import concourse.bass as bass
import concourse.mybir as mybir
from concourse.bass_utils import run_bass_kernel_spmd

F32 = mybir.dt.float32
BF16 = mybir.dt.bfloat16
AF = mybir.ActivationFunctionType
ALU = mybir.AluOpType

D = 1024
KC = 8
DFF = 2816
FC = 22
NMETA = 16
CW = 31
HALO = 30
EPS = 1e-5
WIN_COLS = 3584
PIECE_ROWS = 2048


class Ins:
    __slots__ = ("eng", "fn", "deps", "signal", "val", "key", "slot", "inc")


class Slot:
    def __init__(self):
        self.count = 0
        self.sem = None


class Buf:
    def __init__(self, ap=None, name=""):
        self.ap = ap
        self.name = name
        self.w = {}
        self.r = {}
        self.slot = None


class Kern:
    def __init__(self, nc, arena, arena_cols, banks):
        self.nc = nc
        self.engs = {n: [] for n in ("pe", "act", "dve", "pool", "sp")}
        self.pending = {n: [] for n in self.engs}
        self.last = {}
        self.arena = arena
        self.arena_cols = arena_cols
        self.off = 0
        self.banks = banks
        self.bank_i = 0
        self.slots = []
        self.free_slots = {}
        self.live = []
        self.dram_bufs = {}

    def sb(self, shape, dtype, name=""):
        esz = 4 if dtype == F32 else 2
        n = 1
        for x in shape[1:]:
            n *= x
        nbytes = (n * esz + 63) // 64 * 64
        o = self.off
        self.off += nbytes
        assert self.off <= self.arena_cols * 2, f"SBUF arena overflow at {name}: {self.off}"
        ap = self.arena[:, o // 2:(o + n * esz) // 2]
        if dtype == F32:
            ap = ap.bitcast(F32)
        if len(shape) == 3:
            ap = ap.rearrange("p (a b) -> p a b", b=shape[2])
        elif len(shape) == 4:
            ap = ap.rearrange("p (a b c) -> p a b c", b=shape[2], c=shape[3])
        b = Buf(ap, name)
        self.live.append((o, b))
        return b

    def mark(self):
        return self.off

    def release(self, m):
        keep = []
        for (o, b) in self.live:
            if o >= m:
                if b.slot is not None:
                    self.free_slots.setdefault(b.slot.qk, []).append(b.slot)
                    b.slot = None
            else:
                keep.append((o, b))
        self.live = keep
        self.off = m

    def bank(self):
        b = self.banks[self.bank_i % len(self.banks)]
        self.bank_i += 1
        return b

    def D(self, name, j=0):
        k = (name, j)
        if k not in self.dram_bufs:
            self.dram_bufs[k] = Buf(None, f"{name}:{j}")
        return self.dram_bufs[k]

    def _add(self, eng, fn, reads, writes, dbuf=None, inc=16):
        ins = Ins()
        ins.eng = eng
        ins.fn = fn
        ins.signal = False
        ins.val = None
        ins.slot = None
        ins.inc = 1
        if dbuf is not None:
            if dbuf.slot is None:
                qk = (eng, inc)
                fl = self.free_slots.setdefault(qk, [])
                if fl:
                    dbuf.slot = fl.pop()
                else:
                    dbuf.slot = Slot()
                    dbuf.slot.qk = qk
                    self.slots.append(dbuf.slot)
            assert dbuf.slot.qk == (eng, inc), f"buffer {dbuf.name} used from two DMA queues"
            sl = dbuf.slot
            ins.slot = sl
            ins.inc = inc
            ins.key = ("d", id(sl))
            sl.count += inc
            ins.val = sl.count
            ins.signal = True
        else:
            ins.key = eng
        deps = {}
        for b in reads:
            for d in b.w.values():
                deps[id(d)] = d
        for b in writes:
            for d in b.w.values():
                deps[id(d)] = d
            for d in b.r.values():
                deps[id(d)] = d
        for d in self.pending[eng]:
            deps[id(d)] = d
        self.pending[eng] = []
        ins.deps = [d for d in deps.values() if not (d.key == "pe" and eng == "pe")]
        for d in ins.deps:
            d.signal = True
        for b in writes:
            b.w[ins.key] = ins
            b.r = {}
        for b in reads:
            b.r[ins.key] = ins
        self.engs[eng].append(ins)
        self.last[ins.key] = ins
        return ins

    def barrier(self):
        lst = list(self.last.values())
        for n in self.engs:
            self.pending[n] = list(lst)

    def finish(self):
        self.barrier()
        for n in self.engs:
            self._add(n, None, [], [])

    def mm(self, out, lhsT, rhs, start, stop, reads, writes, **kw):
        return self._add("pe", lambda e: e.matmul(out, lhsT, rhs, start=start, stop=stop, **kw), reads, writes)

    def transpose(self, out, in_, ident, reads, writes):
        return self._add("pe", lambda e: e.transpose(out, in_, ident), reads, writes)

    def act(self, out, in_, func, reads, writes, bias=None, scale=None):
        kw = {}
        if bias is not None:
            kw["bias"] = bias
        if scale is not None:
            kw["scale"] = scale
        return self._add("act", lambda e: e.activation(out, in_, func, **kw), reads, writes)

    def tt(self, eng, out, in0, in1, op, reads, writes):
        return self._add(eng, lambda e: e.tensor_tensor(out, in0, in1, op), reads, writes)

    def ts(self, eng, out, in0, s1, op0, reads, writes, s2=None, op1=None):
        if op1 is None:
            return self._add(eng, lambda e: e.tensor_scalar(out, in0, s1, None, op0), reads, writes)
        return self._add(eng, lambda e: e.tensor_scalar(out, in0, s1, s2, op0, op1), reads, writes)

    def stt(self, out, in0, scalar, in1, op0, op1, reads, writes, accum_out=None):
        return self._add("dve", lambda e: e.scalar_tensor_tensor(out, in0, scalar, in1, op0, op1, accum_out=accum_out),
                         reads, writes)

    def copy(self, eng, out, in_, reads, writes):
        if eng == "act":
            return self._add("act", lambda e: e.activation(out, in_, AF.Copy), reads, writes)
        return self._add(eng, lambda e: e.tensor_copy(out, in_), reads, writes)

    def recip(self, out, in_, reads, writes):
        return self._add("dve", lambda e: e.reciprocal(out, in_), reads, writes)

    def memset(self, eng, ap, val, writes):
        return self._add(eng, lambda e: e.memset(ap, val), [], writes)

    def dma(self, q, out, in_, sbuf, reads, writes):
        return self._add(q, lambda e: e.dma_start(out=out, in_=in_), reads, writes, dbuf=sbuf)

    def emit(self, stack):
        nc = self.nc
        esem = {n: stack.enter_context(nc.semaphore("es_" + n)) for n in self.engs}
        for i, sl in enumerate(self.slots):
            sl.sem = stack.enter_context(nc.semaphore(f"ds{i}"))
        for n, lst in self.engs.items():
            c = 0
            for ins in lst:
                if ins.slot is None and ins.signal:
                    c += 1
                    ins.val = c

        def sem_of(ins):
            return ins.slot.sem if ins.slot is not None else esem[ins.eng]

        def body_for(name):
            def body(e):
                waited = {}
                for ins in self.engs[name]:
                    for d in sorted(ins.deps, key=lambda d: d.val):
                        sm = sem_of(d)
                        k = id(sm)
                        if waited.get(k, 0) >= d.val:
                            continue
                        e.wait_ge(sm, d.val)
                        waited[k] = d.val
                    if ins.fn is None:
                        continue
                    bi = ins.fn(e)
                    if ins.signal:
                        bi.then_inc(sem_of(ins), ins.inc)
            return body

        with nc.Block() as block:
            block.tensor(body_for("pe"))
            block.scalar(body_for("act"))
            block.vector(body_for("dve"))
            block.gpsimd(body_for("pool"))
            block.sync(body_for("sp"))


def prm_layout(depth):
    o = {}
    c = 0
    for name, n in (("g1", depth * 8), ("g2", depth * 8), ("gf", 8), ("bglu", depth * 8), ("convw", depth * 4 * CW),
                    ("convb", depth * 4), ("lng", depth * 4), ("lnb", depth * 4), ("sel", 2),
                    ("lq1", depth * 64), ("lk1", depth * 64), ("lq2", depth * 64), ("lk2", depth * 64),
                    ("subg", depth * 128)):
        o[name] = c
        c += n
    o["_n"] = c
    return o


CB_IDENT, CB_ONES, CB_ME, CB_MO, CB_MM, CB_N = 0, 128, 256, 384, 512, 528


def build_program(NS, DEPTH, stages, fused, n_pairs):
    NX = NS * 128
    TL = NX + NMETA
    NB = NS // 4
    NKT = 1 + 2 * NS
    NKEY = NMETA + 2 * NX
    OV = 4 * 128 * NX
    OZ = OV + NX * 512
    NXCH = OZ + 4 * 128 * (NS + 1) * HALO
    XR = NXCH // 512
    PL = prm_layout(DEPTH)

    nc = bass.Bass("TRN2", target_bir_lowering=False)
    stack = contextlib.ExitStack()

    phases = set(stages)
    first_stage = stages[0]
    last_stage = stages[-1]

    produced = {}
    ext_in = []
    ext_out = []

    def dram(name, shape, dtype, role):
        kind = {"in": "ExternalInput", "out": "ExternalOutput", "tmp": "Internal"}[role]
        t = nc.dram_tensor(name, list(shape), dtype, kind=kind)
        if role == "in":
            ext_in.append(name)
        if role == "out":
            ext_out.append(name)
        return t.ap()

    xT = dram("xT", [8, 128, TL], F32, "in")
    prm_d = dram("prm", [128, PL["_n"]], F32, "in")
    cb_d = dram("cb", [128, CB_N], BF16, "in")
    cos_d = dram("cosT", [128, TL], F32, "in")
    sin_d = dram("sinT", [128, TL], F32, "in")
    w_in = dram("w_in", [DEPTH, D, WIN_COLS], F32, "in")
    w_out = dram("w_out", [DEPTH, D, D], F32, "in")
    w_gu = dram("w_gu", [DEPTH, D, 2 * DFF], F32, "in")
    w_dn = dram("w_dn", [DEPTH, DFF, D], F32, "in")

    def handoff(name, shape, dtype, producer_phase_of, consumer_phases_of):
        if fused:
            ap = dram(name, shape, dtype, "tmp")
            return {"r": ap, "w": ap}
        res = {}
        res["r"] = dram(name + "_in", shape, dtype, "in")
        res["w"] = dram(name + "_out", shape, dtype, "out")
        return res

    hT = handoff("hT", [8, 128, TL], F32, None, None)
    qT = handoff("qT", [4, 128, TL], BF16, None, None)
    zT = handoff("zT", [4, 128, TL], BF16, None, None)
    kTm = handoff("kTm", [4, 128, NMETA], BF16, None, None)
    vm = handoff("vm", [NMETA, 512], BF16, None, None)
    PR = PIECE_ROWS
    HP = min(4, (PR * 4) // NX)
    TP = min(NX, PR)
    pieces = [(f"k{j}", HP * NX // 4) for j in range(4 // HP)] + [(f"v{j}", TP) for j in range(NX // TP)] + \
             [("z", 30 * (NS + 1))]
    xown_t = {}
    xall_t = {}
    for (pn, rows) in pieces:
        if fused:
            xown_t[pn] = [dram(f"xo_{pn}{i}", [rows, 512], BF16, "tmp") for i in range(2)]
            xall_t[pn] = [dram(f"xa_{pn}{i}", [2 * rows, 512], BF16, "tmp") for i in range(2)]
        else:
            o_ = dram(f"xo_{pn}_out", [rows, 512], BF16, "out")
            a_ = dram(f"xa_{pn}_in", [2 * rows, 512], BF16, "in")
            xown_t[pn] = [o_, o_]
            xall_t[pn] = [a_, a_]

    def _flat(t):
        return t.rearrange("r c -> (r c)")

    def k_own(l, j):
        return _flat(xown_t[f"k{j}"][l % 2]).rearrange("(h p t) -> h p t", h=HP, p=128)

    def k_all(l, r, h):
        j, hl = h // HP, h % HP
        sz = HP * 128 * NX
        return _flat(xall_t[f"k{j}"][l % 2])[r * sz:(r + 1) * sz].rearrange("(h p t) -> h p t", h=HP, p=128)[hl]

    def v_own(l, t0, n):
        j = t0 // TP
        return xown_t[f"v{j}"][l % 2][t0 - j * TP:t0 - j * TP + n, :]

    def v_all(l, r, t0, n):
        j = t0 // TP
        return xall_t[f"v{j}"][l % 2][r * TP + t0 - j * TP:r * TP + t0 - j * TP + n, :]

    def z_own(l):
        return _flat(xown_t["z"][l % 2]).rearrange("(c p s t) -> c p s t", c=4, p=128, s=NS + 1)

    def z_all(l, r):
        sz = 4 * 128 * (NS + 1) * HALO
        return _flat(xall_t["z"][l % 2])[r * sz:(r + 1) * sz].rearrange("(c p s t) -> c p s t", c=4, p=128, s=NS + 1)

    ao_d = dram("ao", [TL, 512], BF16, "tmp")
    mixc_d = dram("mixc", [4, 128, TL], BF16, "tmp")
    hmid_d = dram("hmid", [8, 128, TL], F32, "tmp")
    hn2_d = dram("hn2", [8, 128, TL], BF16, "tmp")
    outT = dram("outT", [8, 128, NX], F32, "out")

    ARENA_COLS = 94 * 1024
    arena = stack.enter_context(nc.sbuf_tensor("arena", [128, ARENA_COLS], BF16))
    psum_all = stack.enter_context(nc.psum_tensor("psall", [128, 8 * 512], F32))
    banks = [Buf(psum_all[:, i * 512:(i + 1) * 512], f"bank{i}") for i in range(8)]
    bankpair = [Buf(psum_all[:, 0:1024].rearrange("p (c x) -> p c x", x=512), "bp0"),
                Buf(psum_all[:, 1024:2048].rearrange("p (c x) -> p c x", x=512), "bp1")]
    K = Kern(nc, arena, ARENA_COLS, banks)

    PRM = K.sb([128, PL["_n"]], F32, "prm")
    CB = K.sb([128, CB_N], BF16, "cb")
    NEGLAM = K.sb([128, 4], F32, "neglam")
    K.dma("sp", PRM.ap, prm_d, PRM, [], [PRM])
    K.dma("sp", CB.ap, cb_d, CB, [], [CB])
    ident = CB.ap[:, CB_IDENT:CB_IDENT + 128]
    ones = CB.ap[:, CB_ONES:CB_ONES + 128]
    maskE = CB.ap[:, CB_ME:CB_ME + 128]
    maskO = CB.ap[:, CB_MO:CB_MO + 128]
    maskM = CB.ap[:, CB_MM:CB_MM + 16]

    def pc(name, idx):
        o = PL[name] + idx
        return PRM.ap[:, o:o + 1]

    xblocks = [(j * 512, 512) for j in range(NB)] + [(NX, NMETA)]

    def rmsnorm_block(hb, n, gname, gidx0, SQ, RS, HN, out_dtype_f32_inplace=False):
        K.act(SQ.ap[:, :, :n], hb.ap[:, :, :n], AF.Square, [hb], [SQ])
        bk = K.bank()
        for c in range(8):
            K.mm(bk.ap[:, :n], ones, SQ.ap[:, c, :n], c == 0, c == 7, [SQ, CB], [bk])
        K.act(RS.ap[:, :n], bk.ap[:, :n], AF.Sqrt, [bk], [RS], bias=EPS, scale=1.0 / D)
        K.recip(RS.ap[:, :n], RS.ap[:, :n], [RS], [RS])
        for c in range(8):
            dst = hb if out_dtype_f32_inplace else HN
            K.stt(dst.ap[:, c, :n], hb.ap[:, c, :n], pc(gname, gidx0 + c), RS.ap[:, :n], ALU.mult, ALU.mult,
                  [hb, RS, PRM], [dst])

    def phaseA(l, h_src):
        m0 = K.mark()
        WGR = [None] * 7
        for g in (0, 5, 1, 6, 3, 4, 2):
            WGR[g] = K.sb([128, 8, 512], BF16, f"W_in{g}")
            K.dma("pool", WGR[g].ap, w_in[l, :, g * 512:(g + 1) * 512].rearrange("(kc p) c -> p kc c", p=128),
                  WGR[g], [], [WGR[g]])

        def Wc(kc, col):
            g, o = col // 512, col % 512
            return WGR[g].ap[:, kc, o:o + 128], WGR[g]

        def Wv(kc):
            return WGR[2].ap[:, kc, :], WGR[2]
        HB = [K.sb([128, 8, 512], F32, "hb") for _ in range(2)]
        CS = [K.sb([128, 2, 512], F32, "cs") for _ in range(2)]
        SQ = K.sb([128, 8, 512], BF16, "sq")
        RS = K.sb([128, 512], F32, "rs")
        HNb = [K.sb([128, 8, 512], BF16, "hn") for _ in range(2)]
        T1 = [K.sb([128, 512], F32, "t1") for _ in range(2)]
        T2 = [K.sb([128, 512], F32, "t2") for _ in range(2)]
        SG = [K.sb([128, 512], F32, "sg") for _ in range(2)]
        QST = [K.sb([128, 4, 512], BF16, "qst") for _ in range(2)]
        KST = [K.sb([128, 4, 512], BF16, "kst") for _ in range(2)]
        ZST = [K.sb([128, 4, 512], BF16, "zst") for _ in range(2)]
        VST = [K.sb([128, 4, 512], BF16, "vst") for _ in range(2)]
        ZER = K.sb([128, 4, 16], BF16, "zer")
        zt_o = z_own(l)
        K.memset("pool", ZER.ap, 0.0, [ZER])
        K.dma("sp", zt_o[:, :, 0, 0:14].rearrange("c p t -> p c t"), ZER.ap[:, :, 0:14], ZER, [ZER], [K.D("xown", l)])
        tcount = 0
        def loadA(bi):
            c0, n = xblocks[bi]
            hb = HB[bi % 2]
            cs = CS[bi % 2]
            K.dma("sp", hb.ap[:, :, :n], h_src[:, :, c0:c0 + n].rearrange("c p t -> p c t"), hb, [K.D("hT", bi)], [hb])
            K.dma("sp", cs.ap[:, 0, :n], cos_d[:, c0:c0 + n], cs, [], [cs])
            K.dma("sp", cs.ap[:, 1, :n], sin_d[:, c0:c0 + n], cs, [], [cs])

        loadA(0)
        rmsnorm_block(HB[0], xblocks[0][1], "g1", l * 8, SQ, RS, HNb[0])
        for bi, (c0, n) in enumerate(xblocks):
            meta = (n == NMETA)
            hb = HB[bi % 2]
            cs = CS[bi % 2]
            HN = HNb[bi % 2]
            if bi + 1 < len(xblocks):
                loadA(bi + 1)
            qst, kst, zst, vst = QST[bi % 2], KST[bi % 2], ZST[bi % 2], VST[bi % 2]
            for which, st in (("q", qst), ("k", kst)):
                for hh in range(4):
                    oc = (0 if which == "q" else 512) + hh * 128
                    ocr = 2560 + (0 if which == "q" else 512) + hh * 128
                    b1 = K.bank()
                    for kc in range(8):
                        wa, wb = Wc(kc, oc)
                        K.mm(b1.ap[:, :n], wa, HN.ap[:, kc, :n], kc == 0, kc == 7, [wb, HN], [b1])
                    b2 = K.bank()
                    for kc in range(8):
                        wa, wb = Wc(kc, ocr)
                        K.mm(b2.ap[:, :n], wa, HN.ap[:, kc, :n], kc == 0, kc == 7, [wb, HN], [b2])
                    t1 = T1[tcount % 2]
                    t2 = T2[tcount % 2]
                    tcount += 1
                    K.tt("dve", t1.ap[:, :n], b1.ap[:, :n], cs.ap[:, 0, :n], ALU.mult, [b1, cs], [t1])
                    K.tt("dve", t2.ap[:, :n], b2.ap[:, :n], cs.ap[:, 1, :n], ALU.mult, [b2, cs], [t2])
                    K.tt("pool", st.ap[:, hh, :n], t1.ap[:, :n], t2.ap[:, :n], ALU.add, [t1, t2], [st])
            K.dma("sp", qT["w"][:, :, c0:c0 + n].rearrange("h p t -> p h t"), qst.ap[:, :, :n], qst, [qst],
                  [K.D("qT", bi)])
            if meta:
                K.dma("sp", kTm["w"].rearrange("h p t -> p h t"), kst.ap[:, :, :n], kst, [kst], [K.D("kTm")])
            else:
                for j in range(4 // HP):
                    K.dma("sp", k_own(l, j)[:, :, c0:c0 + n].rearrange("h p t -> p h t"),
                          kst.ap[:, j * HP:(j + 1) * HP, :n], kst, [kst], [K.D("xown", l)])
            if bi + 1 < len(xblocks):
                rmsnorm_block(HB[(bi + 1) % 2], xblocks[bi + 1][1], "g1", l * 8, SQ, RS, HNb[(bi + 1) % 2])
            for cc in range(4):
                ba = K.bank()
                for kc in range(8):
                    wa, wb = Wc(kc, 1536 + cc * 128)
                    K.mm(ba.ap[:, :n], wa, HN.ap[:, kc, :n], kc == 0, kc == 7, [wb, HN], [ba])
                bg = K.bank()
                for kc in range(8):
                    wa, wb = Wc(kc, 2048 + cc * 128)
                    K.mm(bg.ap[:, :n], wa, HN.ap[:, kc, :n], kc == 0, kc == 7, [wb, HN], [bg])
                sg = SG[cc % 2]
                K.act(sg.ap[:, :n], bg.ap[:, :n], AF.Sigmoid, [bg, PRM], [sg], bias=pc("bglu", l * 8 + 4 + cc))
                K.stt(zst.ap[:, cc, :n], ba.ap[:, :n], pc("bglu", l * 8 + cc), sg.ap[:, :n], ALU.add, ALU.mult,
                      [ba, sg, PRM], [zst])
            K.dma("sp", zT["w"][:, :, c0:c0 + n].rearrange("c p t -> p c t"), zst.ap[:, :, :n], zst, [zst],
                  [K.D("zT", bi)])
            if meta:
                K.dma("sp", zt_o[:, :, 0, 14:30].rearrange("c p t -> p c t"), zst.ap[:, :, 0:16], zst, [zst],
                      [K.D("xown", l)])
            else:
                for i in range(4):
                    s = (c0 // 128) + i
                    K.dma("sp", zt_o[:, :, s + 1, :].rearrange("c p t -> p c t"),
                          zst.ap[:, :, i * 128 + 98:i * 128 + 128], zst, [zst], [K.D("xown", l)])
            nts = (n + 127) // 128
            for ts_ in range(nts):
                m = min(128, n - ts_ * 128)
                bv = K.bank()
                for kc in range(8):
                    wa, wb = Wv(kc)
                    K.mm(bv.ap[:m, :512], HN.ap[:, kc, ts_ * 128:ts_ * 128 + m], wa, kc == 0, kc == 7, [wb, HN], [bv])
                K.copy("act", vst.ap[:m, ts_, :], bv.ap[:m, :512], [bv], [vst])
            if meta:
                K.dma("sp", vm["w"], vst.ap[:NMETA, 0, :], vst, [vst], [K.D("vm")])
            else:
                K.dma("sp", v_own(l, c0, n).rearrange("(s i) e -> i s e", i=128), vst.ap[:, :, :], vst, [vst],
                      [K.D("xown", l)])
        K.barrier()
        K.release(m0)

    XSEM = {pn: Buf(None, "xsem_" + pn) for (pn, _) in pieces}

    def exchange(l):
        for (pn, rows) in pieces:
            o_, a_ = xown_t[pn][l % 2], xall_t[pn][l % 2]
            K._add("pool", lambda e, o_=o_, a_=a_: e.collective_compute(
                "AllGather", ALU.bypass, replica_groups=[[2 * i, 2 * i + 1] for i in range(n_pairs)],
                ins=[o_.opt()], outs=[a_.opt()]), [K.D("xown", l)], [K.D("xall", l)],
                dbuf=XSEM[pn], inc=1)

    def phaseB(l):
        lam_init = 0.8 - 0.6 * math.exp(-0.3 * l)
        zviews = [z_all(l, r) for r in range(2)]
        m0 = K.mark()
        LJ = K.sb([128, 64], F32, "lj")
        LD = K.sb([128, 4], F32, "ld")
        K.stt(LJ.ap, PRM.ap[:, PL["lq1"] + l * 64:PL["lq1"] + (l + 1) * 64], 1.0,
              PRM.ap[:, PL["lk1"] + l * 64:PL["lk1"] + (l + 1) * 64], ALU.mult, ALU.mult, [PRM], [LJ, LD],
              accum_out=LD.ap[:, 0:1])
        K.stt(LJ.ap, PRM.ap[:, PL["lq2"] + l * 64:PL["lq2"] + (l + 1) * 64], 1.0,
              PRM.ap[:, PL["lk2"] + l * 64:PL["lk2"] + (l + 1) * 64], ALU.mult, ALU.mult, [PRM, LJ], [LJ, LD],
              accum_out=LD.ap[:, 1:2])
        K.act(LD.ap[:, 2:4], LD.ap[:, 0:2], AF.Exp, [LD], [LD])
        K.tt("dve", LD.ap[:, 0:1], LD.ap[:, 2:3], LD.ap[:, 3:4], ALU.subtract, [LD], [LD])
        K.ts("dve", NEGLAM.ap[:, 0:1], LD.ap[:, 0:1], -1.0, ALU.mult, [LD], [NEGLAM], s2=-lam_init, op1=ALU.add)

        KTb = [K.sb([128, NKEY], BF16, "kt"), None]
        Vb = [K.sb([128, NKT, 129], BF16, "v"), None]
        QTb = [K.sb([128, TL], BF16, "qt"), None]
        K.memset("pool", Vb[0].ap[:, :, 128:129], 1.0, [Vb[0]])

        def loadH(hh):
            KT, V, QT = KTb[hh % 2], Vb[hh % 2], QTb[hh % 2]
            K.dma("sp", KT.ap[:, 0:NMETA], kTm["r"][hh], KT, [K.D("kTm")], [KT])
            K.dma("sp", V.ap[:NMETA, 0, 0:128], vm["r"][:, hh * 128:(hh + 1) * 128], V, [K.D("vm")], [V])
            for r in range(2):
                for sa in range(0, NS, 4):
                    sb_ = min(NS, sa + 4)
                    K.dma("sp", KT.ap[:, NMETA:].rearrange("p (s r c) -> p s r c", r=2, c=128)[:, sa:sb_, r, :],
                          k_all(l, r, hh).rearrange("p (s c) -> p s c", c=128)[:, sa:sb_, :], KT, [K.D("xall", l)],
                          [KT])
                    K.dma("sp", V.ap[:, 1:, :].rearrange("p (s r) e -> p s r e", r=2)[:, sa:sb_, r, 0:128],
                          v_all(l, r, sa * 128, (sb_ - sa) * 128).rearrange("(s i) e -> i s e", i=128)[:, :, hh * 128:(hh + 1) * 128],
                          V, [K.D("xall", l)], [V])
            K.dma("sp", QT.ap, qT["r"][hh], QT, [K.D("qT", j) for j in range(len(xblocks))], [QT])

        m1 = K.mark()
        DG = K.sb([128, 4, CW, 128], BF16, "dg")
        for cc in range(4):
            for j in range(CW):
                K.ts("dve", DG.ap[:, cc, j, :], ident, pc("convw", (l * 4 + cc) * CW + j), ALU.mult, [CB, PRM], [DG])
        ZC = [K.sb([128, 4, 4, 158], BF16, "zc") for _ in range(2)]
        CA = [K.sb([128, 4, 4, HALO], BF16, "ca") for _ in range(2)]
        CBB = [K.sb([128, 4, 4, HALO], BF16, "cbb") for _ in range(2)]
        CT = K.sb([128, 4, 4, HALO], F32, "ct")
        CT2 = K.sb([128, 4, 4, HALO], F32, "ct2")
        Y32b = [K.sb([128, 4, 512], F32, "y32") for _ in range(2)]
        YBF = K.sb([128, 4, 512], BF16, "ybf")
        YSQ = K.sb([128, 4, 512], BF16, "ysq")
        MEAN = K.sb([128, 512], F32, "mean")
        MSQ = K.sb([128, 512], F32, "msq")
        RSD = K.sb([128, 512], F32, "rsd")
        TT = [K.sb([128, 512], F32, "tt") for _ in range(2)]
        CST = [K.sb([128, 4, 512], BF16, "cst") for _ in range(2)]
        def loadB1(bi):
            c0, n = xblocks[bi]
            meta = (n == NMETA)
            nsl, wd = (1, NMETA) if meta else (4, 128)
            zc = ZC[bi % 2]
            for cc in range(4):
                K.dma("sp", zc.ap[:, cc, :nsl, HALO:HALO + wd],
                      zT["r"][cc, :, c0:c0 + n].rearrange("p (s t) -> p s t", t=wd), zc, [K.D("zT", bi)], [zc])
            if not meta:
                ca, cbb = CA[bi % 2], CBB[bi % 2]
                s0 = c0 // 128
                for cc in range(4):
                    K.dma("sp", ca.ap[:, cc, :, :], zviews[0][cc, :, s0 + 1:s0 + 5, :], ca, [K.D("xall", l)], [ca])
                    K.dma("sp", cbb.ap[:, cc, :, :], zviews[1][cc, :, s0:s0 + 4, :], cbb, [K.D("xall", l)], [cbb])

        def frontB1(bi):
            c0, n = xblocks[bi]
            meta = (n == NMETA)
            nsl, wd = (1, NMETA) if meta else (4, 128)
            zc = ZC[bi % 2]
            Y32 = Y32b[bi % 2]
            if meta:
                K.memset("pool", zc.ap[:, :, 0, 0:HALO], 0.0, [zc])
            else:
                ca, cbb = CA[bi % 2], CBB[bi % 2]
                K.ts("pool", CT.ap, ca.ap, pc("sel", 0), ALU.mult, [ca, PRM], [CT])
                K.ts("pool", CT2.ap, cbb.ap, pc("sel", 1), ALU.mult, [cbb, PRM], [CT2])
                K.tt("pool", zc.ap[:, :, :, 0:HALO], CT.ap, CT2.ap, ALU.add, [CT, CT2], [zc])
            for cc in range(4):
                bk = K.bank()
                for j in range(CW):
                    K.mm(bk.ap[:, :n].rearrange("p (s t) -> p s t", t=wd), DG.ap[:, cc, j, :],
                         zc.ap[:, cc, :nsl, j:j + wd], j == 0, j == CW - 1, [DG, zc], [bk])
                K.act(Y32.ap[:, cc, :n], bk.ap[:, :n], AF.Identity, [bk, PRM], [Y32], bias=pc("convb", l * 4 + cc))

        def backB1(bi):
            c0, n = xblocks[bi]
            Y32 = Y32b[bi % 2]
            K.copy("act", YBF.ap[:, :, :n], Y32.ap[:, :, :n], [Y32], [YBF])
            K.act(YSQ.ap[:, :, :n], Y32.ap[:, :, :n], AF.Square, [Y32], [YSQ])
            bs = K.bank()
            for cc in range(4):
                K.mm(bs.ap[:, :n], ones, YBF.ap[:, cc, :n], cc == 0, cc == 3, [YBF, CB], [bs])
            bq = K.bank()
            for cc in range(4):
                K.mm(bq.ap[:, :n], ones, YSQ.ap[:, cc, :n], cc == 0, cc == 3, [YSQ, CB], [bq])
            K.ts("dve", MEAN.ap[:, :n], bs.ap[:, :n], 1.0 / 512, ALU.mult, [bs], [MEAN])
            K.tt("dve", MSQ.ap[:, :n], MEAN.ap[:, :n], MEAN.ap[:, :n], ALU.mult, [MEAN], [MSQ])
            K.stt(RSD.ap[:, :n], bq.ap[:, :n], 1.0 / 512, MSQ.ap[:, :n], ALU.mult, ALU.subtract, [bq, MSQ], [RSD])
            K.act(RSD.ap[:, :n], RSD.ap[:, :n], AF.Sqrt, [RSD], [RSD], bias=EPS, scale=1.0)
            K.recip(RSD.ap[:, :n], RSD.ap[:, :n], [RSD], [RSD])
            cst = CST[bi % 2]
            for cc in range(4):
                t = TT[cc % 2]
                K.tt("dve", t.ap[:, :n], Y32.ap[:, cc, :n], MEAN.ap[:, :n], ALU.subtract, [Y32, MEAN], [t])
                K.tt("dve", t.ap[:, :n], t.ap[:, :n], RSD.ap[:, :n], ALU.mult, [t, RSD], [t])
                K.act(cst.ap[:, cc, :n], t.ap[:, :n], AF.Silu, [t, PRM], [cst], bias=pc("lnb", l * 4 + cc),
                      scale=pc("lng", l * 4 + cc))
            K.dma("sp", mixc_d[:, :, c0:c0 + n].rearrange("c p t -> p c t"), cst.ap[:, :, :n], cst, [cst],
                  [K.D("mixc", bi)])

        nblk = len(xblocks)
        loadB1(0)
        if nblk > 1:
            loadB1(1)
        loadH(0)
        frontB1(0)
        for bi in range(nblk):
            if bi + 2 < nblk:
                loadB1(bi + 2)
            if bi + 1 < nblk:
                frontB1(bi + 1)
            backB1(bi)
        K.barrier()
        K.release(m1)

        KTb[1] = K.sb([128, NKEY], BF16, "kt")
        Vb[1] = K.sb([128, NKT, 129], BF16, "v")
        QTb[1] = K.sb([128, TL], BF16, "qt")
        K.memset("pool", Vb[1].ap[:, :, 128:129], 1.0, [Vb[1]])
        Pb = [K.sb([128, 2, 512], BF16, "p") for _ in range(4)]
        OA = [K.sb([128, 4, 2, 129], F32, "oa") for _ in range(2)]
        RZ = K.sb([128, 4, 2], F32, "rz")
        S1 = K.sb([128, 4], F32, "s1")
        O0 = [K.sb([128, 128], F32, "o0") for _ in range(2)]
        OO = [K.sb([128, 4, 128], F32, "oo") for _ in range(2)]
        JK = K.sb([128, 128], F32, "jk")
        SS = K.sb([128, 4], F32, "ss")
        RSTD = K.sb([128, 4], F32, "rstd")
        AOS = [K.sb([128, 4, 128], BF16, "aos") for _ in range(2)]
        abank = banks[4:8]
        sc_ = 1.0 / (128.0 * (1.0 - lam_init) ** 2)
        bb_ = EPS / ((1.0 - lam_init) ** 2)
        it = 0
        pidx = 0
        pending_tail = []
        for hh in range(4):
            KT, V, QT = KTb[hh % 2], Vb[hh % 2], QTb[hh % 2]
            if hh + 1 < 4:
                loadH(hh + 1)
            qblocks = [(m, 4, 128) for m in range(NB)] + [(NB, 1, NMETA)]
            for (m, nsl, wq) in qblocks:
                meta = (wq == NMETA)
                if meta:
                    ktiles = [(0, NMETA, 0, "M", 0)]
                    qbase = NX
                else:
                    ktiles = [(0, NMETA, 0, None, 0)]
                    for g in range(8 * m + 8):
                        r = g - 8 * m
                        if r < 0:
                            ktiles.append((1 + g, 128, 0, None, 0))
                        else:
                            ktiles.append((1 + g, 128, r // 2, "E" if r % 2 == 0 else "O", r // 2))
                    qbase = 4 * m * 128
                oa = OA[it % 2]
                nkt = len(ktiles)
                pps = {}

                def qk_stage(idx):
                    nonlocal pidx
                    (kt, nk, i0, mk, im) = ktiles[idx]
                    ncols = (nsl - i0) * wq
                    q0 = qbase + i0 * wq
                    k0 = 0 if kt == 0 else NMETA + (kt - 1) * 128
                    bp = bankpair[pidx % 2]
                    pp = Pb[pidx % len(Pb)]
                    pidx += 1
                    pps[idx] = pp
                    for c in range(2):
                        K.mm(bp.ap[:nk, c, :ncols], KT.ap[c * 64:(c + 1) * 64, k0:k0 + nk],
                             QT.ap[c * 64:(c + 1) * 64, q0:q0 + ncols], True, True, [KT, QT], [bp])
                    K.act(pp.ap[:nk, :, :ncols], bp.ap[:nk, :, :ncols], AF.Exp, [bp], [pp], scale=0.125)
                    if mk is not None:
                        mka = {"E": maskE, "O": maskO, "M": maskM}[mk]
                        for c in range(2):
                            a = pp.ap[:nk, c, (im - i0) * wq:(im - i0 + 1) * wq]
                            K.tt("dve", a, a, mka[:nk, :wq], ALU.mult, [pp, CB], [pp])

                def pv_stage(idx):
                    (kt, nk, i0, mk, im) = ktiles[idx]
                    pp = pps.pop(idx)
                    last = idx == nkt - 1
                    for i in range(i0, nsl):
                        for c in range(2):
                            K.mm(abank[i].ap[:wq, c * 256:c * 256 + 129], pp.ap[:nk, c, (i - i0) * wq:(i - i0 + 1) * wq],
                                 V.ap[:nk, kt, :], idx == 0 and c == 0, last, [pp, V], [abank[i]],
                                 skip_group_check=True)

                for idx in range(nkt + 2):
                    if idx < nkt:
                        qk_stage(idx)
                    if idx >= 2:
                        pv_stage(idx - 2)
                    if idx == min(5, nkt + 1) and pending_tail:
                        pending_tail.pop(0)()
                for i in range(nsl):
                    K.copy("dve", oa.ap[:wq, i, :, :],
                           abank[i].ap[:wq, :].rearrange("p (c x) -> p c x", x=256)[:, :, 0:129], [abank[i]], [oa])
                def make_tail(oa=oa, wq=wq, nsl=nsl, qbase=qbase, m=m, hh=hh, it_=it):
                    def tail():
                        K.recip(RZ.ap[:wq, :nsl, :], oa.ap[:wq, :nsl, :, 128], [oa], [RZ])
                        K.ts("dve", S1.ap[:wq, :nsl], RZ.ap[:wq, :nsl, 1], NEGLAM.ap[:wq, 0:1], ALU.mult, [RZ, NEGLAM],
                             [S1])
                        aos = AOS[it_ % 2]
                        oo = OO[it_ % 2]
                        for i in range(nsl):
                            o0 = O0[i % 2]
                            K.ts("dve", o0.ap[:wq, :], oa.ap[:wq, i, 0, 0:128], RZ.ap[:wq, i, 0:1], ALU.mult, [oa, RZ],
                                 [o0])
                            K.stt(oo.ap[:wq, i, :], oa.ap[:wq, i, 1, 0:128], S1.ap[:wq, i:i + 1], o0.ap[:wq, :], ALU.mult,
                                  ALU.add, [oa, S1, o0], [oo])
                            K.stt(JK.ap[:wq, :], oo.ap[:wq, i, :], 1.0, oo.ap[:wq, i, :], ALU.mult, ALU.mult, [oo],
                                  [JK, SS], accum_out=SS.ap[:wq, i:i + 1])
                        K.act(RSTD.ap[:wq, :nsl], SS.ap[:wq, :nsl], AF.Ln, [SS], [RSTD], bias=bb_, scale=sc_)
                        K.act(RSTD.ap[:wq, :nsl], RSTD.ap[:wq, :nsl], AF.Exp, [RSTD], [RSTD], scale=-0.5)
                        for i in range(nsl):
                            K.stt(aos.ap[:wq, i, :], oo.ap[:wq, i, :], RSTD.ap[:wq, i:i + 1],
                                  PRM.ap[:wq, PL["subg"] + l * 128:PL["subg"] + (l + 1) * 128], ALU.mult, ALU.mult,
                                  [oo, RSTD, PRM], [aos])
                        K.dma("sp", ao_d[qbase:qbase + nsl * wq, hh * 128:(hh + 1) * 128].rearrange(
                            "(s q) e -> q s e", q=wq), aos.ap[:wq, :nsl, :], aos, [aos], [K.D("ao", m)])
                    return tail

                pending_tail.append(make_tail())
                it += 1
        while pending_tail:
            pending_tail.pop(0)()
        K.barrier()
        K.release(m0)

    def phaseC(l, h_src, final):
        m0 = K.mark()
        WO = K.sb([128, 8, D], BF16, "wo")
        for kc in range(8):
            K.dma("pool", WO.ap[:, kc, :], w_out[l, kc * 128:(kc + 1) * 128, :], WO, [], [WO])
        HB = [K.sb([128, 8, 512], F32, "hb") for _ in range(3)]
        AOB = [K.sb([128, 4, 512], BF16, "aob") for _ in range(2)]
        MIX = [K.sb([128, 8, 512], BF16, "mix") for _ in range(2)]
        SQ = K.sb([128, 8, 512], BF16, "sq")
        RS = K.sb([128, 512], F32, "rs")
        HN2 = [K.sb([128, 8, 512], BF16, "hn2") for _ in range(2)]
        def loadC1(bi):
            c0, n = xblocks[bi]
            meta = (n == NMETA)
            nsl, wd = (1, NMETA) if meta else (4, 128)
            hb, aob, mix = HB[bi % 3], AOB[bi % 2], MIX[bi % 2]
            K.dma("sp", hb.ap[:, :, :n], h_src[:, :, c0:c0 + n].rearrange("c p t -> p c t"), hb, [K.D("hT", bi)], [hb])
            K.dma("sp", aob.ap[:wd, :nsl, :], ao_d[c0:c0 + n, :].rearrange("(s q) e -> q s e", q=wd), aob,
                  [K.D("ao", bi)], [aob])
            K.dma("sp", mix.ap[:, 4:8, :n], mixc_d[:, :, c0:c0 + n].rearrange("c p t -> p c t"), mix,
                  [K.D("mixc", bi)], [mix])

        def frontC1(bi):
            c0, n = xblocks[bi]
            meta = (n == NMETA)
            nsl, wd = (1, NMETA) if meta else (4, 128)
            hb, aob, mix = HB[bi % 3], AOB[bi % 2], MIX[bi % 2]
            for hh in range(4):
                bt = K.bank()
                btb = bt.ap.bitcast(BF16)
                for i in range(nsl):
                    K.transpose(btb[:, i * wd:(i + 1) * wd], aob.ap[:wd, i, hh * 128:(hh + 1) * 128], ident[:wd, :wd],
                                [aob, CB], [bt])
                K.copy("act", mix.ap[:, hh, :n], btb[:, :n], [bt], [mix])
            for oc in range(8):
                by = K.bank()
                for kc in range(8):
                    K.mm(by.ap[:, :n], WO.ap[:, kc, oc * 128:(oc + 1) * 128], mix.ap[:, kc, :n], kc == 0, kc == 7,
                         [WO, mix], [by])
                K.tt("dve", hb.ap[:, oc, :n], hb.ap[:, oc, :n], by.ap[:, :n], ALU.add, [hb, by], [hb])
            K.dma("sp", hmid_d[:, :, c0:c0 + n].rearrange("c p t -> p c t"), hb.ap[:, :, :n], hb, [hb],
                  [K.D("hmid", bi)])

        def backC1(bi):
            c0, n = xblocks[bi]
            hb, hn2 = HB[bi % 3], HN2[bi % 2]
            rmsnorm_block(hb, n, "g2", l * 8, SQ, RS, hn2)
            K.dma("sp", hn2_d[:, :, c0:c0 + n].rearrange("c p t -> p c t"), hn2.ap[:, :, :n], hn2, [hn2],
                  [K.D("hn2", bi)])

        nblk = len(xblocks)
        loadC1(0)
        if nblk > 1:
            loadC1(1)
        frontC1(0)
        for bi in range(nblk):
            if bi + 2 < nblk:
                loadC1(bi + 2)
            if bi + 1 < nblk:
                frontC1(bi + 1)
            backC1(bi)
        K.barrier()
        K.release(m0)
        NG = (FC + 3) // 4
        WGg = [None] * NG
        WUg = [None] * NG
        for g in range(NG):
            nf = min(4, FC - 4 * g)
            WGg[g] = K.sb([128, 8, nf * 128], BF16, f"wg{g}")
            WUg[g] = K.sb([128, 8, nf * 128], BF16, f"wu{g}")
            K.dma("pool", WGg[g].ap, w_gu[l, :, g * 512:g * 512 + nf * 128].rearrange("(kc p) c -> p kc c", p=128),
                  WGg[g], [], [WGg[g]])
            K.dma("pool", WUg[g].ap,
                  w_gu[l, :, DFF + g * 512:DFF + g * 512 + nf * 128].rearrange("(kc p) c -> p kc c", p=128),
                  WUg[g], [], [WUg[g]])
        WDh = [K.sb([128, FC // 2, D], BF16, f"wd{i}") for i in range(2)]
        for i in range(2):
            K.dma("pool", WDh[i].ap,
                  w_dn[l, i * (FC // 2) * 128:(i + 1) * (FC // 2) * 128, :].rearrange("(f p) c -> p f c", p=128),
                  WDh[i], [], [WDh[i]])
        HN = [K.sb([128, 8, 256], BF16, "hn") for _ in range(2)]
        HB2 = [K.sb([128, 8, 256], F32, "hb2") for _ in range(2)]
        ACT = K.sb([128, FC, 256], BF16, "act")
        SG = [K.sb([128, 256], F32, "sg") for _ in range(2)]
        SQ2 = K.sb([128, 8, 256], BF16, "sq2")
        RS2 = K.sb([128, 256], F32, "rs2")
        fblocks = [(j * 256, 256) for j in range(NX // 256)] + [(NX, NMETA)]
        def loadC2(bi):
            c0, n = fblocks[bi]
            sbi = NB if n == NMETA else c0 // 512
            hn, hb = HN[bi % 2], HB2[bi % 2]
            K.dma("sp", hn.ap[:, :, :n], hn2_d[:, :, c0:c0 + n].rearrange("c p t -> p c t"), hn, [K.D("hn2", sbi)], [hn])
            K.dma("sp", hb.ap[:, :, :n], hmid_d[:, :, c0:c0 + n].rearrange("c p t -> p c t"), hb, [K.D("hmid", sbi)],
                  [hb])

        loadC2(0)
        for bi, (c0, n) in enumerate(fblocks):
            meta = (n == NMETA)
            sbi = NB if meta else c0 // 512
            hn, hb = HN[bi % 2], HB2[bi % 2]
            if bi + 1 < len(fblocks):
                loadC2(bi + 1)
            for f in range(FC):
                bg = K.bank()
                for kc in range(8):
                    K.mm(bg.ap[:, :n], WGg[f // 4].ap[:, kc, (f % 4) * 128:(f % 4 + 1) * 128], hn.ap[:, kc, :n],
                         kc == 0, kc == 7, [WGg[f // 4], hn], [bg])
                bu = K.bank()
                for kc in range(8):
                    K.mm(bu.ap[:, :n], WUg[f // 4].ap[:, kc, (f % 4) * 128:(f % 4 + 1) * 128], hn.ap[:, kc, :n],
                         kc == 0, kc == 7, [WUg[f // 4], hn], [bu])
                sg = SG[f % 2]
                K.act(sg.ap[:, :n], bg.ap[:, :n], AF.Silu, [bg], [sg])
                K.tt("dve", ACT.ap[:, f, :n], sg.ap[:, :n], bu.ap[:, :n], ALU.mult, [sg, bu], [ACT])
            for oc in range(8):
                bd = K.bank()
                for f in range(FC):
                    wdb = WDh[f // (FC // 2)]
                    K.mm(bd.ap[:, :n], wdb.ap[:, f % (FC // 2), oc * 128:(oc + 1) * 128], ACT.ap[:, f, :n], f == 0,
                         f == FC - 1, [wdb, ACT], [bd])
                K.tt("dve", hb.ap[:, oc, :n], hb.ap[:, oc, :n], bd.ap[:, :n], ALU.add, [hb, bd], [hb])
            if final:
                if not meta:
                    rmsnorm_block(hb, n, "gf", 0, SQ2, RS2, None, out_dtype_f32_inplace=True)
                    K.dma("sp", outT[:, :, c0:c0 + n].rearrange("c p t -> p c t"), hb.ap[:, :, :n], hb, [hb],
                          [K.D("outT", bi)])
            else:
                K.dma("sp", hT["w"][:, :, c0:c0 + n].rearrange("c p t -> p c t"), hb.ap[:, :, :n], hb, [hb],
                      [K.D("hT", sbi)])
        K.barrier()
        K.release(m0)

    h_written_here = False
    for (ph, l) in stages:
        if ph == "A":
            phaseA(l, xT if l == 0 else (hT["w"] if h_written_here else hT["r"]))
            if fused:
                exchange(l)
        elif ph == "B":
            phaseB(l)
        elif ph == "C":
            hsrc = xT if l == 0 else hT["r"]
            phaseC(l, hsrc, final=(l == DEPTH - 1))
            h_written_here = True
    K.finish()
    K.emit(stack)
    stack.close()
    return nc, ext_in, ext_out


def _bf(a):
    return np.asarray(a, dtype=np.float32).astype(ml_dtypes.bfloat16)


def host_prepare(inputs, NS, DEPTH, B):
    NX = NS * 128
    TL = NX + NMETA
    PL = prm_layout(DEPTH)
    f32 = np.float32
    x = np.asarray(inputs["x"], f32)
    meta = np.asarray(inputs["meta_tokens"], f32)
    w_in = np.asarray(inputs["w_in"], f32)
    perm = np.arange(512).reshape(4, 2, 64)
    perm = np.concatenate([perm[:, :, 32:], perm[:, :, :32]], axis=-1).reshape(512)
    w_in_ext = np.ascontiguousarray(np.concatenate([w_in, w_in[:, :, perm], w_in[:, :, 512 + perm]], axis=-1))
    w_out = np.ascontiguousarray(np.asarray(inputs["w_out"], f32))
    w_gu = np.ascontiguousarray(np.asarray(inputs["w_gate_up"], f32))
    w_dn = np.ascontiguousarray(np.asarray(inputs["w_down"], f32))

    def colmajor(v, nch):
        v = np.asarray(v, f32)
        lead = v.shape[:-1]
        v = v.reshape(*lead, nch, 128)
        v = np.moveaxis(v, -1, 0)
        return v.reshape(128, -1)

    def rep(v):
        v = np.asarray(v, f32).reshape(1, -1)
        return np.broadcast_to(v, (128, v.shape[1]))

    prm_base = np.zeros((128, PL["_n"]), f32)
    prm_base[:, PL["g1"]:PL["g1"] + DEPTH * 8] = colmajor(inputs["norm1_g"], 8)
    prm_base[:, PL["g2"]:PL["g2"] + DEPTH * 8] = colmajor(inputs["norm2_g"], 8)
    prm_base[:, PL["gf"]:PL["gf"] + 8] = colmajor(inputs["final_g"], 8)
    prm_base[:, PL["bglu"]:PL["bglu"] + DEPTH * 8] = colmajor(inputs["b_glu"], 8)
    cw = np.asarray(inputs["conv_w"], f32)
    cw = cw.reshape(DEPTH, CW, 4, 128).transpose(3, 0, 2, 1).reshape(128, -1)
    prm_base[:, PL["convw"]:PL["convw"] + DEPTH * 4 * CW] = cw
    prm_base[:, PL["convb"]:PL["convb"] + DEPTH * 4] = colmajor(inputs["conv_b"], 4)
    prm_base[:, PL["lng"]:PL["lng"] + DEPTH * 4] = colmajor(inputs["conv_ln_g"], 4)
    prm_base[:, PL["lnb"]:PL["lnb"] + DEPTH * 4] = colmajor(inputs["conv_ln_b"], 4)
    for nm, key in (("lq1", "lam_q1"), ("lk1", "lam_k1"), ("lq2", "lam_q2"), ("lk2", "lam_k2")):
        prm_base[:, PL[nm]:PL[nm] + DEPTH * 64] = rep(inputs[key])
    prm_base[:, PL["subg"]:PL["subg"] + DEPTH * 128] = rep(inputs["subln_g"])

    inv = (10000.0 ** (-np.arange(0, 64, 2, dtype=f32) / 64.0)).astype(f32)
    tri = (np.arange(128)[:, None] <= np.arange(128)[None, :]).astype(f32)
    in_maps = []
    for b in range(B):
        for p in range(2):
            tiles = [x[b, 128 * (2 * s + p):128 * (2 * s + p) + 128, :] for s in range(NS)]
            hx = np.concatenate(tiles + [meta], axis=0)
            xTc = np.ascontiguousarray(hx.T).reshape(8, 128, TL)
            pos = np.concatenate([NMETA + 128 * (2 * s + p) + np.arange(128) for s in range(NS)] + [np.arange(NMETA)])
            ang = pos.astype(f32)[:, None] * inv[None, :]
            ang = np.concatenate([ang, ang], axis=-1).astype(f32)
            cosd = np.cos(ang).astype(f32).T
            sind = np.sin(ang).astype(f32).T
            sign = np.where(np.arange(64) < 32, -1.0, 1.0).astype(f32)[:, None]
            sind = sind * sign
            cosT = np.ascontiguousarray(np.concatenate([cosd, cosd], axis=0))
            sinT = np.ascontiguousarray(np.concatenate([sind, sind], axis=0))
            cb = np.zeros((128, CB_N), f32)
            cb[:, CB_IDENT:CB_IDENT + 128] = np.eye(128, dtype=f32)
            cb[:, CB_ONES:CB_ONES + 128] = 1.0
            cb[:, CB_ME:CB_ME + 128] = tri if p == 0 else 1.0
            cb[:, CB_MO:CB_MO + 128] = 0.0 if p == 0 else tri
            cb[:16, CB_MM:CB_MM + 16] = tri[:16, :16]
            prm = prm_base.copy()
            prm[:, PL["sel"]] = 1.0 if p == 1 else 0.0
            prm[:, PL["sel"] + 1] = 0.0 if p == 1 else 1.0
            in_maps.append({"xT": xTc, "prm": prm, "cb": _bf(cb), "cosT": cosT, "sinT": sinT, "w_in": w_in_ext,
                            "w_out": w_out, "w_gu": w_gu, "w_dn": w_dn})
    return in_maps


def assemble_output(outs, NS, B):
    NX = NS * 128
    S = 2 * NX
    out = np.empty((B, S, D), np.float32)
    for b in range(B):
        for p in range(2):
            o = np.asarray(outs[2 * b + p]).reshape(D, NX).T
            for s in range(NS):
                g = 2 * s + p
                out[b, 128 * g:128 * g + 128, :] = o[128 * s:128 * s + 128, :]
    return out


_PROG_CACHE = {}


def run_model(inputs, NS, DEPTH, B, fused=True):
    n_cores = 2 * B
    in_maps = host_prepare(inputs, NS, DEPTH, B)
    if fused:
        stages = []
        for l in range(DEPTH):
            stages += [("A", l), ("B", l), ("C", l)]
        key = ("f", NS, DEPTH, B)
        if key not in _PROG_CACHE:
            _PROG_CACHE[key] = build_program(NS, DEPTH, stages, True, B)
        nc, ext_in, ext_out = _PROG_CACHE[key]
        res = run_bass_kernel_spmd(nc, [{k: m[k] for k in ext_in} for m in in_maps], core_ids=list(range(n_cores)))
        return assemble_output([r["outT"] for r in res.results], NS, B)
    groups = [[("A", 0)]]
    for l in range(DEPTH):
        g = [("B", l), ("C", l)]
        if l + 1 < DEPTH:
            g.append(("A", l + 1))
        groups.append(g)
    state = [dict() for _ in range(n_cores)]
    outs = None
    for gi, g in enumerate(groups):
        nc, ext_in, ext_out = build_program(NS, DEPTH, g, False, B)
        maps = []
        for c in range(n_cores):
            m = {}
            for k in ext_in:
                if k in in_maps[c]:
                    m[k] = in_maps[c][k]
                elif k.endswith("_in"):
                    base = k[:-3]
                    if base in state[c]:
                        m[k] = state[c][base]
                    else:
                        shp, dt = _shape_of(nc, k)
                        m[k] = np.zeros(shp, dt)
                else:
                    raise KeyError(k)
            maps.append(m)
        res = run_bass_kernel_spmd(nc, maps, core_ids=list(range(n_cores)))
        for c in range(n_cores):
            r = res.results[c]
            for k in ext_out:
                if k.endswith("_out"):
                    state[c][k[:-4]] = np.asarray(r[k])
        for b in range(B):
            for kname in list(state[2 * b].keys()):
                if kname.startswith("xo_"):
                    xa = np.concatenate([state[2 * b][kname], state[2 * b + 1][kname]], axis=0)
                    state[2 * b]["xa_" + kname[3:]] = xa
                    state[2 * b + 1]["xa_" + kname[3:]] = xa
        outs = [np.asarray(r["outT"]) for r in res.results]
    return assemble_output(outs, NS, B)


def _shape_of(nc, name):
    for alloc in nc.allocations:
        if isinstance(alloc, mybir.MemoryLocationSet) and alloc.memorylocations and alloc.memorylocations[0].name == name:
            return tuple(alloc.tensor_shape), mybir.dt.np(alloc.dtype)
    raise KeyError(name)


def kernel(**inputs):
    return run_model(inputs, NS=32, DEPTH=4, B=4, fused=True)
```

```python
import math
import contextlib
import numpy as np
import ml_dtypes
import concourse.bass as bass
import concourse.mybir as mybir
from concourse.bass_utils import run_bass_kernel_spmd

F32 = mybir.dt.float32
BF16 = mybir.dt.bfloat16
AF = mybir.ActivationFunctionType
ALU = mybir.AluOpType

D = 1024
KC = 8
DFF = 2816
FC = 22
NMETA = 16
CW = 31
HALO = 30
EPS = 1e-5
WIN_COLS = 3584
PIECE_ROWS = 2048


class Ins:
    __slots__ = ("eng", "fn", "deps", "signal", "val", "key", "slot", "inc")


class Slot:
    def __init__(self):
        self.count = 0
        self.sem = None


class Buf:
    def __init__(self, ap=None, name=""):
        self.ap = ap
        self.name = name
        self.w = {}
        self.r = {}
        self.slot = None


class Kern:
    def __init__(self, nc, arena, arena_cols, banks):
        self.nc = nc
        self.engs = {n: [] for n in ("pe", "act", "dve", "pool", "sp")}
        self.pending = {n: [] for n in self.engs}
        self.last = {}
        self.arena = arena
        self.arena_cols = arena_cols
        self.off = 0
        self.banks = banks
        self.bank_i = 0
        self.slots = []
        self.free_slots = {}
        self.live = []
        self.dram_bufs = {}

    def sb(self, shape, dtype, name=""):
        esz = 4 if dtype == F32 else 2
        n = 1
        for x in shape[1:]:
            n *= x
        nbytes = (n * esz + 63) // 64 * 64
        o = self.off
        self.off += nbytes
        assert self.off <= self.arena_cols * 2, f"SBUF arena overflow at {name}: {self.off}"
        ap = self.arena[:, o // 2:(o + n * esz) // 2]
        if dtype == F32:
            ap = ap.bitcast(F32)
        if len(shape) == 3:
            ap = ap.rearrange("p (a b) -> p a b", b=shape[2])
        elif len(shape) == 4:
            ap = ap.rearrange("p (a b c) -> p a b c", b=shape[2], c=shape[3])
        b = Buf(ap, name)
        self.live.append((o, b))
        return b

    def mark(self):
        return self.off

    def release(self, m):
        keep = []
        for (o, b) in self.live:
            if o >= m:
                if b.slot is not None:
                    self.free_slots.setdefault(b.slot.qk, []).append(b.slot)
                    b.slot = None
            else:
                keep.append((o, b))
        self.live = keep
        self.off = m

    def bank(self):
        b = self.banks[self.bank_i % len(self.banks)]
        self.bank_i += 1
        return b

    def D(self, name, j=0):
        k = (name, j)
        if k not in self.dram_bufs:
            self.dram_bufs[k] = Buf(None, f"{name}:{j}")
        return self.dram_bufs[k]

    def _add(self, eng, fn, reads, writes, dbuf=None, inc=16):
        ins = Ins()
        ins.eng = eng
        ins.fn = fn
        ins.signal = False
        ins.val = None
        ins.slot = None
        ins.inc = 1
        if dbuf is not None:
            if dbuf.slot is None:
                qk = (eng, inc)
                fl = self.free_slots.setdefault(qk, [])
                if fl:
                    dbuf.slot = fl.pop()
                else:
                    dbuf.slot = Slot()
                    dbuf.slot.qk = qk
                    self.slots.append(dbuf.slot)
            assert dbuf.slot.qk == (eng, inc), f"buffer {dbuf.name} used from two DMA queues"
            sl = dbuf.slot
            ins.slot = sl
            ins.inc = inc
            ins.key = ("d", id(sl))
            sl.count += inc
            ins.val = sl.count
            ins.signal = True
        else:
            ins.key = eng
        deps = {}
        for b in reads:
            for d in b.w.values():
                deps[id(d)] = d
        for b in writes:
            for d in b.w.values():
                deps[id(d)] = d
            for d in b.r.values():
                deps[id(d)] = d
        for d in self.pending[eng]:
            deps[id(d)] = d
        self.pending[eng] = []
        ins.deps = [d for d in deps.values() if not (d.key == "pe" and eng == "pe")]
        for d in ins.deps:
            d.signal = True
        for b in writes:
            b.w[ins.key] = ins
            b.r = {}
        for b in reads:
            b.r[ins.key] = ins
        self.engs[eng].append(ins)
        self.last[ins.key] = ins
        return ins

    def barrier(self):
        lst = list(self.last.values())
        for n in self.engs:
            self.pending[n] = list(lst)

    def finish(self):
        self.barrier()
        for n in self.engs:
            self._add(n, None, [], [])

    def mm(self, out, lhsT, rhs, start, stop, reads, writes, **kw):
        return self._add("pe", lambda e: e.matmul(out, lhsT, rhs, start=start, stop=stop, **kw), reads, writes)

    def transpose(self, out, in_, ident, reads, writes):
        return self._add("pe", lambda e: e.transpose(out, in_, ident), reads, writes)

    def act(self, out, in_, func, reads, writes, bias=None, scale=None):
        kw = {}
        if bias is not None:
            kw["bias"] = bias
        if scale is not None:
            kw["scale"] = scale
        return self._add("act", lambda e: e.activation(out, in_, func, **kw), reads, writes)

    def tt(self, eng, out, in0, in1, op, reads, writes):
        return self._add(eng, lambda e: e.tensor_tensor(out, in0, in1, op), reads, writes)

    def ts(self, eng, out, in0, s1, op0, reads, writes, s2=None, op1=None):
        if op1 is None:
            return self._add(eng, lambda e: e.tensor_scalar(out, in0, s1, None, op0), reads, writes)
        return self._add(eng, lambda e: e.tensor_scalar(out, in0, s1, s2, op0, op1), reads, writes)

    def stt(self, out, in0, scalar, in1, op0, op1, reads, writes, accum_out=None):
        return self._add("dve", lambda e: e.scalar_tensor_tensor(out, in0, scalar, in1, op0, op1, accum_out=accum_out),
                         reads, writes)

    def copy(self, eng, out, in_, reads, writes):
        if eng == "act":
            return self._add("act", lambda e: e.activation(out, in_, AF.Copy), reads, writes)
        return self._add(eng, lambda e: e.tensor_copy(out, in_), reads, writes)

    def recip(self, out, in_, reads, writes):
        return self._add("dve", lambda e: e.reciprocal(out, in_), reads, writes)

    def memset(self, eng, ap, val, writes):
        return self._add(eng, lambda e: e.memset(ap, val), [], writes)

    def dma(self, q, out, in_, sbuf, reads, writes):
        return self._add(q, lambda e: e.dma_start(out=out, in_=in_), reads, writes, dbuf=sbuf)

    def emit(self, stack):
        nc = self.nc
        esem = {n: stack.enter_context(nc.semaphore("es_" + n)) for n in self.engs}
        for i, sl in enumerate(self.slots):
            sl.sem = stack.enter_context(nc.semaphore(f"ds{i}"))
        for n, lst in self.engs.items():
            c = 0
            for ins in lst:
                if ins.slot is None and ins.signal:
                    c += 1
                    ins.val = c

        def sem_of(ins):
            return ins.slot.sem if ins.slot is not None else esem[ins.eng]

        def body_for(name):
            def body(e):
                waited = {}
                for ins in self.engs[name]:
                    for d in sorted(ins.deps, key=lambda d: d.val):
                        sm = sem_of(d)
                        k = id(sm)
                        if waited.get(k, 0) >= d.val:
                            continue
                        e.wait_ge(sm, d.val)
                        waited[k] = d.val
                    if ins.fn is None:
                        continue
                    bi = ins.fn(e)
                    if ins.signal:
                        bi.then_inc(sem_of(ins), ins.inc)
            return body

        with nc.Block() as block:
            block.tensor(body_for("pe"))
            block.scalar(body_for("act"))
            block.vector(body_for("dve"))
            block.gpsimd(body_for("pool"))
            block.sync(body_for("sp"))


def prm_layout(depth):
    o = {}
    c = 0
    for name, n in (("g1", depth * 8), ("g2", depth * 8), ("gf", 8), ("bglu", depth * 8), ("convw", depth * 4 * CW),
                    ("convb", depth * 4), ("lng", depth * 4), ("lnb", depth * 4), ("sel", 2),
                    ("lq1", depth * 64), ("lk1", depth * 64), ("lq2", depth * 64), ("lk2", depth * 64),
                    ("subg", depth * 128)):
        o[name] = c
        c += n
    o["_n"] = c
    return o


CB_IDENT, CB_ONES, CB_ME, CB_MO, CB_MM, CB_N = 0, 128, 256, 384, 512, 528


def build_program(NS, DEPTH, stages, fused, n_pairs):
    NX = NS * 128
    TL = NX + NMETA
    NB = NS // 4
    NKT = 1 + 2 * NS
    NKEY = NMETA + 2 * NX
    OV = 4 * 128 * NX
    OZ = OV + NX * 512
    NXCH = OZ + 4 * 128 * (NS + 1) * HALO
    XR = NXCH // 512
    PL = prm_layout(DEPTH)

    nc = bass.Bass("TRN2", target_bir_lowering=False)
    stack = contextlib.ExitStack()

    phases = set(stages)
    first_stage = stages[0]
    last_stage = stages[-1]

    produced = {}
    ext_in = []
    ext_out = []

    def dram(name, shape, dtype, role):
        kind = {"in": "ExternalInput", "out": "ExternalOutput", "tmp": "Internal"}[role]
        t = nc.dram_tensor(name, list(shape), dtype, kind=kind)
        if role == "in":
            ext_in.append(name)
        if role == "out":
            ext_out.append(name)
        return t.ap()

    xT = dram("xT", [8, 128, TL], F32, "in")
    prm_d = dram("prm", [128, PL["_n"]], F32, "in")
    cb_d = dram("cb", [128, CB_N], BF16, "in")
    cos_d = dram("cosT", [128, TL], F32, "in")
    sin_d = dram("sinT", [128, TL], F32, "in")
    w_in = dram("w_in", [DEPTH, D, WIN_COLS], F32, "in")
    w_out = dram("w_out", [DEPTH, D, D], F32, "in")
    w_gu = dram("w_gu", [DEPTH, D, 2 * DFF], F32, "in")
    w_dn = dram("w_dn", [DEPTH, DFF, D], F32, "in")

    def handoff(name, shape, dtype, producer_phase_of, consumer_phases_of):
        if fused:
            ap = dram(name, shape, dtype, "tmp")
            return {"r": ap, "w": ap}
        res = {}
        res["r"] = dram(name + "_in", shape, dtype, "in")
        res["w"] = dram(name + "_out", shape, dtype, "out")
        return res

    hT = handoff("hT", [8, 128, TL], F32, None, None)
    qT = handoff("qT", [4, 128, TL], BF16, None, None)
    zT = handoff("zT", [4, 128, TL], BF16, None, None)
    kTm = handoff("kTm", [4, 128, NMETA], BF16, None, None)
    vm = handoff("vm", [NMETA, 512], BF16, None, None)
    PR = PIECE_ROWS
    HP = min(4, (PR * 4) // NX)
    TP = min(NX, PR)
    pieces = [(f"k{j}", HP * NX // 4) for j in range(4 // HP)] + [(f"v{j}", TP) for j in range(NX // TP)] + \
             [("z", 30 * (NS + 1))]
    xown_t = {}
    xall_t = {}
    for (pn, rows) in pieces:
        if fused:
            xown_t[pn] = [dram(f"xo_{pn}{i}", [rows, 512], BF16, "tmp") for i in range(2)]
            xall_t[pn] = [dram(f"xa_{pn}{i}", [2 * rows, 512], BF16, "tmp") for i in range(2)]
        else:
            o_ = dram(f"xo_{pn}_out", [rows, 512], BF16, "out")
            a_ = dram(f"xa_{pn}_in", [2 * rows, 512], BF16, "in")
            xown_t[pn] = [o_, o_]
            xall_t[pn] = [a_, a_]

    def _flat(t):
        return t.rearrange("r c -> (r c)")

    def k_own(l, j):
        return _flat(xown_t[f"k{j}"][l % 2]).rearrange("(h p t) -> h p t", h=HP, p=128)

    def k_all(l, r, h):
        j, hl = h // HP, h % HP
        sz = HP * 128 * NX
        return _flat(xall_t[f"k{j}"][l % 2])[r * sz:(r + 1) * sz].rearrange("(h p t) -> h p t", h=HP, p=128)[hl]

    def v_own(l, t0, n):
        j = t0 // TP
        return xown_t[f"v{j}"][l % 2][t0 - j * TP:t0 - j * TP + n, :]

    def v_all(l, r, t0, n):
        j = t0 // TP
        return xall_t[f"v{j}"][l % 2][r * TP + t0 - j * TP:r * TP + t0 - j * TP + n, :]

    def z_own(l):
        return _flat(xown_t["z"][l % 2]).rearrange("(c p s t) -> c p s t", c=4, p=128, s=NS + 1)

    def z_all(l, r):
        sz = 4 * 128 * (NS + 1) * HALO
        return _flat(xall_t["z"][l % 2])[r * sz:(r + 1) * sz].rearrange("(c p s t) -> c p s t", c=4, p=128, s=NS + 1)

    ao_d = dram("ao", [TL, 512], BF16, "tmp")
    mixc_d = dram("mixc", [4, 128, TL], BF16, "tmp")
    hmid_d = dram("hmid", [8, 128, TL], F32, "tmp")
    hn2_d = dram("hn2", [8, 128, TL], BF16, "tmp")
    outT = dram("outT", [8, 128, NX], F32, "out")

    ARENA_COLS = 94 * 1024
    arena = stack.enter_context(nc.sbuf_tensor("arena", [128, ARENA_COLS], BF16))
    psum_all = stack.enter_context(nc.psum_tensor("psall", [128, 8 * 512], F32))
    banks = [Buf(psum_all[:, i * 512:(i + 1) * 512], f"bank{i}") for i in range(8)]
    bankpair = [Buf(psum_all[:, 0:1024].rearrange("p (c x) -> p c x", x=512), "bp0"),
                Buf(psum_all[:, 1024:2048].rearrange("p (c x) -> p c x", x=512), "bp1")]
    K = Kern(nc, arena, ARENA_COLS, banks)

    PRM = K.sb([128, PL["_n"]], F32, "prm")
    CB = K.sb([128, CB_N], BF16, "cb")
    NEGLAM = K.sb([128, 4], F32, "neglam")
    K.dma("sp", PRM.ap, prm_d, PRM, [], [PRM])
    K.dma("sp", CB.ap, cb_d, CB, [], [CB])
    ident = CB.ap[:, CB_IDENT:CB_IDENT + 128]
    ones = CB.ap[:, CB_ONES:CB_ONES + 128]
    maskE = CB.ap[:, CB_ME:CB_ME + 128]
    maskO = CB.ap[:, CB_MO:CB_MO + 128]
    maskM = CB.ap[:, CB_MM:CB_MM + 16]

    def pc(name, idx):
        o = PL[name] + idx
        return PRM.ap[:, o:o + 1]

    xblocks = [(j * 512, 512) for j in range(NB)] + [(NX, NMETA)]

    def rmsnorm_block(hb, n, gname, gidx0, SQ, RS, HN, out_dtype_f32_inplace=False):
        K.act(SQ.ap[:, :, :n], hb.ap[:, :, :n], AF.Square, [hb], [SQ])
        bk = K.bank()
        for c in range(8):
            K.mm(bk.ap[:, :n], ones, SQ.ap[:, c, :n], c == 0, c == 7, [SQ, CB], [bk])
        K.act(RS.ap[:, :n], bk.ap[:, :n], AF.Sqrt, [bk], [RS], bias=EPS, scale=1.0 / D)
        K.recip(RS.ap[:, :n], RS.ap[:, :n], [RS], [RS])
        for c in range(8):
            dst = hb if out_dtype_f32_inplace else HN
            K.stt(dst.ap[:, c, :n], hb.ap[:, c, :n], pc(gname, gidx0 + c), RS.ap[:, :n], ALU.mult, ALU.mult,
                  [hb, RS, PRM], [dst])

    def phaseA(l, h_src):
        m0 = K.mark()
        WGR = [None] * 7
        for g in (0, 5, 1, 6, 3, 4, 2):
            WGR[g] = K.sb([128, 8, 512], BF16, f"W_in{g}")
            K.dma("pool", WGR[g].ap, w_in[l, :, g * 512:(g + 1) * 512].rearrange("(kc p) c -> p kc c", p=128),
                  WGR[g], [], [WGR[g]])

        def Wc(kc, col):
            g, o = col // 512, col % 512
            return WGR[g].ap[:, kc, o:o + 128], WGR[g]

        def Wv(kc):
            return WGR[2].ap[:, kc, :], WGR[2]
        HB = [K.sb([128, 8, 512], F32, "hb") for _ in range(2)]
        CS = [K.sb([128, 2, 512], F32, "cs") for _ in range(2)]
        SQ = K.sb([128, 8, 512], BF16, "sq")
        RS = K.sb([128, 512], F32, "rs")
        HNb = [K.sb([128, 8, 512], BF16, "hn") for _ in range(2)]
        T1 = [K.sb([128, 512], F32, "t1") for _ in range(2)]
        T2 = [K.sb([128, 512], F32, "t2") for _ in range(2)]
        SG = [K.sb([128, 512], F32, "sg") for _ in range(2)]
        QST = [K.sb([128, 4, 512], BF16, "qst") for _ in range(2)]
        KST = [K.sb([128, 4, 512], BF16, "kst") for _ in range(2)]
        ZST = [K.sb([128, 4, 512], BF16, "zst") for _ in range(2)]
        VST = [K.sb([128, 4, 512], BF16, "vst") for _ in range(2)]
        ZER = K.sb([128, 4, 16], BF16, "zer")
        zt_o = z_own(l)
        K.memset("pool", ZER.ap, 0.0, [ZER])
        K.dma("sp", zt_o[:, :, 0, 0:14].rearrange("c p t -> p c t"), ZER.ap[:, :, 0:14], ZER, [ZER], [K.D("xown", l)])
        tcount = 0
        def loadA(bi):
            c0, n = xblocks[bi]
            hb = HB[bi % 2]
            cs = CS[bi % 2]
            K.dma("sp", hb.ap[:, :, :n], h_src[:, :, c0:c0 + n].rearrange("c p t -> p c t"), hb, [K.D("hT", bi)], [hb])
            K.dma("sp", cs.ap[:, 0, :n], cos_d[:, c0:c0 + n], cs, [], [cs])
            K.dma("sp", cs.ap[:, 1, :n], sin_d[:, c0:c0 + n], cs, [], [cs])

        loadA(0)
        rmsnorm_block(HB[0], xblocks[0][1], "g1", l * 8, SQ, RS, HNb[0])
        for bi, (c0, n) in enumerate(xblocks):
            meta = (n == NMETA)
            hb = HB[bi % 2]
            cs = CS[bi % 2]
            HN = HNb[bi % 2]
            if bi + 1 < len(xblocks):
                loadA(bi + 1)
            qst, kst, zst, vst = QST[bi % 2], KST[bi % 2], ZST[bi % 2], VST[bi % 2]
            for which, st in (("q", qst), ("k", kst)):
                for hh in range(4):
                    oc = (0 if which == "q" else 512) + hh * 128
                    ocr = 2560 + (0 if which == "q" else 512) + hh * 128
                    b1 = K.bank()
                    for kc in range(8):
                        wa, wb = Wc(kc, oc)
                        K.mm(b1.ap[:, :n], wa, HN.ap[:, kc, :n], kc == 0, kc == 7, [wb, HN], [b1])
                    b2 = K.bank()
                    for kc in range(8):
                        wa, wb = Wc(kc, ocr)
                        K.mm(b2.ap[:, :n], wa, HN.ap[:, kc, :n], kc == 0, kc == 7, [wb, HN], [b2])
                    t1 = T1[tcount % 2]
                    t2 = T2[tcount % 2]
                    tcount += 1
                    K.tt("dve", t1.ap[:, :n], b1.ap[:, :n], cs.ap[:, 0, :n], ALU.mult, [b1, cs], [t1])
                    K.tt("dve", t2.ap[:, :n], b2.ap[:, :n], cs.ap[:, 1, :n], ALU.mult, [b2, cs], [t2])
                    K.tt("pool", st.ap[:, hh, :n], t1.ap[:, :n], t2.ap[:, :n], ALU.add, [t1, t2], [st])
            K.dma("sp", qT["w"][:, :, c0:c0 + n].rearrange("h p t -> p h t"), qst.ap[:, :, :n], qst, [qst],
                  [K.D("qT", bi)])
            if meta:
                K.dma("sp", kTm["w"].rearrange("h p t -> p h t"), kst.ap[:, :, :n], kst, [kst], [K.D("kTm")])
            else:
                for j in range(4 // HP):
                    K.dma("sp", k_own(l, j)[:, :, c0:c0 + n].rearrange("h p t -> p h t"),
                          kst.ap[:, j * HP:(j + 1) * HP, :n], kst, [kst], [K.D("xown", l)])
            if bi + 1 < len(xblocks):
                rmsnorm_block(HB[(bi + 1) % 2], xblocks[bi + 1][1], "g1", l * 8, SQ, RS, HNb[(bi + 1) % 2])
            for cc in range(4):
                ba = K.bank()
                for kc in range(8):
                    wa, wb = Wc(kc, 1536 + cc * 128)
                    K.mm(ba.ap[:, :n], wa, HN.ap[:, kc, :n], kc == 0, kc == 7, [wb, HN], [ba])
                bg = K.bank()
                for kc in range(8):
                    wa, wb = Wc(kc, 2048 + cc * 128)
                    K.mm(bg.ap[:, :n], wa, HN.ap[:, kc, :n], kc == 0, kc == 7, [wb, HN], [bg])
                sg = SG[cc % 2]
                K.act(sg.ap[:, :n], bg.ap[:, :n], AF.Sigmoid, [bg, PRM], [sg], bias=pc("bglu", l * 8 + 4 + cc))
                K.stt(zst.ap[:, cc, :n], ba.ap[:, :n], pc("bglu", l * 8 + cc), sg.ap[:, :n], ALU.add, ALU.mult,
                      [ba, sg, PRM], [zst])
            K.dma("sp", zT["w"][:, :, c0:c0 + n].rearrange("c p t -> p c t"), zst.ap[:, :, :n], zst, [zst],
                  [K.D("zT", bi)])
            if meta:
                K.dma("sp", zt_o[:, :, 0, 14:30].rearrange("c p t -> p c t"), zst.ap[:, :, 0:16], zst, [zst],
                      [K.D("xown", l)])
            else:
                for i in range(4):
                    s = (c0 // 128) + i
                    K.dma("sp", zt_o[:, :, s + 1, :].rearrange("c p t -> p c t"),
                          zst.ap[:, :, i * 128 + 98:i * 128 + 128], zst, [zst], [K.D("xown", l)])
            nts = (n + 127) // 128
            for ts_ in range(nts):
                m = min(128, n - ts_ * 128)
                bv = K.bank()
                for kc in range(8):
                    wa, wb = Wv(kc)
                    K.mm(bv.ap[:m, :512], HN.ap[:, kc, ts_ * 128:ts_ * 128 + m], wa, kc == 0, kc == 7, [wb, HN], [bv])
                K.copy("act", vst.ap[:m, ts_, :], bv.ap[:m, :512], [bv], [vst])
            if meta:
                K.dma("sp", vm["w"], vst.ap[:NMETA, 0, :], vst, [vst], [K.D("vm")])
            else:
                K.dma("sp", v_own(l, c0, n).rearrange("(s i) e -> i s e", i=128), vst.ap[:, :, :], vst, [vst],
                      [K.D("xown", l)])
        K.barrier()
        K.release(m0)

    XSEM = {pn: Buf(None, "xsem_" + pn) for (pn, _) in pieces}
    carry = {}

    def exchange(l):
        for (pn, rows) in pieces:
            o_, a_ = xown_t[pn][l % 2], xall_t[pn][l % 2]
            K._add("pool", lambda e, o_=o_, a_=a_: e.collective_compute(
                "AllGather", ALU.bypass, replica_groups=[[2 * i, 2 * i + 1] for i in range(n_pairs)],
                ins=[o_.opt()], outs=[a_.opt()]), [K.D("xown", l)], [K.D("xall", l)],
                dbuf=XSEM[pn], inc=1)

    def phaseB(l):
        lam_init = 0.8 - 0.6 * math.exp(-0.3 * l)
        zviews = [z_all(l, r) for r in range(2)]
        mB0 = K.mark()
        WO = K.sb([128, 8, D], BF16, "wo")
        for kc in range(8):
            K.dma("pool", WO.ap[:, kc, :], w_out[l, kc * 128:(kc + 1) * 128, :], WO, [], [WO])
        carry["WO"] = WO
        carry["m0"] = mB0
        m0 = K.mark()
        LJ = K.sb([128, 64], F32, "lj")
        LD = K.sb([128, 4], F32, "ld")
        K.stt(LJ.ap, PRM.ap[:, PL["lq1"] + l * 64:PL["lq1"] + (l + 1) * 64], 1.0,
              PRM.ap[:, PL["lk1"] + l * 64:PL["lk1"] + (l + 1) * 64], ALU.mult, ALU.mult, [PRM], [LJ, LD],
              accum_out=LD.ap[:, 0:1])
        K.stt(LJ.ap, PRM.ap[:, PL["lq2"] + l * 64:PL["lq2"] + (l + 1) * 64], 1.0,
              PRM.ap[:, PL["lk2"] + l * 64:PL["lk2"] + (l + 1) * 64], ALU.mult, ALU.mult, [PRM, LJ], [LJ, LD],
              accum_out=LD.ap[:, 1:2])
        K.act(LD.ap[:, 2:4], LD.ap[:, 0:2], AF.Exp, [LD], [LD])
        K.tt("dve", LD.ap[:, 0:1], LD.ap[:, 2:3], LD.ap[:, 3:4], ALU.subtract, [LD], [LD])
        K.ts("dve", NEGLAM.ap[:, 0:1], LD.ap[:, 0:1], -1.0, ALU.mult, [LD], [NEGLAM], s2=-lam_init, op1=ALU.add)

        KTb = [K.sb([128, NKEY], BF16, "kt"), None]
        Vb = [K.sb([128, NKT, 129], BF16, "v"), None]
        QTb = [K.sb([128, TL], BF16, "qt"), None]
        K.memset("pool", Vb[0].ap[:, :, 128:129], 1.0, [Vb[0]])

        def loadH(hh):
            KT, V, QT = KTb[hh % 2], Vb[hh % 2], QTb[hh % 2]
            K.dma("sp", KT.ap[:, 0:NMETA], kTm["r"][hh], KT, [K.D("kTm")], [KT])
            K.dma("sp", V.ap[:NMETA, 0, 0:128], vm["r"][:, hh * 128:(hh + 1) * 128], V, [K.D("vm")], [V])
            for r in range(2):
                for sa in range(0, NS, 4):
                    sb_ = min(NS, sa + 4)
                    K.dma("sp", KT.ap[:, NMETA:].rearrange("p (s r c) -> p s r c", r=2, c=128)[:, sa:sb_, r, :],
                          k_all(l, r, hh).rearrange("p (s c) -> p s c", c=128)[:, sa:sb_, :], KT, [K.D("xall", l)],
                          [KT])
                    K.dma("sp", V.ap[:, 1:, :].rearrange("p (s r) e -> p s r e", r=2)[:, sa:sb_, r, 0:128],
                          v_all(l, r, sa * 128, (sb_ - sa) * 128).rearrange("(s i) e -> i s e", i=128)[:, :, hh * 128:(hh + 1) * 128],
                          V, [K.D("xall", l)], [V])
            K.dma("sp", QT.ap, qT["r"][hh], QT, [K.D("qT", j) for j in range(len(xblocks))], [QT])

        m1 = K.mark()
        DG = K.sb([128, 4, CW, 128], BF16, "dg")
        for cc in range(4):
            for j in range(CW):
                K.ts("dve", DG.ap[:, cc, j, :], ident, pc("convw", (l * 4 + cc) * CW + j), ALU.mult, [CB, PRM], [DG])
        ZC = [K.sb([128, 4, 4, 158], BF16, "zc") for _ in range(2)]
        CA = [K.sb([128, 4, 4, HALO], BF16, "ca") for _ in range(2)]
        CBB = [K.sb([128, 4, 4, HALO], BF16, "cbb") for _ in range(2)]
        CT = K.sb([128, 4, 4, HALO], F32, "ct")
        CT2 = K.sb([128, 4, 4, HALO], F32, "ct2")
        Y32b = [K.sb([128, 4, 512], F32, "y32") for _ in range(2)]
        YBF = K.sb([128, 4, 512], BF16, "ybf")
        YSQ = K.sb([128, 4, 512], BF16, "ysq")
        MEAN = K.sb([128, 512], F32, "mean")
        MSQ = K.sb([128, 512], F32, "msq")
        RSD = K.sb([128, 512], F32, "rsd")
        TT = [K.sb([128, 512], F32, "tt") for _ in range(2)]
        CST = [K.sb([128, 4, 512], BF16, "cst") for _ in range(2)]
        def loadB1(bi):
            c0, n = xblocks[bi]
            meta = (n == NMETA)
            nsl, wd = (1, NMETA) if meta else (4, 128)
            zc = ZC[bi % 2]
            for cc in range(4):
                K.dma("sp", zc.ap[:, cc, :nsl, HALO:HALO + wd],
                      zT["r"][cc, :, c0:c0 + n].rearrange("p (s t) -> p s t", t=wd), zc, [K.D("zT", bi)], [zc])
            if not meta:
                ca, cbb = CA[bi % 2], CBB[bi % 2]
                s0 = c0 // 128
                for cc in range(4):
                    K.dma("sp", ca.ap[:, cc, :, :], zviews[0][cc, :, s0 + 1:s0 + 5, :], ca, [K.D("xall", l)], [ca])
                    K.dma("sp", cbb.ap[:, cc, :, :], zviews[1][cc, :, s0:s0 + 4, :], cbb, [K.D("xall", l)], [cbb])

        def frontB1(bi):
            c0, n = xblocks[bi]
            meta = (n == NMETA)
            nsl, wd = (1, NMETA) if meta else (4, 128)
            zc = ZC[bi % 2]
            Y32 = Y32b[bi % 2]
            if meta:
                K.memset("pool", zc.ap[:, :, 0, 0:HALO], 0.0, [zc])
            else:
                ca, cbb = CA[bi % 2], CBB[bi % 2]
                K.ts("pool", CT.ap, ca.ap, pc("sel", 0), ALU.mult, [ca, PRM], [CT])
                K.ts("pool", CT2.ap, cbb.ap, pc("sel", 1), ALU.mult, [cbb, PRM], [CT2])
                K.tt("pool", zc.ap[:, :, :, 0:HALO], CT.ap, CT2.ap, ALU.add, [CT, CT2], [zc])
            for cc in range(4):
                bk = K.bank()
                for j in range(CW):
                    K.mm(bk.ap[:, :n].rearrange("p (s t) -> p s t", t=wd), DG.ap[:, cc, j, :],
                         zc.ap[:, cc, :nsl, j:j + wd], j == 0, j == CW - 1, [DG, zc], [bk])
                K.act(Y32.ap[:, cc, :n], bk.ap[:, :n], AF.Identity, [bk, PRM], [Y32], bias=pc("convb", l * 4 + cc))

        def preB1(bi):
            c0, n = xblocks[bi]
            Y32 = Y32b[bi % 2]
            K.copy("act", YBF.ap[:, :, :n], Y32.ap[:, :, :n], [Y32], [YBF])
            K.act(YSQ.ap[:, :, :n], Y32.ap[:, :, :n], AF.Square, [Y32], [YSQ])

        def backB1(bi):
            c0, n = xblocks[bi]
            Y32 = Y32b[bi % 2]
            bs = K.bank()
            for cc in range(4):
                K.mm(bs.ap[:, :n], ones, YBF.ap[:, cc, :n], cc == 0, cc == 3, [YBF, CB], [bs])
            bq = K.bank()
            for cc in range(4):
                K.mm(bq.ap[:, :n], ones, YSQ.ap[:, cc, :n], cc == 0, cc == 3, [YSQ, CB], [bq])
            K.ts("dve", MEAN.ap[:, :n], bs.ap[:, :n], 1.0 / 512, ALU.mult, [bs], [MEAN])
            K.tt("dve", MSQ.ap[:, :n], MEAN.ap[:, :n], MEAN.ap[:, :n], ALU.mult, [MEAN], [MSQ])
            K.stt(RSD.ap[:, :n], bq.ap[:, :n], 1.0 / 512, MSQ.ap[:, :n], ALU.mult, ALU.subtract, [bq, MSQ], [RSD])
            K.act(RSD.ap[:, :n], RSD.ap[:, :n], AF.Sqrt, [RSD], [RSD], bias=EPS, scale=1.0)
            K.recip(RSD.ap[:, :n], RSD.ap[:, :n], [RSD], [RSD])
            cst = CST[bi % 2]
            for cc in range(4):
                t = TT[cc % 2]
                K.tt("dve", t.ap[:, :n], Y32.ap[:, cc, :n], MEAN.ap[:, :n], ALU.subtract, [Y32, MEAN], [t])
                K.tt("dve", t.ap[:, :n], t.ap[:, :n], RSD.ap[:, :n], ALU.mult, [t, RSD], [t])
                K.act(cst.ap[:, cc, :n], t.ap[:, :n], AF.Silu, [t, PRM], [cst], bias=pc("lnb", l * 4 + cc),
                      scale=pc("lng", l * 4 + cc))
            K.dma("sp", mixc_d[:, :, c0:c0 + n].rearrange("c p t -> p c t"), cst.ap[:, :, :n], cst, [cst],
                  [K.D("mixc", bi)])

        nblk = len(xblocks)
        loadB1(0)
        if nblk > 1:
            loadB1(1)
        loadH(0)
        frontB1(0)
        for bi in range(nblk):
            if bi + 2 < nblk:
                loadB1(bi + 2)
            preB1(bi)
            if bi + 1 < nblk:
                frontB1(bi + 1)
            backB1(bi)
        K.barrier()
        K.release(m1)

        KTb[1] = K.sb([128, NKEY], BF16, "kt")
        Vb[1] = K.sb([128, NKT, 129], BF16, "v")
        QTb[1] = K.sb([128, TL], BF16, "qt")
        K.memset("pool", Vb[1].ap[:, :, 128:129], 1.0, [Vb[1]])
        Pb = [K.sb([128, 2, 512], BF16, "p") for _ in range(4)]
        OA = [K.sb([128, 4, 2, 129], F32, "oa") for _ in range(2)]
        RZ = K.sb([128, 4, 2], F32, "rz")
        S1 = K.sb([128, 4], F32, "s1")
        O0 = [K.sb([128, 128], F32, "o0") for _ in range(2)]
        OO = [K.sb([128, 4, 128], F32, "oo") for _ in range(2)]
        JK = K.sb([128, 128], F32, "jk")
        SS = K.sb([128, 4], F32, "ss")
        RSTD = K.sb([128, 4], F32, "rstd")
        AOS = [K.sb([128, 4, 128], BF16, "aos") for _ in range(2)]
        abank = banks[4:8]
        sc_ = 1.0 / (128.0 * (1.0 - lam_init) ** 2)
        bb_ = EPS / ((1.0 - lam_init) ** 2)
        it = 0
        pidx = 0
        pending_tail = []
        for hh in range(4):
            KT, V, QT = KTb[hh % 2], Vb[hh % 2], QTb[hh % 2]
            if hh + 1 < 4:
                loadH(hh + 1)
            qblocks = [(m, 4, 128) for m in range(NB)] + [(NB, 1, NMETA)]
            for (m, nsl, wq) in qblocks:
                meta = (wq == NMETA)
                if meta:
                    ktiles = [(0, NMETA, 0, "M", 0)]
                    qbase = NX
                else:
                    ktiles = [(0, NMETA, 0, None, 0)]
                    for g in range(8 * m + 8):
                        r = g - 8 * m
                        if r < 0:
                            ktiles.append((1 + g, 128, 0, None, 0))
                        else:
                            ktiles.append((1 + g, 128, r // 2, "E" if r % 2 == 0 else "O", r // 2))
                    qbase = 4 * m * 128
                oa = OA[it % 2]
                nkt = len(ktiles)
                pps = {}

                def qk_stage(idx):
                    nonlocal pidx
                    (kt, nk, i0, mk, im) = ktiles[idx]
                    ncols = (nsl - i0) * wq
                    q0 = qbase + i0 * wq
                    k0 = 0 if kt == 0 else NMETA + (kt - 1) * 128
                    bp = bankpair[pidx % 2]
                    pp = Pb[pidx % len(Pb)]
                    pidx += 1
                    pps[idx] = pp
                    for c in range(2):
                        K.mm(bp.ap[:nk, c, :ncols], KT.ap[c * 64:(c + 1) * 64, k0:k0 + nk],
                             QT.ap[c * 64:(c + 1) * 64, q0:q0 + ncols], True, True, [KT, QT], [bp])
                    K.act(pp.ap[:nk, :, :ncols], bp.ap[:nk, :, :ncols], AF.Exp, [bp], [pp], scale=0.125)
                    if mk is not None:
                        mka = {"E": maskE, "O": maskO, "M": maskM}[mk]
                        for c in range(2):
                            a = pp.ap[:nk, c, (im - i0) * wq:(im - i0 + 1) * wq]
                            K.tt("dve", a, a, mka[:nk, :wq], ALU.mult, [pp, CB], [pp])

                def pv_stage(idx):
                    (kt, nk, i0, mk, im) = ktiles[idx]
                    pp = pps.pop(idx)
                    last = idx == nkt - 1
                    for i in range(i0, nsl):
                        for c in range(2):
                            K.mm(abank[i].ap[:wq, c * 256:c * 256 + 129], pp.ap[:nk, c, (i - i0) * wq:(i - i0 + 1) * wq],
                                 V.ap[:nk, kt, :], idx == 0 and c == 0, last, [pp, V], [abank[i]],
                                 skip_group_check=True)

                for idx in range(nkt + 2):
                    if idx < nkt:
                        qk_stage(idx)
                    if idx >= 2:
                        pv_stage(idx - 2)
                    if idx == min(5, nkt + 1) and pending_tail:
                        pending_tail.pop(0)()
                for i in range(nsl):
                    K.copy("dve", oa.ap[:wq, i, :, :],
                           abank[i].ap[:wq, :].rearrange("p (c x) -> p c x", x=256)[:, :, 0:129], [abank[i]], [oa])
                def make_tail(oa=oa, wq=wq, nsl=nsl, qbase=qbase, m=m, hh=hh, it_=it):
                    def tail():
                        K.recip(RZ.ap[:wq, :nsl, :], oa.ap[:wq, :nsl, :, 128], [oa], [RZ])
                        K.ts("dve", S1.ap[:wq, :nsl], RZ.ap[:wq, :nsl, 1], NEGLAM.ap[:wq, 0:1], ALU.mult, [RZ, NEGLAM],
                             [S1])
                        aos = AOS[it_ % 2]
                        oo = OO[it_ % 2]
                        for i in range(nsl):
                            o0 = O0[i % 2]
                            K.ts("dve", o0.ap[:wq, :], oa.ap[:wq, i, 0, 0:128], RZ.ap[:wq, i, 0:1], ALU.mult, [oa, RZ],
                                 [o0])
                            K.stt(oo.ap[:wq, i, :], oa.ap[:wq, i, 1, 0:128], S1.ap[:wq, i:i + 1], o0.ap[:wq, :], ALU.mult,
                                  ALU.add, [oa, S1, o0], [oo])
                            K.stt(JK.ap[:wq, :], oo.ap[:wq, i, :], 1.0, oo.ap[:wq, i, :], ALU.mult, ALU.mult, [oo],
                                  [JK, SS], accum_out=SS.ap[:wq, i:i + 1])
                        K.act(RSTD.ap[:wq, :nsl], SS.ap[:wq, :nsl], AF.Ln, [SS], [RSTD], bias=bb_, scale=sc_)
                        K.act(RSTD.ap[:wq, :nsl], RSTD.ap[:wq, :nsl], AF.Exp, [RSTD], [RSTD], scale=-0.5)
                        for i in range(nsl):
                            K.stt(aos.ap[:wq, i, :], oo.ap[:wq, i, :], RSTD.ap[:wq, i:i + 1],
                                  PRM.ap[:wq, PL["subg"] + l * 128:PL["subg"] + (l + 1) * 128], ALU.mult, ALU.mult,
                                  [oo, RSTD, PRM], [aos])
                        K.dma("sp", ao_d[qbase:qbase + nsl * wq, hh * 128:(hh + 1) * 128].rearrange(
                            "(s q) e -> q s e", q=wq), aos.ap[:wq, :nsl, :], aos, [aos], [K.D("ao", m)])
                    return tail

                pending_tail.append(make_tail())
                it += 1
        while pending_tail:
            pending_tail.pop(0)()
        K.barrier()
        K.release(m0)

    def phaseC(l, h_src, final):
        m0 = carry["m0"]
        WO = carry["WO"]
        HB = [K.sb([128, 8, 512], F32, "hb") for _ in range(3)]
        AOB = [K.sb([128, 4, 512], BF16, "aob") for _ in range(2)]
        MIX = [K.sb([128, 8, 512], BF16, "mix") for _ in range(2)]
        SQ = K.sb([128, 8, 512], BF16, "sq")
        RS = K.sb([128, 512], F32, "rs")
        HN2 = [K.sb([128, 8, 512], BF16, "hn2") for _ in range(2)]
        def loadC1(bi):
            c0, n = xblocks[bi]
            meta = (n == NMETA)
            nsl, wd = (1, NMETA) if meta else (4, 128)
            hb, aob, mix = HB[bi % 3], AOB[bi % 2], MIX[bi % 2]
            K.dma("sp", hb.ap[:, :, :n], h_src[:, :, c0:c0 + n].rearrange("c p t -> p c t"), hb, [K.D("hT", bi)], [hb])
            K.dma("sp", aob.ap[:wd, :nsl, :], ao_d[c0:c0 + n, :].rearrange("(s q) e -> q s e", q=wd), aob,
                  [K.D("ao", bi)], [aob])
            K.dma("sp", mix.ap[:, 4:8, :n], mixc_d[:, :, c0:c0 + n].rearrange("c p t -> p c t"), mix,
                  [K.D("mixc", bi)], [mix])

        def frontC1(bi):
            c0, n = xblocks[bi]
            meta = (n == NMETA)
            nsl, wd = (1, NMETA) if meta else (4, 128)
            hb, aob, mix = HB[bi % 3], AOB[bi % 2], MIX[bi % 2]
            for hh in range(4):
                bt = K.bank()
                btb = bt.ap.bitcast(BF16)
                for i in range(nsl):
                    K.transpose(btb[:, i * wd:(i + 1) * wd], aob.ap[:wd, i, hh * 128:(hh + 1) * 128], ident[:wd, :wd],
                                [aob, CB], [bt])
                K.copy("act", mix.ap[:, hh, :n], btb[:, :n], [bt], [mix])
            for oc in range(8):
                by = K.bank()
                for kc in range(8):
                    K.mm(by.ap[:, :n], WO.ap[:, kc, oc * 128:(oc + 1) * 128], mix.ap[:, kc, :n], kc == 0, kc == 7,
                         [WO, mix], [by])
                K.tt("dve", hb.ap[:, oc, :n], hb.ap[:, oc, :n], by.ap[:, :n], ALU.add, [hb, by], [hb])
            K.dma("sp", hmid_d[:, :, c0:c0 + n].rearrange("c p t -> p c t"), hb.ap[:, :, :n], hb, [hb],
                  [K.D("hmid", bi)])

        def backC1(bi):
            c0, n = xblocks[bi]
            hb, hn2 = HB[bi % 3], HN2[bi % 2]
            rmsnorm_block(hb, n, "g2", l * 8, SQ, RS, hn2)
            K.dma("sp", hn2_d[:, :, c0:c0 + n].rearrange("c p t -> p c t"), hn2.ap[:, :, :n], hn2, [hn2],
                  [K.D("hn2", bi)])

        nblk = len(xblocks)
        loadC1(0)
        if nblk > 1:
            loadC1(1)
        frontC1(0)
        for bi in range(nblk):
            if bi + 2 < nblk:
                loadC1(bi + 2)
            if bi + 1 < nblk:
                frontC1(bi + 1)
            backC1(bi)
        K.barrier()
        K.release(m0)
        NG = (FC + 3) // 4
        WGg = [None] * NG
        WUg = [None] * NG
        for g in range(NG):
            nf = min(4, FC - 4 * g)
            WGg[g] = K.sb([128, 8, nf * 128], BF16, f"wg{g}")
            WUg[g] = K.sb([128, 8, nf * 128], BF16, f"wu{g}")
            K.dma("pool", WGg[g].ap, w_gu[l, :, g * 512:g * 512 + nf * 128].rearrange("(kc p) c -> p kc c", p=128),
                  WGg[g], [], [WGg[g]])
            K.dma("pool", WUg[g].ap,
                  w_gu[l, :, DFF + g * 512:DFF + g * 512 + nf * 128].rearrange("(kc p) c -> p kc c", p=128),
                  WUg[g], [], [WUg[g]])
        WDh = [K.sb([128, FC // 2, D], BF16, f"wd{i}") for i in range(2)]
        for i in range(2):
            K.dma("pool", WDh[i].ap,
                  w_dn[l, i * (FC // 2) * 128:(i + 1) * (FC // 2) * 128, :].rearrange("(f p) c -> p f c", p=128),
                  WDh[i], [], [WDh[i]])
        HN = [K.sb([128, 8, 256], BF16, "hn") for _ in range(2)]
        HB2 = [K.sb([128, 8, 256], F32, "hb2") for _ in range(2)]
        ACT = K.sb([128, FC, 256], BF16, "act")
        SG = [K.sb([128, 256], F32, "sg") for _ in range(2)]
        SQ2 = K.sb([128, 8, 256], BF16, "sq2")
        RS2 = K.sb([128, 256], F32, "rs2")
        fblocks = [(j * 256, 256) for j in range(NX // 256)] + [(NX, NMETA)]
        def loadC2(bi):
            c0, n = fblocks[bi]
            sbi = NB if n == NMETA else c0 // 512
            hn, hb = HN[bi % 2], HB2[bi % 2]
            K.dma("sp", hn.ap[:, :, :n], hn2_d[:, :, c0:c0 + n].rearrange("c p t -> p c t"), hn, [K.D("hn2", sbi)], [hn])
            K.dma("sp", hb.ap[:, :, :n], hmid_d[:, :, c0:c0 + n].rearrange("c p t -> p c t"), hb, [K.D("hmid", sbi)],
                  [hb])

        loadC2(0)
        for bi, (c0, n) in enumerate(fblocks):
            meta = (n == NMETA)
            sbi = NB if meta else c0 // 512
            hn, hb = HN[bi % 2], HB2[bi % 2]
            if bi + 1 < len(fblocks):
                loadC2(bi + 1)
            for f in range(FC):
                bg = K.bank()
                for kc in range(8):
                    K.mm(bg.ap[:, :n], WGg[f // 4].ap[:, kc, (f % 4) * 128:(f % 4 + 1) * 128], hn.ap[:, kc, :n],
                         kc == 0, kc == 7, [WGg[f // 4], hn], [bg])
                bu = K.bank()
                for kc in range(8):
                    K.mm(bu.ap[:, :n], WUg[f // 4].ap[:, kc, (f % 4) * 128:(f % 4 + 1) * 128], hn.ap[:, kc, :n],
                         kc == 0, kc == 7, [WUg[f // 4], hn], [bu])
                sg = SG[f % 2]
                K.act(sg.ap[:, :n], bg.ap[:, :n], AF.Silu, [bg], [sg])
                K.tt("dve", ACT.ap[:, f, :n], sg.ap[:, :n], bu.ap[:, :n], ALU.mult, [sg, bu], [ACT])
            for oc in range(8):
                bd = K.bank()
                for f in range(FC):
                    wdb = WDh[f // (FC // 2)]
                    K.mm(bd.ap[:, :n], wdb.ap[:, f % (FC // 2), oc * 128:(oc + 1) * 128], ACT.ap[:, f, :n], f == 0,
                         f == FC - 1, [wdb, ACT], [bd])
                K.tt("dve", hb.ap[:, oc, :n], hb.ap[:, oc, :n], bd.ap[:, :n], ALU.add, [hb, bd], [hb])
            if final:
                if not meta:
                    rmsnorm_block(hb, n, "gf", 0, SQ2, RS2, None, out_dtype_f32_inplace=True)
                    K.dma("sp", outT[:, :, c0:c0 + n].rearrange("c p t -> p c t"), hb.ap[:, :, :n], hb, [hb],
                          [K.D("outT", bi)])
            else:
                K.dma("sp", hT["w"][:, :, c0:c0 + n].rearrange("c p t -> p c t"), hb.ap[:, :, :n], hb, [hb],
                      [K.D("hT", sbi)])
        K.barrier()
        K.release(m0)

    h_written_here = False
    for (ph, l) in stages:
        if ph == "A":
            phaseA(l, xT if l == 0 else (hT["w"] if h_written_here else hT["r"]))
            if fused:
                exchange(l)
        elif ph == "B":
            phaseB(l)
        elif ph == "C":
            hsrc = xT if l == 0 else hT["r"]
            phaseC(l, hsrc, final=(l == DEPTH - 1))
            h_written_here = True
    K.finish()
    K.emit(stack)
    stack.close()
    return nc, ext_in, ext_out


def _bf(a):
    return np.asarray(a, dtype=np.float32).astype(ml_dtypes.bfloat16)


def host_prepare(inputs, NS, DEPTH, B):
    NX = NS * 128
    TL = NX + NMETA
    PL = prm_layout(DEPTH)
    f32 = np.float32
    x = np.asarray(inputs["x"], f32)
    meta = np.asarray(inputs["meta_tokens"], f32)
    w_in = np.asarray(inputs["w_in"], f32)
    perm = np.arange(512).reshape(4, 2, 64)
    perm = np.concatenate([perm[:, :, 32:], perm[:, :, :32]], axis=-1).reshape(512)
    w_in_ext = np.ascontiguousarray(np.concatenate([w_in, w_in[:, :, perm], w_in[:, :, 512 + perm]], axis=-1))
    w_out = np.ascontiguousarray(np.asarray(inputs["w_out"], f32))
    w_gu = np.ascontiguousarray(np.asarray(inputs["w_gate_up"], f32))
    w_dn = np.ascontiguousarray(np.asarray(inputs["w_down"], f32))

    def colmajor(v, nch):
        v = np.asarray(v, f32)
        lead = v.shape[:-1]
        v = v.reshape(*lead, nch, 128)
        v = np.moveaxis(v, -1, 0)
        return v.reshape(128, -1)

    def rep(v):
        v = np.asarray(v, f32).reshape(1, -1)
        return np.broadcast_to(v, (128, v.shape[1]))

    prm_base = np.zeros((128, PL["_n"]), f32)
    prm_base[:, PL["g1"]:PL["g1"] + DEPTH * 8] = colmajor(inputs["norm1_g"], 8)
    prm_base[:, PL["g2"]:PL["g2"] + DEPTH * 8] = colmajor(inputs["norm2_g"], 8)
    prm_base[:, PL["gf"]:PL["gf"] + 8] = colmajor(inputs["final_g"], 8)
    prm_base[:, PL["bglu"]:PL["bglu"] + DEPTH * 8] = colmajor(inputs["b_glu"], 8)
    cw = np.asarray(inputs["conv_w"], f32)
    cw = cw.reshape(DEPTH, CW, 4, 128).transpose(3, 0, 2, 1).reshape(128, -1)
    prm_base[:, PL["convw"]:PL["convw"] + DEPTH * 4 * CW] = cw
    prm_base[:, PL["convb"]:PL["convb"] + DEPTH * 4] = colmajor(inputs["conv_b"], 4)
    prm_base[:, PL["lng"]:PL["lng"] + DEPTH * 4] = colmajor(inputs["conv_ln_g"], 4)
    prm_base[:, PL["lnb"]:PL["lnb"] + DEPTH * 4] = colmajor(inputs["conv_ln_b"], 4)
    for nm, key in (("lq1", "lam_q1"), ("lk1", "lam_k1"), ("lq2", "lam_q2"), ("lk2", "lam_k2")):
        prm_base[:, PL[nm]:PL[nm] + DEPTH * 64] = rep(inputs[key])
    prm_base[:, PL["subg"]:PL["subg"] + DEPTH * 128] = rep(inputs["subln_g"])

    inv = (10000.0 ** (-np.arange(0, 64, 2, dtype=f32) / 64.0)).astype(f32)
    tri = (np.arange(128)[:, None] <= np.arange(128)[None, :]).astype(f32)
    in_maps = []
    for b in range(B):
        for p in range(2):
            tiles = [x[b, 128 * (2 * s + p):128 * (2 * s + p) + 128, :] for s in range(NS)]
            hx = np.concatenate(tiles + [meta], axis=0)
            xTc = np.ascontiguousarray(hx.T).reshape(8, 128, TL)
            pos = np.concatenate([NMETA + 128 * (2 * s + p) + np.arange(128) for s in range(NS)] + [np.arange(NMETA)])
            ang = pos.astype(f32)[:, None] * inv[None, :]
            ang = np.concatenate([ang, ang], axis=-1).astype(f32)
            cosd = np.cos(ang).astype(f32).T
            sind = np.sin(ang).astype(f32).T
            sign = np.where(np.arange(64) < 32, -1.0, 1.0).astype(f32)[:, None]
            sind = sind * sign
            cosT = np.ascontiguousarray(np.concatenate([cosd, cosd], axis=0))
            sinT = np.ascontiguousarray(np.concatenate([sind, sind], axis=0))
            cb = np.zeros((128, CB_N), f32)
            cb[:, CB_IDENT:CB_IDENT + 128] = np.eye(128, dtype=f32)
            cb[:, CB_ONES:CB_ONES + 128] = 1.0
            cb[:, CB_ME:CB_ME + 128] = tri if p == 0 else 1.0
            cb[:, CB_MO:CB_MO + 128] = 0.0 if p == 0 else tri
            cb[:16, CB_MM:CB_MM + 16] = tri[:16, :16]
            prm = prm_base.copy()
            prm[:, PL["sel"]] = 1.0 if p == 1 else 0.0
            prm[:, PL["sel"] + 1] = 0.0 if p == 1 else 1.0
            in_maps.append({"xT": xTc, "prm": prm, "cb": _bf(cb), "cosT": cosT, "sinT": sinT, "w_in": w_in_ext,
                            "w_out": w_out, "w_gu": w_gu, "w_dn": w_dn})
    return in_maps


def assemble_output(outs, NS, B):
    NX = NS * 128
    S = 2 * NX
    out = np.empty((B, S, D), np.float32)
    for b in range(B):
        for p in range(2):
            o = np.asarray(outs[2 * b + p]).reshape(D, NX).T
            for s in range(NS):
                g = 2 * s + p
                out[b, 128 * g:128 * g + 128, :] = o[128 * s:128 * s + 128, :]
    return out


_PROG_CACHE = {}


def run_model(inputs, NS, DEPTH, B, fused=True):
    n_cores = 2 * B
    in_maps = host_prepare(inputs, NS, DEPTH, B)
    if fused:
        stages = []
        for l in range(DEPTH):
            stages += [("A", l), ("B", l), ("C", l)]
        key = ("f", NS, DEPTH, B)
        if key not in _PROG_CACHE:
            _PROG_CACHE[key] = build_program(NS, DEPTH, stages, True, B)
        nc, ext_in, ext_out = _PROG_CACHE[key]
        res = run_bass_kernel_spmd(nc, [{k: m[k] for k in ext_in} for m in in_maps], core_ids=list(range(n_cores)))
        return assemble_output([r["outT"] for r in res.results], NS, B)
    groups = [[("A", 0)]]
    for l in range(DEPTH):
        g = [("B", l), ("C", l)]
        if l + 1 < DEPTH:
            g.append(("A", l + 1))
        groups.append(g)
    state = [dict() for _ in range(n_cores)]
    outs = None
    for gi, g in enumerate(groups):
        nc, ext_in, ext_out = build_program(NS, DEPTH, g, False, B)
        maps = []
        for c in range(n_cores):
            m = {}
            for k in ext_in:
                if k in in_maps[c]:
                    m[k] = in_maps[c][k]
                elif k.endswith("_in"):
                    base = k[:-3]
                    if base in state[c]:
                        m[k] = state[c][base]
                    else:
                        shp, dt = _shape_of(nc, k)
                        m[k] = np.zeros(shp, dt)
                else:
                    raise KeyError(k)
            maps.append(m)
        res = run_bass_kernel_spmd(nc, maps, core_ids=list(range(n_cores)))
        for c in range(n_cores):
            r = res.results[c]
            for k in ext_out:
                if k.endswith("_out"):
                    state[c][k[:-4]] = np.asarray(r[k])
        for b in range(B):
            for kname in list(state[2 * b].keys()):
                if kname.startswith("xo_"):
                    xa = np.concatenate([state[2 * b][kname], state[2 * b + 1][kname]], axis=0)
                    state[2 * b]["xa_" + kname[3:]] = xa
                    state[2 * b + 1]["xa_" + kname[3:]] = xa
        outs = [np.asarray(r["outT"]) for r in res.results]
    return assemble_output(outs, NS, B)


def _shape_of(nc, name):
    for alloc in nc.allocations:
        if isinstance(alloc, mybir.MemoryLocationSet) and alloc.memorylocations and alloc.memorylocations[0].name == name:
            return tuple(alloc.tensor_shape), mybir.dt.np(alloc.dtype)
    raise KeyError(name)


def kernel(**inputs):
    return run_model(inputs, NS=32, DEPTH=4, B=4, fused=True)
```
